# Optimizing a Trainium2 kernel written in Bass

```python
import math
import jax, jax.numpy as jnp
from jax import lax
import numpy as np

D_MODEL = 1024
BATCH = 8
SEQ = 8192
DEPTH = 1

HEAD_DIM = 64
DIL_PATTERNS = ((128, 1), (512, 4), (2048, 16))
N_DIL_GROUPS = len(DIL_PATTERNS)
HEADS_PER_GROUP = 4
N_ATTN_HEADS = N_DIL_GROUPS * HEADS_PER_GROUP
ATTN_WIDTH = N_ATTN_HEADS * HEAD_DIM
ATTN_OUT_WIDTH = HEADS_PER_GROUP * HEAD_DIM
QBLK = 128
ROPE_THETA = 10000.0

CHUNK = 128
GMLP_GROUPS = 4
GMLP_GROUP_CH = 128
GMLP_WIDTH = GMLP_GROUPS * GMLP_GROUP_CH

D_FF = 4 * D_MODEL
EPS = 1e-6

Q0 = 0
K0 = Q0 + ATTN_WIDTH
V0 = K0 + ATTN_WIDTH
U0 = V0 + ATTN_WIDTH
Z0 = U0 + GMLP_WIDTH
GA0 = Z0 + GMLP_WIDTH
GB0 = GA0 + D_MODEL
IN_WIDTH = GB0 + D_MODEL

kernel_name = "hybrid_dilated_attn_gmlp_block"


def _rmsnorm(x, gain):
    xf = x.astype(jnp.float32)
    y = xf * lax.rsqrt(jnp.mean(xf * xf, axis=-1, keepdims=True) + EPS)
    return (y * gain.astype(jnp.float32)).astype(x.dtype)


def _layernorm(x, gain, bias):
    xf = x.astype(jnp.float32)
    mu = jnp.mean(xf, axis=-1, keepdims=True)
    var = jnp.mean(jnp.square(xf - mu), axis=-1, keepdims=True)
    y = (xf - mu) * lax.rsqrt(var + EPS)
    return (y * gain.astype(jnp.float32) + bias.astype(jnp.float32)).astype(x.dtype)


def _rope(x):
    S, Dh = x.shape[1], x.shape[-1]
    half = Dh // 2
    inv_freq = ROPE_THETA ** (-jnp.arange(half, dtype=jnp.float32) / half)
    ang = jnp.arange(S, dtype=jnp.float32)[:, None] * inv_freq[None, :]
    cos = jnp.cos(ang)[None, :, None, :]
    sin = jnp.sin(ang)[None, :, None, :]
    xf = x.astype(jnp.float32)
    x1, x2 = xf[..., :half], xf[..., half:]
    return jnp.concatenate([x1 * cos - x2 * sin, x2 * cos + x1 * sin], axis=-1).astype(x.dtype)


def _dilated_window_attention(q, k, v, dilation, n_back):
    B, S, H, Dh = q.shape
    L = S // dilation
    nb = -(-L // QBLK)
    Lp = nb * QBLK

    def to_sub(t):
        t = t.reshape(B, L, dilation, H, Dh).transpose(0, 2, 3, 1, 4)
        t = jnp.pad(t, ((0, 0), (0, 0), (0, 0), (0, Lp - L), (0, 0)))
        return t.reshape(B, dilation, H, nb, QBLK, Dh)

    def with_prev(t):
        prev = jnp.pad(t[:, :, :, :-1], ((0, 0), (0, 0), (0, 0), (1, 0), (0, 0), (0, 0)))
        return jnp.concatenate([prev, t], axis=4)

    qb, kb, vb = to_sub(q), to_sub(k), to_sub(v)
    kc, vc = with_prev(kb), with_prev(vb)
    s = jnp.einsum('brhnqe,brhnke->brhnqk', qb.astype(jnp.float32),
                   kc.astype(jnp.float32)) * (Dh ** -0.5)
    qi = jnp.arange(QBLK)[:, None]
    kj = jnp.arange(2 * QBLK)[None, :]
    dist = qi + QBLK - kj
    band = (dist >= 0) & (dist <= n_back)
    key_sub = jnp.arange(nb)[:, None, None] * QBLK + kj[None] - QBLK
    mask = band[None] & (key_sub >= 0)
    s = jnp.where(mask, s, jnp.float32(-1e30))
    m = jnp.max(s, axis=-1, keepdims=True)
    p = jnp.exp(s - m)
    den = jnp.sum(p, axis=-1, keepdims=True)
    o = jnp.einsum('brhnqk,brhnke->brhnqe', p, vc.astype(jnp.float32)) / den
    lse = (m + jnp.log(den))[..., 0]
    o = o.reshape(B, dilation, H, Lp, Dh)[:, :, :, :L].transpose(0, 3, 1, 2, 4).reshape(B, S, H, Dh)
    lse = lse.reshape(B, dilation, H, Lp)[:, :, :, :L].transpose(0, 3, 1, 2).reshape(B, S, H)
    return o, lse


def _mixer_dilated_attention(q, k, v):
    B, S = q.shape[0], q.shape[1]
    outs, lses = [], []
    for g, (window, dilation) in enumerate(DIL_PATTERNS):
        sl = slice(g * HEADS_PER_GROUP, (g + 1) * HEADS_PER_GROUP)
        o, lse = _dilated_window_attention(q[:, :, sl], k[:, :, sl], v[:, :, sl],
                                           dilation, window // dilation)
        outs.append(o)
        lses.append(lse)
    alpha = jax.nn.softmax(jnp.stack(lses, axis=0), axis=0)
    o = jnp.sum(alpha[..., None] * jnp.stack(outs, axis=0), axis=0)
    return o.reshape(B, S, ATTN_OUT_WIDTH).astype(q.dtype)


def _mixer_chunked_gmlp(u, z, ln_gain, ln_bias, w_spatial, b_spatial):
    B, S, _ = u.shape
    nc = S // CHUNK
    z = _layernorm(z, ln_gain, ln_bias)
    zc = z.reshape(B, nc, CHUNK, GMLP_GROUPS, GMLP_GROUP_CH)
    tril = jnp.tril(jnp.ones((CHUNK, CHUNK), dtype=bool))
    w = jnp.where(tril[None], w_spatial, jnp.zeros_like(w_spatial))
    sz = jnp.einsum('gij,bcjgd->bcigd', w.astype(jnp.float32), zc.astype(jnp.float32))
    sz = sz + b_spatial.T.astype(jnp.float32)[None, None, :, :, None]
    out = u.reshape(B, nc, CHUNK, GMLP_GROUPS, GMLP_GROUP_CH).astype(jnp.float32) * sz
    return out.reshape(B, S, GMLP_WIDTH).astype(u.dtype)


def setup_inputs(seed: int = 0) -> dict:
    key = jax.random.key(seed)
    ks = jax.random.split(key, 16)
    f32 = jnp.float32

    def dense(k, fan_in, fan_out):
        return jax.random.normal(k, (DEPTH, fan_in, fan_out), f32) * (fan_in ** -0.5)

    def gain(k, n):
        return 1.0 + 0.02 * jax.random.normal(k, (DEPTH, n), f32)

    return {
        "x": jax.random.normal(ks[0], (BATCH, SEQ, D_MODEL), f32),
        "norm_pre_mix": gain(ks[1], D_MODEL),
        "w_in": dense(ks[2], D_MODEL, IN_WIDTH),
        "w_spatial": jax.random.normal(ks[3], (DEPTH, GMLP_GROUPS, CHUNK, CHUNK), f32) * (CHUNK ** -0.5),
        "b_spatial": 1.0 + 0.1 * jax.random.normal(ks[4], (DEPTH, GMLP_GROUPS, CHUNK), f32),
        "ln_v_gain": gain(ks[5], GMLP_WIDTH),
        "ln_v_bias": 0.02 * jax.random.normal(ks[6], (DEPTH, GMLP_WIDTH), f32),
        "w_branch_attn": dense(ks[7], ATTN_OUT_WIDTH, D_MODEL),
        "w_branch_gmlp": dense(ks[8], GMLP_WIDTH, D_MODEL),
        "w_out": dense(ks[9], D_MODEL, D_MODEL),
        "norm_post_mix": gain(ks[10], D_MODEL),
        "norm_pre_mlp": gain(ks[11], D_MODEL),
        "w_mlp_in": dense(ks[12], D_MODEL, D_FF),
        "w_mlp_out": dense(ks[13], D_FF, D_MODEL),
        "norm_post_mlp": gain(ks[14], D_MODEL),
    }


def reference(x, norm_pre_mix, w_in, w_spatial, b_spatial, ln_v_gain, ln_v_bias,
              w_branch_attn, w_branch_gmlp, w_out, norm_post_mix, norm_pre_mlp,
              w_mlp_in, w_mlp_out, norm_post_mlp):
    B, S, D = x.shape
    for layer in range(DEPTH):
        h = _rmsnorm(x, norm_pre_mix[layer])
        proj = jnp.einsum('bsd,de->bse', h, w_in[layer])
        q = _rope(proj[..., Q0:K0].reshape(B, S, N_ATTN_HEADS, HEAD_DIM))
        k = _rope(proj[..., K0:V0].reshape(B, S, N_ATTN_HEADS, HEAD_DIM))
        v = proj[..., V0:U0].reshape(B, S, N_ATTN_HEADS, HEAD_DIM)
        u = jax.nn.gelu(proj[..., U0:Z0])
        z = jax.nn.gelu(proj[..., Z0:GA0])
        gate_a = jax.nn.sigmoid(proj[..., GA0:GB0])
        gate_b = jax.nn.sigmoid(proj[..., GB0:IN_WIDTH])

        y_attn = _mixer_dilated_attention(q, k, v)
        y_gmlp = _mixer_chunked_gmlp(u, z, ln_v_gain[layer], ln_v_bias[layer],
                                     w_spatial[layer], b_spatial[layer])
        merged = (gate_a * jnp.einsum('bse,ed->bsd', y_attn, w_branch_attn[layer])
                  + gate_b * jnp.einsum('bse,ed->bsd', y_gmlp, w_branch_gmlp[layer]))
        y = jnp.einsum('bsd,de->bse', merged, w_out[layer])
        x = x + _rmsnorm(y, norm_post_mix[layer])

        h = _rmsnorm(x, norm_pre_mlp[layer])
        a = jax.nn.relu(jnp.einsum('bsd,df->bsf', h, w_mlp_in[layer]))
        y = jnp.einsum('bsf,fd->bsd', a * a, w_mlp_out[layer])
        x = x + _rmsnorm(y, norm_post_mlp[layer])
    return x
```

```python
import contextlib
import numpy as np
import ml_dtypes
import concourse.bass as bass
import concourse.mybir as mybir
from concourse.bass_utils import run_bass_kernel_spmd

F32 = mybir.dt.float32
BF16 = mybir.dt.bfloat16
AF = mybir.ActivationFunctionType
ALU = mybir.AluOpType

D = 1024
INW = 5376
Q0, K0, V0, U0, Z0, GA0, GB0 = 0, 768, 1536, 2304, 2816, 3328, 4352
EPS = 1e-6
SELF_SYNC = True
import os
STAGE = float(os.environ.get('KSTAGE', '99'))
NRING = 3


class Prog:
    ENGS = ("pe", "act", "dve", "pool", "sp")

    def __init__(self):
        self.ops = {e: [] for e in self.ENGS}
        self.cnt = {e: 0 for e in self.ENGS}
        self.last_w = {}
        self.readers = {}
        self.waited = {e: {} for e in self.ENGS}
        self.dma_cnt = {}
        self.semnames = set()
        self.pending = {}

    def barrier_sp(self):
        self.pending = dict(self.dma_cnt)

    def op(self, eng, fn, reads=(), writes=(), chan=None):
        deps = []
        for r in reads:
            if r in self.last_w:
                deps.append((self.last_w[r], "raw"))
            if r.startswith("ps"):
                for t in self.readers.get(r, ()):
                    if t[2] != eng:
                        deps.append((t, "rar"))
        for w in writes:
            if w in self.last_w:
                deps.append((self.last_w[w], "waw"))
            for t in self.readers.get(w, ()):
                deps.append((t, "war"))
        waits = {}
        for (s, v, e), kind in deps:
            if e == eng:
                if eng == "pe" or eng == "sp":
                    if eng == "pe":
                        continue
                elif not SELF_SYNC or kind == "war":
                    continue
            if self.waited[eng].get(s, 0) >= v:
                continue
            waits[s] = max(waits.get(s, 0), v)
        if eng == "sp" and self.pending:
            for s, v in self.pending.items():
                if self.waited[eng].get(s, 0) < v:
                    waits[s] = max(waits.get(s, 0), v)
            self.pending = {}
        for s, v in waits.items():
            self.waited[eng][s] = v
        if eng == "sp":
            assert chan is not None
            s = "d_" + chan
            self.dma_cnt[s] = self.dma_cnt.get(s, 0) + 16
            tok = (s, self.dma_cnt[s], eng)
            inc = 16
        else:
            s = "c_" + eng
            self.cnt[eng] += 1
            tok = (s, self.cnt[eng], eng)
            inc = 1
        self.semnames.add(s)
        self.ops[eng].append((fn, sorted(waits.items()), s, inc))
        for w in writes:
            self.last_w[w] = tok
            self.readers[w] = []
        for r in reads:
            self.readers.setdefault(r, []).append(tok)
        return tok

    def replay(self, eng_name, eng, sems, final_waits=()):
        for fn, waits, s, inc in self.ops[eng_name]:
            for ws, wv in waits:
                eng.wait_ge(sems[ws], wv)
            ins = fn(eng)
            ins.then_inc(sems[s], inc)
        for ws, wv in final_waits:
            eng.wait_ge(sems[ws], wv)


def build_nc(S):
    NT = S // 512
    NST = S // 2048
    nc = bass.Bass("TRN2", target_bir_lowering=False)
    P = Prog()

    def din(name, shape, dt=F32):
        return nc.dram_tensor(name, list(shape), dt, kind="ExternalInput")

    x = din("x", [S, D])
    w_in = din("w_in", [D, INW])
    w_ba = din("w_ba", [256, D])
    w_bg = din("w_bg", [512, D])
    w_out = din("w_out", [D, D])
    w1 = din("w1", [D, 4096])
    w2 = din("w2", [4096, D])
    gpre_d = din("gpre", [128, 8])
    gpre2_d = din("gpre2", [128, 8])
    gpm_d = din("gpm_b", [128, D])
    gpl_d = din("gpl_b", [128, D])
    wspT_d = din("wspT", [128, 512])
    bsp_d = din("bsp_b", [128, 512])
    lng_d = din("lng", [128, 4])
    lnb_d = din("lnb", [128, 4])
    cos_d = din("cos_t", [3, 128, S])
    sin_d = din("sin_t", [3, 128, S])
    ident_d = din("ident", [128, 128], BF16)
    rsw_d = din("rsw", [128, 128], BF16)
    perm4_d = din("perm4", [128, 128], BF16)
    mask2_d = din("mask2", [128, 512], BF16)
    tril_d = din("trilT", [128, 512])
    out = nc.dram_tensor("out", [S, D], F32, kind="ExternalOutput")

    def dscr(name, shape, dt):
        return nc.dram_tensor(name, list(shape), dt, kind="Internal")

    win_s = dscr("win_s", [128, 8, INW], BF16)
    w1_s = dscr("w1_s", [128, 8, 4096], BF16)
    w2_s = dscr("w2_s", [2, 128, 32, 512], BF16)
    wout_s = dscr("wout_s", [128, 8, D], BF16)
    wbg_s = dscr("wbg_s", [128, 4, D], BF16)
    wba_s = dscr("wba_s", [64, 4, D], BF16)
    k2_s = dscr("k2_s", [NST, 4, 128, 1024], BF16)
    v2_s = dscr("v2_s", [NST, 4, 128, 1280], BF16)
    acc2_s = dscr("acc2_s", [NST, 4, 65, 4, 512], F32)

    es = contextlib.ExitStack()
    with es:
        def sb(name, shape, dt):
            return es.enter_context(nc.sbuf_tensor("s_" + name, list(shape), dt))

        ring = [sb(f"ring{k}", [128, 4096], BF16) for k in range(NRING)]
        wba = sb("wba", [64, 4, D], BF16)
        wbg = sb("wbg", [128, 4, D], BF16)
        hT = sb("hT", [128, 8, 512], BF16)
        hT1 = sb("hT1", [128, 8, 512], BF16)
        xt = [sb(f"xt{k}", [128, D], F32) for k in range(4)]
        hb = [sb(f"hb{k}", [128, D], BF16) for k in range(2)]
        a2 = sb("a2", [128, 32, 512], BF16)
        rb = [sb(f"rb{k}", [128, 512], BF16) for k in range(2)]
        k01 = [[sb(f"k01_{g}{p}", [128, 2, 512], BF16) for p in range(2)] for g in range(2)]
        v01 = [[sb(f"v01_{g}{p}", [128, 4, 4, 80], BF16) for p in range(2)] for g in range(2)]
        q2q = sb("q2q", [128, 2, 512], BF16)
        k2q = sb("k2q", [128, 2, 512], BF16)
        k2p = sb("k2p", [128, 2, 512], BF16)
        v2q = sb("v2q", [128, 4, 4, 80], BF16)
        v2p = sb("v2p", [128, 4, 4, 80], BF16)
        acc0 = sb("acc0", [65, 2048], F32)
        acc2t = sb("acc2t", [65, 2048], F32)
        rden = sb("rden", [65, 2048], F32)
        fs = [sb(f"fs{k}", [128, 512], F32) for k in range(6)]
        cosT = [sb(f"cosT{k}", [128, 512], F32) for k in range(2)]
        sinT = [sb(f"sinT{k}", [128, 512], F32) for k in range(2)]
        ident = sb("ident", [128, 128], BF16)
        rsw = sb("rsw", [128, 128], BF16)
        perm4 = sb("perm4", [128, 128], BF16)
        mask2 = sb("mask2", [128, 512], BF16)
        wspT = sb("wspT", [128, 512], BF16)
        ones_bf = sb("ones_bf", [128, 128], BF16)
        ones_f = sb("ones_f", [128, 64], F32)
        Cgm = sb("Cgm", [128, 512], F32)
        lng = sb("lng", [128, 4], F32)
        lnb = sb("lnb", [128, 4], F32)
        gpre = sb("gpre", [128, 8], F32)
        gpre2 = sb("gpre2", [128, 8], F32)
        gpm = sb("gpm", [128, D], F32)
        gpl = sb("gpl", [128, D], F32)
        NSTAT = 16
        stat = sb("stat", [128, NSTAT, 16], F32)
        ps = [es.enter_context(nc.psum_tensor(f"ps{k}", [128, 512], F32)) for k in range(8)]

        bank_ctr = [0]

        def nb():
            b = bank_ctr[0] % 8
            bank_ctr[0] += 1
            return b

        stat_ctr = [0]

        def nstat():
            s = stat_ctr[0] % NSTAT
            stat_ctr[0] += 1
            return s

        fs_ctr = [0]

        def nfs():
            s = fs_ctr[0] % 6
            fs_ctr[0] += 1
            return s

        def dma(out_ap, in_ap, chan, reads, writes):
            P.op("sp", lambda e, o=out_ap, i=in_ap: e.dma_start(out=o, in_=i), reads, writes, chan=chan)

        def mm(out_ap, pairs, reads, writes):
            def fn(e, o=out_ap, pairs=pairs):
                n = len(pairs)
                ins = None
                for i, (l, r) in enumerate(pairs):
                    ins = e.matmul(o, lhsT=l, rhs=r, start=(i == 0), stop=(i == n - 1))
                return ins
            P.op("pe", fn, reads, writes)

        def act(out_ap, in_ap, func, reads, writes, scale=1.0, bias=0.0):
            P.op("act", lambda e: e.activation(out=out_ap, in_=in_ap, func=func, bias=bias, scale=scale), reads, writes)

        def tt(eng, out_ap, in0, in1, op, reads, writes):
            P.op(eng, lambda e: e.tensor_tensor(out=out_ap, in0=in0, in1=in1, op=op), reads, writes)

        def ts(eng, out_ap, in0, s1, s2, op0, op1, reads, writes):
            if s2 is None:
                P.op(eng, lambda e: e.tensor_scalar(out=out_ap, in0=in0, scalar1=s1, scalar2=None, op0=op0), reads, writes)
            else:
                P.op(eng, lambda e: e.tensor_scalar(out=out_ap, in0=in0, scalar1=s1, scalar2=s2, op0=op0, op1=op1), reads, writes)

        def stt(eng, out_ap, in0, scalar, in1, op0, op1, reads, writes):
            P.op(eng, lambda e: e.scalar_tensor_tensor(out=out_ap, in0=in0, scalar=scalar, in1=in1, op0=op0, op1=op1), reads, writes)

        def cp(eng, out_ap, in_ap, reads, writes):
            if eng == "act":
                P.op(eng, lambda e: e.activation(out=out_ap, in_=in_ap, func=AF.Copy), reads, writes)
            else:
                P.op(eng, lambda e: e.tensor_copy(out=out_ap, in_=in_ap), reads, writes)

        for i, (t, d, nm) in enumerate([(ident, ident_d, "ident"), (rsw, rsw_d, "rsw"), (perm4, perm4_d, "perm4"), (mask2, mask2_d, "mask2"),
                                        (lng, lng_d, "lng"), (lnb, lnb_d, "lnb"), (gpre, gpre_d, "gpre"),
                                        (gpre2, gpre2_d, "gpre2"), (gpm, gpm_d, "gpm"), (gpl, gpl_d, "gpl")]):
            dma(t[:], d.ap(), "c" + str(i), [], [nm])
        P.op("dve", lambda e: e.memset(ones_bf[:], 1.0), [], ["ones_bf"])
        P.op("dve", lambda e: e.memset(ones_f[:], 1.0), [], ["ones_f"])
        for g in range(2):
            for p in range(2):
                P.op("pool", lambda e, g=g, p=p: e.memset(v01[g][p][:, :, :, 64:80], 1.0), [], [f"v01_{g}{p}"])
        P.op("pool", lambda e: e.memset(v2q[:, :, :, 64:80], 1.0), [], ["v2q"])
        dma(fs[0][:], wspT_d.ap(), "fs0", [], ["fs0"])
        dma(fs[1][:], tril_d.ap(), "fs1", [], ["fs1"])
        dma(fs[2][:], bsp_d.ap(), "fs2", [], ["fs2"])
        tt("dve", wspT[:], fs[0][:], fs[1][:], ALU.mult, ["fs0", "fs1"], ["wspT"])
        mm(ps[0][:], [(ones_bf[:], wspT[:])], ["ones_bf", "wspT"], ["ps0"])
        for g in range(4):
            stt("dve", Cgm[:, g * 128:(g + 1) * 128], ps[0][:, g * 128:(g + 1) * 128], lnb[:, g:g + 1],
                fs[2][:, g * 128:(g + 1) * 128], ALU.mult, ALU.add, ["ps0", "lnb", "fs2"], ["Cgm"])

        stg = [0]

        def prep(src_ap, dst_ap, npart, ncol, scal, dst_res):
            k = stg[0] % 4
            r = stg[0] % NRING
            stg[0] += 1
            dma(xt[k][0:npart, 0:ncol], src_ap, f"xt{k}", [], [f"xt{k}"])
            if scal is None:
                cp("dve" if stg[0] % 2 else "pool", ring[r][0:npart, 0:ncol], xt[k][0:npart, 0:ncol], [f"xt{k}"], [f"ring{r}"])
            else:
                ts("dve" if stg[0] % 2 else "pool", ring[r][0:npart, 0:ncol], xt[k][0:npart, 0:ncol], scal, None, ALU.mult, None,
                   [f"xt{k}", "gpre", "gpre2"], [f"ring{r}"])
            dma(dst_ap, ring[r][0:npart, 0:ncol], f"ring{r}", [f"ring{r}"], [dst_res])

        for kc in range(8):
            for c0 in range(0, INW, 1024):
                c1 = min(c0 + 1024, INW)
                prep(w_in[kc * 128:(kc + 1) * 128, c0:c1], win_s[:, kc, c0:c1], 128, c1 - c0, gpre[:, kc:kc + 1], "win_s")
        for kc in range(8):
            for c0 in range(0, 4096, 1024):
                prep(w1[kc * 128:(kc + 1) * 128, c0:c0 + 1024], w1_s[:, kc, c0:c0 + 1024], 128, 1024, gpre2[:, kc:kc + 1], "w1_s")
        for f in range(32):
            k = stg[0] % 4
            r = stg[0] % NRING
            stg[0] += 1
            dma(xt[k][:], w2[f * 128:(f + 1) * 128, :], f"xt{k}", [], [f"xt{k}"])
            cp("dve" if f % 2 else "pool", ring[r][:, 0:1024], xt[k][:], [f"xt{k}"], [f"ring{r}"])
            for hf in range(2):
                dma(w2_s[hf, :, f, :], ring[r][:, hf * 512:(hf + 1) * 512], f"ring{r}", [f"ring{r}"], ["w2_s"])
        for kc in range(8):
            prep(w_out[kc * 128:(kc + 1) * 128, :], wout_s[:, kc, :], 128, 1024, None, "wout_s")
        for g in range(4):
            prep(w_bg[g * 128:(g + 1) * 128, :], wbg_s[:, g, :], 128, 1024, None, "wbg_s")
        for j in range(4):
            prep(w_ba[j * 64:(j + 1) * 64, :], wba_s[:, j, :], 64, 1024, None, "wba_s")
        P.barrier_sp()
        dma(wba[:], wba_s.ap(), "wba", ["wba_s"], ["wba"])
        dma(wbg[:], wbg_s.ap(), "wbg", ["wbg_s"], ["wbg"])

        def rms_stats(src_aps, src_res, what):
            s = nstat()
            sr = f"stat{s}"
            n = len(src_aps)
            st3 = stat[:, s, 0:6 * n].rearrange("p (a t) -> p a t", t=3)
            for i, a in enumerate(src_aps):
                P.op("dve", lambda e, i=i, a=a: e.bn_stats(st3[:, 2 * i:2 * i + 2, :], a), src_res, [sr])
            mv = stat[:, s, 12:14]
            P.op("dve", lambda e: e.bn_aggr(mv, st3), [sr], [sr])
            if what == "rms":
                stt("dve", stat[:, s, 14:15], stat[:, s, 12:13], stat[:, s, 12:13], stat[:, s, 13:14], ALU.mult, ALU.add, [sr], [sr])
                src = stat[:, s, 14:15]
            else:
                src = stat[:, s, 13:14]
            act(stat[:, s, 15:16], src, AF.Sqrt, [sr], [sr], scale=1.0, bias=EPS)
            P.op("dve", lambda e: e.reciprocal(stat[:, s, 15:16], stat[:, s, 15:16]), [sr], [sr])
            return s

        def norm_transpose(k, col, xres, hres_idx, g1blk=None):
            s = rms_stats([xt[k][:, 0:512], xt[k][:, 512:1024]], [xres], "rms")
            h = hb[hres_idx]
            hres = f"hb{hres_idx}"
            P.op("act", lambda e: e.activation(out=h[:], in_=xt[k][:], func=AF.Copy, scale=stat[:, s, 15:16]),
                 [xres, f"stat{s}"], [hres])
            b = nb()
            pst = ps[b][:].bitcast(BF16).rearrange("p (c t) -> p c t", t=128)

            def fn(e):
                ins = None
                for kc in range(8):
                    ins = e.transpose(pst[:, kc, :], h[:, kc * 128:(kc + 1) * 128], ident[:])
                return ins
            P.op("pe", fn, [hres, "ident"], [f"ps{b}"])
            cp("dve", hT[:, :, col:col + 128], pst, [f"ps{b}"], ["hT"])
            if g1blk is not None:
                for half in range(2):
                    bp = nb()

                    def fnp(e, half=half, bp=bp):
                        ins = None
                        for kk in range(4):
                            kc = 4 * half + kk
                            ins = e.matmul(ps[bp][:, kk * 128:(kk + 1) * 128], lhsT=h[:, kc * 128:(kc + 1) * 128], rhs=perm4[:],
                                           start=True, stop=True)
                        return ins
                    P.op("pe", fnp, [hres, "perm4"], [f"ps{bp}"])
                    dst = hT1[:, 4 * half:4 * half + 4, :].rearrange("p k (r i) -> p k r i", r=4)[:, :, :, 32 * g1blk:32 * g1blk + 32]
                    src = ps[bp][:].rearrange("p (k r i) -> p k r i", k=4, r=4)
                    cp("act" if half else "dve", dst, src, [f"ps{bp}"], ["hT1"])

        def rope(b, tabk, dst_ap, dst_res, scratch_slot):
            raw = a2[:, scratch_slot, :]
            rres = f"a2_{scratch_slot}"
            act(raw, ps[b][:], AF.Copy, [f"ps{b}"], [rres])
            b2 = nb()
            mm(ps[b2][:], [(rsw[:], raw)], ["rsw", rres], [f"ps{b2}"])
            f1 = nfs()
            f2 = nfs()
            tt("pool", fs[f1][:], raw, cosT[tabk][:], ALU.mult, [rres, f"cosT{tabk}"], [f"fs{f1}"])
            tt("dve", fs[f2][:], ps[b2][:], sinT[tabk][:], ALU.mult, [f"ps{b2}", f"sinT{tabk}"], [f"fs{f2}"])
            tt("pool", dst_ap, fs[f1][:], fs[f2][:], ALU.add, [f"fs{f1}", f"fs{f2}"], [dst_res])

        ep_ctr = [0]

        def attention(qT, qres, kcur, kcres, vcur, vcres, prev_of, evac):
            for b in range(4):
                bo = nb()
                pv = prev_of(b)
                for j in range(4):
                    jp, hh = j // 2, j % 2
                    lo = 64 * hh
                    bs_ = nb()
                    sl = 28 + (ep_ctr[0] % 2)
                    sp_ = 30 + (ep_ctr[0] % 2)
                    ep_ctr[0] += 1
                    ncol = 256 if pv is not None else 128
                    E = a2[:, sl, 0:ncol]
                    Pm = a2[:, sp_, 0:ncol]
                    reads = list(qres) + [kcres]
                    if pv is not None:
                        reads.append(pv[1])

                    def fn(e, b=b, jp=jp, lo=lo, bs_=bs_, pv=pv):
                        Q = qT[lo:lo + 64, jp, b * 128:(b + 1) * 128]
                        ins = e.matmul(ps[bs_][:, 0:128], lhsT=kcur[lo:lo + 64, jp, b * 128:(b + 1) * 128], rhs=Q, start=True, stop=True)
                        if pv is not None:
                            kb = pv[4]
                            ins = e.matmul(ps[bs_][:, 128:256], lhsT=pv[0][lo:lo + 64, jp, kb * 128:(kb + 1) * 128], rhs=Q,
                                           start=True, stop=True)
                        return ins
                    P.op("pe", fn, reads, [f"ps{bs_}"])
                    act(E, ps[bs_][:, 0:ncol], AF.Exp, [f"ps{bs_}"], [f"a2_{sl}"], scale=0.125)
                    tt("dve", Pm, E, mask2[:, 0:ncol], ALU.mult, [f"a2_{sl}", "mask2"], [f"a2_{sp_}"])
                    reads = [f"a2_{sp_}", vcres]
                    if pv is not None:
                        reads.append(pv[3])

                    def fn2(e, b=b, j=j, bo=bo, pv=pv, sp_=sp_):
                        o = ps[bo][0:65, j * 128:(j + 1) * 128]
                        ins = e.matmul(o, lhsT=vcur[:, b, j, 0:65], rhs=a2[:, sp_, 0:128], start=True, stop=(pv is None))
                        if pv is not None:
                            ins = e.matmul(o, lhsT=pv[2][:, pv[4], j, 0:65], rhs=a2[:, sp_, 128:256], start=False, stop=True)
                        return ins
                    P.op("pe", fn2, reads, [f"ps{bo}"])
                evac(b, bo)

        def load_tables(g, off, k):
            dma(cosT[k][:], cos_d[g, :, off:off + 512], f"cosT{k}", [], [f"cosT{k}"])
            dma(sinT[k][:], sin_d[g, :, off:off + 512], f"sinT{k}", [], [f"sinT{k}"])

        x_g2 = x.ap().rearrange("(t i r) d -> t r i d", i=128, r=16)
        rA, rB = 0, 1
        dma(ring[rA][:].rearrange("p (k c) -> p k c", k=8)[:, :, 0:256], win_s[:, :, Q0 + 512:Q0 + 768], f"ring{rA}", ["win_s"], [f"ring{rA}"])
        dma(ring[rA][:].rearrange("p (k c) -> p k c", k=8)[:, :, 256:512], win_s[:, :, K0 + 512:K0 + 768], f"ring{rA}", ["win_s"], [f"ring{rA}"])
        dma(ring[rB][:, 0:2048].rearrange("p (k c) -> p k c", k=8), win_s[:, :, V0 + 512:V0 + 768], f"ring{rB}", ["win_s"], [f"ring{rB}"])
        WA = ring[rA][:].rearrange("p (k c) -> p k c", k=8)
        WB = ring[rB][:, 0:2048].rearrange("p (k c) -> p k c", k=8)
        qi = 0
        for T in range(NST if STAGE >= 2 else 0):
            for rq in range(4):
                tk = qi % 2
                qi += 1
                load_tables(2, T * 2048 + rq * 512, tk)
                if T > 0:
                    dma(k2p[:], k2_s[T - 1, rq].rearrange("p (c n) -> p c n", c=2), "k2p", ["k2_s"], ["k2p"])
                    dma(v2p[:], v2_s[T - 1, rq].rearrange("p (b j e) -> p b j e", b=4, j=4), "v2p", ["v2_s"], ["v2p"])
                for b in range(4):
                    dma(xt[b][:], x_g2[T, 4 * rq + b], f"xt{b}", [], [f"xt{b}"])
                for b in range(4):
                    norm_transpose(b, b * 128, f"xt{b}", b % 2)
                if STAGE < 2.2:
                    continue
                for c in range(4):
                    bk = nb()
                    mm(ps[bk][:], [(WA[:, kc, c * 128:(c + 1) * 128], hT[:, kc, :]) for kc in range(8)], [f"ring{rA}", "hT"], [f"ps{bk}"])
                    if STAGE < 2.25:
                        continue
                    if c < 2:
                        rope(bk, tk, q2q[:, c, :], "q2q", 24 + c % 2)
                    else:
                        rope(bk, tk, k2q[:, c - 2, :], "k2q", 24 + c % 2)
                if STAGE < 2.3:
                    continue
                for b in range(4):
                    bk = nb()
                    mm(ps[bk][:, 0:256], [(hT[:, kc, b * 128:(b + 1) * 128], WB[:, kc, :]) for kc in range(8)], [f"ring{rB}", "hT"], [f"ps{bk}"])
                    cp("act" if b % 2 else "dve", v2q[:, b, :, 0:64], ps[bk][:, 0:256].rearrange("p (j e) -> p j e", j=4), [f"ps{bk}"], ["v2q"])
                if STAGE < 2.4:
                    continue
                dma(k2_s[T, rq].rearrange("p (c n) -> p c n", c=2), k2q[:], "k2q", ["k2q"], ["k2_s"])
                dma(v2_s[T, rq].rearrange("p (b j e) -> p b j e", b=4, j=4), v2q[:], "v2q", ["v2q"], ["v2_s"])

                if STAGE < 2.5:
                    continue

                def prev2(b, T=T):
                    return None if T == 0 else (k2p, "k2p", v2p, "v2p", b)

                def evac2(b, bo):
                    cp("act", acc0[:, :].rearrange("p (c j b i) -> p j b c i", c=4, j=4, b=4)[:, :, b, :, :],
                       ps[bo][0:65, :].rearrange("p (j c i) -> p j c i", j=4, c=4), [f"ps{bo}"], ["acc0"])
                attention(q2q, ["q2q"], k2q, "k2q", v2q, "v2q", prev2, evac2)
                dma(acc2_s[T, :, :, rq, :].rearrange("c p n -> p c n"), acc0[:, :].rearrange("p (c n) -> p c n", c=4),
                    "acc0", ["acc0"], ["acc2_s"])

        wq = []
        wstate = {"n": 0, "issued": 0, "pieces": []}

        def wpiece(src_view_fn):
            wstate["pieces"].append(src_view_fn)
            return len(wstate["pieces"]) - 1

        def ring_view(r, kind):
            if kind == "k8":
                return ring[r][:].rearrange("p (k c) -> p k c", k=8)
            if kind == "k4":
                return ring[r][:].rearrange("p (k c) -> p k c", k=4)
            raise ValueError

        def wissue(upto):
            while wstate["issued"] <= min(upto, len(wstate["pieces"]) - 1):
                i = wstate["issued"]
                r = i % NRING
                kind, src, sres = wstate["pieces"][i]
                dma(ring_view(r, kind), src, f"ring{r}", [sres], [f"ring{r}"])
                wstate["issued"] += 1

        def wget(i, keep=None):
            wissue((i if keep is None else keep) + NRING - 1)
            r = i % NRING
            return ring_view(r, wstate["pieces"][i][0]), f"ring{r}"

        for t in range(NT if STAGE >= 3 else 0):
            T, c_in = t // 4, t % 4
            par = t % 2
            base = len(wstate["pieces"])
            for c0 in (Q0, K0, V0, U0, Z0, GA0, GB0, GA0 + 512, GB0 + 512):
                wpiece(("k8", win_s[:, :, c0:c0 + 512], "win_s"))
            for h in range(2):
                wpiece(("k4", wout_s[:, 4 * h:4 * h + 4, :], "wout_s"))
            for p in range(8):
                wpiece(("k8", w1_s[:, :, p * 512:(p + 1) * 512], "w1_s"))
            for hf in range(2):
                for p in range(4):
                    wpiece(("k8", w2_s[hf, :, 8 * p:8 * p + 8, :], "w2_s"))
            PQ, PK, PV, PU, PZ, PGA0, PGB0, PGA1, PGB1, PO0, PO1 = [base + i for i in range(11)]
            PW1 = base + 11
            PW2 = base + 19

            for b in range(4):
                dma(xt[b][:], x[t * 512 + b * 128:t * 512 + (b + 1) * 128, :], f"xt{b}", [], [f"xt{b}"])
            if STAGE >= 3.06:
                dma(acc2t[:], acc2_s[T, c_in].rearrange("p q n -> p (q n)"), "acc2t", ["acc2_s"], ["acc2t"])
            for b in range(4):
                norm_transpose(b, b * 128, f"xt{b}", b % 2, g1blk=(b if STAGE >= 3.07 else None))

            if STAGE < 3.1:
                continue
            for which, PIDX in (("q", PQ), ("k", PK)):
                W, wres = wget(PIDX)
                for g in range(2):
                    load_tables(g, t * 512, g) if which == "q" else None
                    for c2 in range(2):
                        bk = nb()
                        col = g * 256 + c2 * 128
                        hsrc, hres_ = (hT, "hT") if g == 0 else (hT1, "hT1")
                        mm(ps[bk][:], [(W[:, kc, col:col + 128], hsrc[:, kc, :]) for kc in range(8)], [wres, hres_], [f"ps{bk}"])
                        if which == "q":
                            rope(bk, g, a2[:, 20 + 2 * g + c2, :], f"a2_{20 + 2 * g + c2}", 24 + c2)
                        else:
                            rope(bk, g, k01[g][par][:, c2, :], f"k01_{g}{par}", 24 + c2)
            if STAGE < 3.15:
                continue
            W, wres = wget(PV)
            for g in range(2):
                for b in range(4):
                    bk = nb()
                    hsrc, hres_ = (hT, "hT") if g == 0 else (hT1, "hT1")
                    mm(ps[bk][:, 0:256], [(hsrc[:, kc, b * 128:(b + 1) * 128], W[:, kc, g * 256:(g + 1) * 256]) for kc in range(8)],
                       [wres, hres_], [f"ps{bk}"])
                    cp("act" if b % 2 else "dve", v01[g][par][:, b, :, 0:64], ps[bk][:, 0:256].rearrange("p (j e) -> p j e", j=4),
                       [f"ps{bk}"], [f"v01_{g}{par}"])
            W, wres = wget(PU)
            for c in range(4):
                bk = nb()
                mm(ps[bk][:], [(W[:, kc, c * 128:(c + 1) * 128], hT[:, kc, :]) for kc in range(8)], [wres, "hT"], [f"ps{bk}"])
                act(a2[:, c, :], ps[bk][:], AF.Gelu_apprx_tanh, [f"ps{bk}"], [f"a2_{c}"])
            W, wres = wget(PZ)
            for b in range(4):
                bk = nb()
                mm(ps[bk][:], [(hT[:, kc, b * 128:(b + 1) * 128], W[:, kc, :]) for kc in range(8)], [wres, "hT"], [f"ps{bk}"])
                f1 = nfs()
                act(fs[f1][:], ps[bk][:], AF.Gelu_apprx_tanh, [f"ps{bk}"], [f"fs{f1}"])
                s = rms_stats([fs[f1][:]], [f"fs{f1}"], "ln")
                ts("dve", a2[:, 4 + b, :], fs[f1][:], stat[:, s, 12:13], stat[:, s, 15:16], ALU.subtract, ALU.mult,
                   [f"fs{f1}", f"stat{s}"], [f"a2_{4 + b}"])

            if STAGE < 3.3:
                continue
            accn = acc0[:, :].rearrange("p (j n) -> p j n", j=4)
            for g in range(2):
                qres_l = [f"a2_{20 + 2 * g}", f"a2_{21 + 2 * g}"]
                kc_t, kc_r = k01[g][par], f"k01_{g}{par}"
                vc_t, vc_r = v01[g][par], f"v01_{g}{par}"
                kp_t, kp_r = k01[g][1 - par], f"k01_{g}{1 - par}"
                vp_t, vp_r = v01[g][1 - par], f"v01_{g}{1 - par}"
                if g == 0:
                    def prev_of(b, t=t, kc_t=kc_t, kc_r=kc_r, vc_t=vc_t, vc_r=vc_r, kp_t=kp_t, kp_r=kp_r, vp_t=vp_t, vp_r=vp_r):
                        if b > 0:
                            return (kc_t, kc_r, vc_t, vc_r, b - 1)
                        return None if t == 0 else (kp_t, kp_r, vp_t, vp_r, 3)

                    def evac(b, bo):
                        cp("act", accn[:, :, b * 128:(b + 1) * 128], ps[bo][0:65, :].rearrange("p (j n) -> p j n", j=4), [f"ps{bo}"], ["acc0"])
                else:
                    def prev_of(b, t=t, kp_t=kp_t, kp_r=kp_r, vp_t=vp_t, vp_r=vp_r):
                        return None if t == 0 else (kp_t, kp_r, vp_t, vp_r, b)

                    def evac(b, bo):
                        dst = acc0[:, :].rearrange("p (j i r) -> p j i r", j=4, r=4)[:, :, :, b]
                        tt("dve", dst, ps[bo][0:65, :].rearrange("p (j n) -> p j n", j=4), dst, ALU.add, [f"ps{bo}", "acc0"], ["acc0"])

                class QT:
                    def __init__(self, g):
                        self.g = g

                    def __getitem__(self, idx):
                        pr, jp, cols = idx
                        return a2[pr, 20 + 2 * self.g + jp, cols]
                attention(QT(g), qres_l, kc_t, kc_r, vc_t, vc_r, prev_of, evac)
            if STAGE < 3.4:
                continue
            a_nat = acc0[:, :].rearrange("p (j i q b) -> p j i q b", j=4, q=4, b=4)
            a_g2 = acc2t[:, :].rearrange("p (q j b i) -> p j i q b", q=4, j=4, b=4)
            for j in range(4):
                tt("dve", a_nat[:, j], a_nat[:, j], a_g2[:, j], ALU.add, ["acc0", "acc2t"], ["acc0"])
            P.op("dve", lambda e: e.reciprocal(rden[64:65, :], acc0[64:65, :]), ["acc0"], ["rden"])
            for j in range(4):
                bk = nb()
                mm(ps[bk][0:64, :], [(ones_f[64:65, 0:64], rden[64:65, j * 512:(j + 1) * 512])], ["ones_f", "rden"], [f"ps{bk}"])
                tt("dve", a2[0:64, 16 + j, :], acc0[0:64, j * 512:(j + 1) * 512], ps[bk][0:64, :], ALU.mult, [f"ps{bk}", "acc0"], [f"a2_{16 + j}"])

            if STAGE < 3.5:
                continue
            for g in range(4):
                bk = nb()

                def fn(e, g=g, bk=bk):
                    ins = None
                    for b in range(4):
                        ins = e.matmul(ps[bk][:, b * 128:(b + 1) * 128], lhsT=a2[:, 4 + b, g * 128:(g + 1) * 128],
                                       rhs=wspT[:, g * 128:(g + 1) * 128], start=True, stop=True)
                    return ins
                P.op("pe", fn, [f"a2_{4 + b}" for b in range(4)] + ["wspT"], [f"ps{bk}"])
                f1 = nfs()
                for b in range(4):
                    stt("dve", fs[f1][:, b * 128:(b + 1) * 128], ps[bk][:, b * 128:(b + 1) * 128], lng[:, g:g + 1],
                        Cgm[:, g * 128:(g + 1) * 128], ALU.mult, ALU.add, [f"ps{bk}", "lng", "Cgm"], [f"fs{f1}"])
                tt("pool", a2[:, 8 + g, :], fs[f1][:], a2[:, g, :], ALU.mult, [f"fs{f1}", f"a2_{g}"], [f"a2_{8 + g}"])

            if STAGE < 3.6:
                continue
            for oc in range(8):
                WGA, rga = wget(PGA0 if oc < 4 else PGA1)
                WGB, rgb = wget(PGB0 if oc < 4 else PGB1, keep=(PGA0 if oc < 4 else PGA1))
                co = (oc % 4) * 128
                bA, bB, bGA, bGB = nb(), nb(), nb(), nb()
                mm(ps[bA][:], [(wba[:, j, oc * 128:(oc + 1) * 128], a2[0:64, 16 + j, :]) for j in range(4)],
                   ["wba"] + [f"a2_{16 + j}" for j in range(4)], [f"ps{bA}"])
                mm(ps[bB][:], [(wbg[:, g, oc * 128:(oc + 1) * 128], a2[:, 8 + g, :]) for g in range(4)],
                   ["wbg"] + [f"a2_{8 + g}" for g in range(4)], [f"ps{bB}"])
                mm(ps[bGA][:], [(WGA[:, kc, co:co + 128], hT[:, kc, :]) for kc in range(8)], [rga, "hT"], [f"ps{bGA}"])
                mm(ps[bGB][:], [(WGB[:, kc, co:co + 128], hT[:, kc, :]) for kc in range(8)], [rgb, "hT"], [f"ps{bGB}"])
                sa, sbb = 24 + (oc % 2), 26 + (oc % 2)
                act(a2[:, sa, :], ps[bGA][:], AF.Sigmoid, [f"ps{bGA}"], [f"a2_{sa}"])
                act(a2[:, sbb, :], ps[bGB][:], AF.Sigmoid, [f"ps{bGB}"], [f"a2_{sbb}"])
                f1, f2 = nfs(), nfs()
                tt("dve", fs[f1][:], ps[bA][:], a2[:, sa, :], ALU.mult, [f"ps{bA}", f"a2_{sa}"], [f"fs{f1}"])
                tt("dve", fs[f2][:], ps[bB][:], a2[:, sbb, :], ALU.mult, [f"ps{bB}", f"a2_{sbb}"], [f"fs{f2}"])
                ms = oc if oc < 8 else oc
                tt("pool", a2[:, ms, :], fs[f1][:], fs[f2][:], ALU.add, [f"fs{f1}", f"fs{f2}"] + [f"a2_{8 + g}" for g in range(4)], [f"a2_{ms}"])

            if STAGE < 3.7:
                continue
            WO0, ro0 = wget(PO0)
            WO1, ro1 = wget(PO1, keep=PO0)
            for b in range(4):
                by = [nb(), nb()]
                for hf in range(2):
                    pairs = []
                    for kc in range(8):
                        Wp = WO0 if kc < 4 else WO1
                        pairs.append((a2[:, kc, b * 128:(b + 1) * 128], Wp[:, kc % 4, hf * 512:(hf + 1) * 512]))
                    mm(ps[by[hf]][:], pairs, [ro0, ro1] + [f"a2_{kc}" for kc in range(8)], [f"ps{by[hf]}"])
                s = rms_stats([ps[by[0]][:], ps[by[1]][:]], [f"ps{by[0]}", f"ps{by[1]}"], "rms")
                for hf in range(2):
                    f1 = nfs()
                    stt("dve", fs[f1][:], ps[by[hf]][:], stat[:, s, 15:16], gpm[:, hf * 512:(hf + 1) * 512], ALU.mult, ALU.mult,
                        [f"ps{by[hf]}", f"stat{s}", "gpm"], [f"fs{f1}"])
                    tt("pool", xt[b][:, hf * 512:(hf + 1) * 512], xt[b][:, hf * 512:(hf + 1) * 512], fs[f1][:], ALU.add,
                       [f"fs{f1}", f"xt{b}"], [f"xt{b}"])
            if STAGE < 3.8:
                continue
            for b in range(4):
                norm_transpose(b, b * 128, f"xt{b}", b % 2)
            for f in range(32):
                W, wres = wget(PW1 + f // 4)
                bk = nb()
                co = (f % 4) * 128
                mm(ps[bk][:], [(W[:, kc, co:co + 128], hT[:, kc, :]) for kc in range(8)], [wres, "hT"], [f"ps{bk}"])
                act(rb[f % 2][:], ps[bk][:], AF.Relu, [f"ps{bk}"], [f"rb{f % 2}"])
                tt("pool", a2[:, f, :], rb[f % 2][:], rb[f % 2][:], ALU.mult, [f"rb{f % 2}"], [f"a2_{f}"])
            if STAGE < 3.9:
                continue
            for hf in range(2):
                for p in range(4):
                    W, wres = wget(PW2 + hf * 4 + p)
                    for b in range(4):
                        bk = hf * 4 + b

                        def fn(e, W=W, p=p, b=b, bk=bk):
                            ins = None
                            for ff in range(8):
                                ins = e.matmul(ps[bk][:], lhsT=a2[:, 8 * p + ff, b * 128:(b + 1) * 128], rhs=W[:, ff, :],
                                               start=(p == 0 and ff == 0), stop=(p == 3 and ff == 7), skip_group_check=True)
                            return ins
                        P.op("pe", fn, [wres] + [f"a2_{8 * p + ff}" for ff in range(8)], [f"ps{bk}"])
            bank_ctr[0] = 0
            for b in range(4):
                s = rms_stats([ps[b][:], ps[4 + b][:]], [f"ps{b}", f"ps{4 + b}"], "rms")
                for hf in range(2):
                    f1 = nfs()
                    bk = hf * 4 + b
                    stt("dve", fs[f1][:], ps[bk][:], stat[:, s, 15:16], gpl[:, hf * 512:(hf + 1) * 512], ALU.mult, ALU.mult,
                        [f"ps{bk}", f"stat{s}", "gpl"], [f"fs{f1}"])
                    tt("pool", xt[b][:, hf * 512:(hf + 1) * 512], xt[b][:, hf * 512:(hf + 1) * 512], fs[f1][:], ALU.add,
                       [f"fs{f1}", f"xt{b}"], [f"xt{b}"])
                dma(out[t * 512 + b * 128:t * 512 + (b + 1) * 128, :], xt[b][:], f"xt{b}", [f"xt{b}"], ["out"])

        sems = {name: es.enter_context(nc.semaphore(name)) for name in sorted(P.semnames)}
        final = [(s, v) for s, v in P.dma_cnt.items()]
        with nc.Block() as block:
            @block.tensor
            def _(e):
                P.replay("pe", e, sems)

            @block.scalar
            def _(e):
                P.replay("act", e, sems)

            @block.vector
            def _(e):
                P.replay("dve", e, sems)

            @block.gpsimd
            def _(e):
                P.replay("pool", e, sems)

            @block.sync
            def _(e):
                P.replay("sp", e, sems, final_waits=final)
    return nc


def _tables(S):
    half = 32
    inv = (10000.0 ** (-np.arange(half, dtype=np.float32) / half)).astype(np.float32)
    idx = np.arange(S)
    pos = [idx.copy()]
    n, r, i = idx // 512, (idx % 512) // 128, idx % 128
    pos.append(512 * n + 4 * i + r)
    T, r, i = idx // 2048, (idx % 2048) // 128, idx % 128
    pos.append(2048 * T + 16 * i + r)
    cos = np.zeros((3, 128, S), np.float32)
    sin = np.zeros((3, 128, S), np.float32)
    m = np.arange(128)
    fr = inv[m % 32]
    sgn = np.where((m % 64) < 32, -1.0, 1.0).astype(np.float32)
    for g in range(3):
        ang = (pos[g].astype(np.float32)[None, :] * fr[:, None]).astype(np.float32)
        cos[g] = np.cos(ang)
        sin[g] = np.sin(ang) * sgn[:, None]
    return cos, sin


def _consts():
    bf = ml_dtypes.bfloat16
    ident = np.eye(128, dtype=np.float32).astype(bf)
    m = np.arange(128)
    sw = np.where((m % 64) < 32, m + 32, m - 32)
    rsw = np.zeros((128, 128), np.float32)
    rsw[sw, m] = 1.0
    k = np.arange(128)[:, None]
    q = np.arange(128)[None, :]
    half = np.concatenate([(k <= q), (k >= q)], axis=1).astype(np.float32)
    mask2 = np.concatenate([half, half], axis=1).astype(bf)
    tril = (k <= q).astype(np.float32)
    trilT = np.tile(tril, (1, 4)).astype(np.float32)
    n = np.arange(128)
    perm4 = np.zeros((128, 128), np.float32)
    perm4[n, 32 * (n % 4) + n // 4] = 1.0
    return ident, rsw.astype(bf), mask2, trilT, perm4.astype(bf)


_NC_CACHE = {}


def _host_inputs(S, x_b, p):
    cos, sin = _tables(S)
    ident, rsw, mask2, trilT, perm4 = _consts()
    f = np.float32

    def col8(v):
        return np.ascontiguousarray(np.asarray(v, f).reshape(8, 128).T)

    def col4(v):
        return np.ascontiguousarray(np.asarray(v, f).reshape(4, 128).T)
    wsp = np.asarray(p["w_spatial"], f)[0]
    wspT = np.ascontiguousarray(wsp.transpose(2, 0, 1).reshape(128, 512))
    bsp = np.asarray(p["b_spatial"], f)[0].reshape(1, 512)
    common = {
        "w_in": np.ascontiguousarray(np.asarray(p["w_in"], f)[0]),
        "w_ba": np.ascontiguousarray(np.asarray(p["w_branch_attn"], f)[0]),
        "w_bg": np.ascontiguousarray(np.asarray(p["w_branch_gmlp"], f)[0]),
        "w_out": np.ascontiguousarray(np.asarray(p["w_out"], f)[0]),
        "w1": np.ascontiguousarray(np.asarray(p["w_mlp_in"], f)[0]),
        "w2": np.ascontiguousarray(np.asarray(p["w_mlp_out"], f)[0]),
        "gpre": col8(np.asarray(p["norm_pre_mix"])[0]),
        "gpre2": col8(np.asarray(p["norm_pre_mlp"])[0]),
        "gpm_b": np.ascontiguousarray(np.broadcast_to(np.asarray(p["norm_post_mix"], f)[0][None, :], (128, D))),
        "gpl_b": np.ascontiguousarray(np.broadcast_to(np.asarray(p["norm_post_mlp"], f)[0][None, :], (128, D))),
        "wspT": wspT,
        "bsp_b": np.ascontiguousarray(np.broadcast_to(bsp, (128, 512))),
        "lng": col4(np.asarray(p["ln_v_gain"])[0]),
        "lnb": col4(np.asarray(p["ln_v_bias"])[0]),
        "cos_t": cos, "sin_t": sin, "ident": ident, "rsw": rsw, "perm4": perm4, "mask2": mask2, "trilT": trilT,
    }
    return [dict(common, x=np.ascontiguousarray(np.asarray(xb, f))) for xb in x_b]


def kernel(**inputs):
    x = np.asarray(inputs["x"], np.float32)
    B, S, _ = x.shape
    if S not in _NC_CACHE:
        _NC_CACHE[S] = build_nc(S)
    nc = _NC_CACHE[S]
    in_maps = _host_inputs(S, [x[b] for b in range(B)], inputs)
    res = run_bass_kernel_spmd(nc, in_maps, core_ids=list(range(B)))
    return np.stack([np.asarray(r["out"], np.float32) for r in res.results], axis=0)
```

```python
import contextlib
import numpy as np
import ml_dtypes
import concourse.bass as bass
import concourse.mybir as mybir
from concourse.bass_utils import run_bass_kernel_spmd

F32 = mybir.dt.float32
BF16 = mybir.dt.bfloat16
AF = mybir.ActivationFunctionType
ALU = mybir.AluOpType

D = 1024
INW = 5376
Q0, K0, V0, U0, Z0, GA0, GB0 = 0, 768, 1536, 2304, 2816, 3328, 4352
EPS = 1e-6
SELF_SYNC = True
import os
STAGE = float(os.environ.get('KSTAGE', '99'))
NRING = 3


class Prog:
    ENGS = ("pe", "act", "dve", "pool", "sp")

    def __init__(self):
        self.ops = {e: [] for e in self.ENGS}
        self.cnt = {e: 0 for e in self.ENGS}
        self.last_w = {}
        self.readers = {}
        self.waited = {e: {} for e in self.ENGS}
        self.dma_cnt = {}
        self.semnames = set()
        self.pending = {}

    def barrier_sp(self):
        self.pending = dict(self.dma_cnt)

    def op(self, eng, fn, reads=(), writes=(), chan=None):
        deps = []
        for r in reads:
            if r in self.last_w:
                deps.append((self.last_w[r], "raw"))
            if r.startswith("ps"):
                for t in self.readers.get(r, ()):
                    if t[2] != eng:
                        deps.append((t, "rar"))
        for w in writes:
            if w in self.last_w:
                deps.append((self.last_w[w], "waw"))
            for t in self.readers.get(w, ()):
                deps.append((t, "war"))
        waits = {}
        for (s, v, e), kind in deps:
            if e == eng:
                if eng == "pe" or eng == "sp":
                    if eng == "pe":
                        continue
                elif not SELF_SYNC or kind == "war":
                    continue
            if self.waited[eng].get(s, 0) >= v:
                continue
            waits[s] = max(waits.get(s, 0), v)
        if eng == "sp" and self.pending:
            for s, v in self.pending.items():
                if self.waited[eng].get(s, 0) < v:
                    waits[s] = max(waits.get(s, 0), v)
            self.pending = {}
        for s, v in waits.items():
            self.waited[eng][s] = v
        if eng == "sp":
            assert chan is not None
            s = "d_" + chan
            self.dma_cnt[s] = self.dma_cnt.get(s, 0) + 16
            tok = (s, self.dma_cnt[s], eng)
            inc = 16
        else:
            s = "c_" + eng
            self.cnt[eng] += 1
            tok = (s, self.cnt[eng], eng)
            inc = 1
        self.semnames.add(s)
        self.ops[eng].append((fn, sorted(waits.items()), s, inc))
        for w in writes:
            self.last_w[w] = tok
            self.readers[w] = []
        for r in reads:
            self.readers.setdefault(r, []).append(tok)
        return tok

    def replay(self, eng_name, eng, sems, final_waits=()):
        for fn, waits, s, inc in self.ops[eng_name]:
            for ws, wv in waits:
                eng.wait_ge(sems[ws], wv)
            ins = fn(eng)
            ins.then_inc(sems[s], inc)
        for ws, wv in final_waits:
            eng.wait_ge(sems[ws], wv)


def build_nc(S):
    NT = S // 512
    NST = S // 2048
    nc = bass.Bass("TRN2", target_bir_lowering=False)
    P = Prog()

    def din(name, shape, dt=F32):
        return nc.dram_tensor(name, list(shape), dt, kind="ExternalInput")

    x = din("x", [S, D])
    w_in = din("w_in", [D, INW])
    w_ba = din("w_ba", [256, D])
    w_bg = din("w_bg", [512, D])
    w_out = din("w_out", [D, D])
    w1 = din("w1", [D, 4096])
    w2 = din("w2", [4096, D])
    gpre_d = din("gpre", [128, 8])
    gpre2_d = din("gpre2", [128, 8])
    gpm_d = din("gpm_b", [128, D])
    gpl_d = din("gpl_b", [128, D])
    wspT_d = din("wspT", [128, 512])
    bsp_d = din("bsp_b", [128, 512])
    lng_d = din("lng", [128, 4])
    lnb_d = din("lnb", [128, 4])
    cos_d = din("cos_t", [3, 128, S])
    sin_d = din("sin_t", [3, 128, S])
    ident_d = din("ident", [128, 128], BF16)
    rsw_d = din("rsw", [128, 128], BF16)
    perm4_d = din("perm4", [128, 128], BF16)
    mask2_d = din("mask2", [128, 512], BF16)
    tril_d = din("trilT", [128, 512])
    out = nc.dram_tensor("out", [S, D], F32, kind="ExternalOutput")

    def dscr(name, shape, dt):
        return nc.dram_tensor(name, list(shape), dt, kind="Internal")

    win_s = dscr("win_s", [128, 8, INW], BF16)
    w1_s = dscr("w1_s", [128, 8, 4096], BF16)
    w2_s = dscr("w2_s", [2, 128, 32, 512], BF16)
    wout_s = dscr("wout_s", [128, 8, D], BF16)
    wbg_s = dscr("wbg_s", [128, 4, D], BF16)
    wba_s = dscr("wba_s", [64, 4, D], BF16)
    k2_s = dscr("k2_s", [NST, 4, 128, 1024], BF16)
    v2_s = dscr("v2_s", [NST, 4, 128, 1280], BF16)
    acc2_s = dscr("acc2_s", [NST, 4, 65, 4, 512], F32)

    es = contextlib.ExitStack()
    with es:
        def sb(name, shape, dt):
            return es.enter_context(nc.sbuf_tensor("s_" + name, list(shape), dt))

        ring = [sb(f"ring{k}", [128, 4096], BF16) for k in range(NRING)]
        wba = sb("wba", [64, 4, D], BF16)
        wbg = sb("wbg", [128, 4, D], BF16)
        hT = sb("hT", [128, 8, 512], BF16)
        hT1 = sb("hT1", [128, 8, 512], BF16)
        xt = [sb(f"xt{k}", [128, D], F32) for k in range(4)]
        hb = [sb(f"hb{k}", [128, D], BF16) for k in range(2)]
        a2 = sb("a2", [128, 32, 512], BF16)
        rb = [sb(f"rb{k}", [128, 512], BF16) for k in range(2)]
        k01 = [[sb(f"k01_{g}{p}", [128, 2, 512], BF16) for p in range(2)] for g in range(2)]
        v01 = [[sb(f"v01_{g}{p}", [128, 4, 4, 80], BF16) for p in range(2)] for g in range(2)]
        q2q = sb("q2q", [128, 2, 512], BF16)
        k2q = sb("k2q", [128, 2, 512], BF16)
        k2p = sb("k2p", [128, 2, 512], BF16)
        v2q = sb("v2q", [128, 4, 4, 80], BF16)
        v2p = sb("v2p", [128, 4, 4, 80], BF16)
        acc0 = sb("acc0", [65, 2048], F32)
        acc2t = sb("acc2t", [65, 2048], F32)
        rden = sb("rden", [65, 2048], F32)
        fs = [sb(f"fs{k}", [128, 512], F32) for k in range(6)]
        cosT = [sb(f"cosT{k}", [128, 512], F32) for k in range(2)]
        sinT = [sb(f"sinT{k}", [128, 512], F32) for k in range(2)]
        ident = sb("ident", [128, 128], BF16)
        rsw = sb("rsw", [128, 128], BF16)
        perm4 = sb("perm4", [128, 128], BF16)
        mask2 = sb("mask2", [128, 512], BF16)
        wspT = sb("wspT", [128, 512], BF16)
        ones_bf = sb("ones_bf", [128, 128], BF16)
        ones_f = sb("ones_f", [128, 64], F32)
        Cgm = sb("Cgm", [128, 512], F32)
        lng = sb("lng", [128, 4], F32)
        lnb = sb("lnb", [128, 4], F32)
        gpre = sb("gpre", [128, 8], F32)
        gpre2 = sb("gpre2", [128, 8], F32)
        gpm = sb("gpm", [128, D], F32)
        gpl = sb("gpl", [128, D], F32)
        NSTAT = 16
        stat = sb("stat", [128, NSTAT, 16], F32)
        ps = [es.enter_context(nc.psum_tensor(f"ps{k}", [128, 512], F32)) for k in range(8)]

        bank_ctr = [0]

        def nb():
            b = bank_ctr[0] % 8
            bank_ctr[0] += 1
            return b

        stat_ctr = [0]

        def nstat():
            s = stat_ctr[0] % NSTAT
            stat_ctr[0] += 1
            return s

        fs_ctr = [0]

        def nfs():
            s = fs_ctr[0] % 6
            fs_ctr[0] += 1
            return s

        def dma(out_ap, in_ap, chan, reads, writes):
            P.op("sp", lambda e, o=out_ap, i=in_ap: e.dma_start(out=o, in_=i), reads, writes, chan=chan)

        def mm(out_ap, pairs, reads, writes):
            def fn(e, o=out_ap, pairs=pairs):
                n = len(pairs)
                ins = None
                for i, (l, r) in enumerate(pairs):
                    ins = e.matmul(o, lhsT=l, rhs=r, start=(i == 0), stop=(i == n - 1))
                return ins
            P.op("pe", fn, reads, writes)

        def act(out_ap, in_ap, func, reads, writes, scale=1.0, bias=0.0):
            P.op("act", lambda e: e.activation(out=out_ap, in_=in_ap, func=func, bias=bias, scale=scale), reads, writes)

        def tt(eng, out_ap, in0, in1, op, reads, writes):
            P.op(eng, lambda e: e.tensor_tensor(out=out_ap, in0=in0, in1=in1, op=op), reads, writes)

        def ts(eng, out_ap, in0, s1, s2, op0, op1, reads, writes):
            if s2 is None:
                P.op(eng, lambda e: e.tensor_scalar(out=out_ap, in0=in0, scalar1=s1, scalar2=None, op0=op0), reads, writes)
            else:
                P.op(eng, lambda e: e.tensor_scalar(out=out_ap, in0=in0, scalar1=s1, scalar2=s2, op0=op0, op1=op1), reads, writes)

        def stt(eng, out_ap, in0, scalar, in1, op0, op1, reads, writes):
            P.op(eng, lambda e: e.scalar_tensor_tensor(out=out_ap, in0=in0, scalar=scalar, in1=in1, op0=op0, op1=op1), reads, writes)

        def cp(eng, out_ap, in_ap, reads, writes):
            if eng == "act":
                P.op(eng, lambda e: e.activation(out=out_ap, in_=in_ap, func=AF.Copy), reads, writes)
            else:
                P.op(eng, lambda e: e.tensor_copy(out=out_ap, in_=in_ap), reads, writes)

        for i, (t, d, nm) in enumerate([(ident, ident_d, "ident"), (rsw, rsw_d, "rsw"), (perm4, perm4_d, "perm4"), (mask2, mask2_d, "mask2"),
                                        (lng, lng_d, "lng"), (lnb, lnb_d, "lnb"), (gpre, gpre_d, "gpre"),
                                        (gpre2, gpre2_d, "gpre2"), (gpm, gpm_d, "gpm"), (gpl, gpl_d, "gpl")]):
            dma(t[:], d.ap(), "c" + str(i), [], [nm])
        P.op("dve", lambda e: e.memset(ones_bf[:], 1.0), [], ["ones_bf"])
        P.op("dve", lambda e: e.memset(ones_f[:], 1.0), [], ["ones_f"])
        for g in range(2):
            for p in range(2):
                P.op("pool", lambda e, g=g, p=p: e.memset(v01[g][p][:, :, :, 64:80], 1.0), [], [f"v01_{g}{p}"])
        P.op("pool", lambda e: e.memset(v2q[:, :, :, 64:80], 1.0), [], ["v2q"])
        dma(fs[0][:], wspT_d.ap(), "fs0", [], ["fs0"])
        dma(fs[1][:], tril_d.ap(), "fs1", [], ["fs1"])
        dma(fs[2][:], bsp_d.ap(), "fs2", [], ["fs2"])
        tt("dve", wspT[:], fs[0][:], fs[1][:], ALU.mult, ["fs0", "fs1"], ["wspT"])
        mm(ps[0][:], [(ones_bf[:], wspT[:])], ["ones_bf", "wspT"], ["ps0"])
        for g in range(4):
            stt("dve", Cgm[:, g * 128:(g + 1) * 128], ps[0][:, g * 128:(g + 1) * 128], lnb[:, g:g + 1],
                fs[2][:, g * 128:(g + 1) * 128], ALU.mult, ALU.add, ["ps0", "lnb", "fs2"], ["Cgm"])

        stg = [0]

        def prep(src_ap, dst_ap, npart, ncol, scal, dst_res):
            k = stg[0] % 4
            r = stg[0] % NRING
            stg[0] += 1
            dma(xt[k][0:npart, 0:ncol], src_ap, f"xt{k}", [], [f"xt{k}"])
            if scal is None:
                cp("dve" if stg[0] % 2 else "pool", ring[r][0:npart, 0:ncol], xt[k][0:npart, 0:ncol], [f"xt{k}"], [f"ring{r}"])
            else:
                ts("dve" if stg[0] % 2 else "pool", ring[r][0:npart, 0:ncol], xt[k][0:npart, 0:ncol], scal, None, ALU.mult, None,
                   [f"xt{k}", "gpre", "gpre2"], [f"ring{r}"])
            dma(dst_ap, ring[r][0:npart, 0:ncol], f"ring{r}", [f"ring{r}"], [dst_res])

        for kc in range(8):
            for c0 in range(0, INW, 1024):
                c1 = min(c0 + 1024, INW)
                prep(w_in[kc * 128:(kc + 1) * 128, c0:c1], win_s[:, kc, c0:c1], 128, c1 - c0, gpre[:, kc:kc + 1], "win_s")
        for kc in range(8):
            for c0 in range(0, 4096, 1024):
                prep(w1[kc * 128:(kc + 1) * 128, c0:c0 + 1024], w1_s[:, kc, c0:c0 + 1024], 128, 1024, gpre2[:, kc:kc + 1], "w1_s")
        for f in range(32):
            k = stg[0] % 4
            r = stg[0] % NRING
            stg[0] += 1
            dma(xt[k][:], w2[f * 128:(f + 1) * 128, :], f"xt{k}", [], [f"xt{k}"])
            cp("dve" if f % 2 else "pool", ring[r][:, 0:1024], xt[k][:], [f"xt{k}"], [f"ring{r}"])
            for hf in range(2):
                dma(w2_s[hf, :, f, :], ring[r][:, hf * 512:(hf + 1) * 512], f"ring{r}", [f"ring{r}"], ["w2_s"])
        for kc in range(8):
            prep(w_out[kc * 128:(kc + 1) * 128, :], wout_s[:, kc, :], 128, 1024, None, "wout_s")
        for g in range(4):
            prep(w_bg[g * 128:(g + 1) * 128, :], wbg_s[:, g, :], 128, 1024, None, "wbg_s")
        for j in range(4):
            prep(w_ba[j * 64:(j + 1) * 64, :], wba_s[:, j, :], 64, 1024, None, "wba_s")
        P.barrier_sp()
        dma(wba[:], wba_s.ap(), "wba", ["wba_s"], ["wba"])
        dma(wbg[:], wbg_s.ap(), "wbg", ["wbg_s"], ["wbg"])

        def rms_stats(src_aps, src_res, what):
            s = nstat()
            sr = f"stat{s}"
            n = len(src_aps)
            st3 = stat[:, s, 0:6 * n].rearrange("p (a t) -> p a t", t=3)
            for i, a in enumerate(src_aps):
                P.op("dve", lambda e, i=i, a=a: e.bn_stats(st3[:, 2 * i:2 * i + 2, :], a), src_res, [sr])
            mv = stat[:, s, 12:14]
            P.op("dve", lambda e: e.bn_aggr(mv, st3), [sr], [sr])
            if what == "rms":
                stt("dve", stat[:, s, 14:15], stat[:, s, 12:13], stat[:, s, 12:13], stat[:, s, 13:14], ALU.mult, ALU.add, [sr], [sr])
                src = stat[:, s, 14:15]
            else:
                src = stat[:, s, 13:14]
            act(stat[:, s, 15:16], src, AF.Sqrt, [sr], [sr], scale=1.0, bias=EPS)
            P.op("dve", lambda e: e.reciprocal(stat[:, s, 15:16], stat[:, s, 15:16]), [sr], [sr])
            return s

        def norm_transpose(k, col, xres, hres_idx, g1blk=None):
            s = rms_stats([xt[k][:, 0:512], xt[k][:, 512:1024]], [xres], "rms")
            h = hb[hres_idx]
            hres = f"hb{hres_idx}"
            P.op("act", lambda e: e.activation(out=h[:], in_=xt[k][:], func=AF.Copy, scale=stat[:, s, 15:16]),
                 [xres, f"stat{s}"], [hres])
            b = nb()
            pst = ps[b][:].bitcast(BF16).rearrange("p (c t) -> p c t", t=128)

            def fn(e):
                ins = None
                for kc in range(8):
                    ins = e.transpose(pst[:, kc, :], h[:, kc * 128:(kc + 1) * 128], ident[:])
                return ins
            P.op("pe", fn, [hres, "ident"], [f"ps{b}"])
            cp("dve", hT[:, :, col:col + 128], pst, [f"ps{b}"], ["hT"])
            if g1blk is not None:
                for half in range(2):
                    bp = nb()

                    def fnp(e, half=half, bp=bp):
                        ins = None
                        for kk in range(4):
                            kc = 4 * half + kk
                            ins = e.matmul(ps[bp][:, kk * 128:(kk + 1) * 128], lhsT=h[:, kc * 128:(kc + 1) * 128], rhs=perm4[:],
                                           start=True, stop=True)
                        return ins
                    P.op("pe", fnp, [hres, "perm4"], [f"ps{bp}"])
                    dst = hT1[:, 4 * half:4 * half + 4, :].rearrange("p k (r i) -> p k r i", r=4)[:, :, :, 32 * g1blk:32 * g1blk + 32]
                    src = ps[bp][:].rearrange("p (k r i) -> p k r i", k=4, r=4)
                    cp("act" if half else "dve", dst, src, [f"ps{bp}"], ["hT1"])

        def rope(b, tabk, dst_ap, dst_res, scratch_slot):
            raw = a2[:, scratch_slot, :]
            rres = f"a2_{scratch_slot}"
            act(raw, ps[b][:], AF.Copy, [f"ps{b}"], [rres])
            b2 = nb()
            mm(ps[b2][:], [(rsw[:], raw)], ["rsw", rres], [f"ps{b2}"])
            f1 = nfs()
            f2 = nfs()
            tt("pool", fs[f1][:], raw, cosT[tabk][:], ALU.mult, [rres, f"cosT{tabk}"], [f"fs{f1}"])
            tt("dve", fs[f2][:], ps[b2][:], sinT[tabk][:], ALU.mult, [f"ps{b2}", f"sinT{tabk}"], [f"fs{f2}"])
            tt("pool", dst_ap, fs[f1][:], fs[f2][:], ALU.add, [f"fs{f1}", f"fs{f2}"], [dst_res])

        ep_ctr = [0]

        def attention(qT, qres, kcur, kcres, vcur, vcres, prev_of, evac):
            items = [(b, j) for b in range(4) for j in range(4)]
            bo_of = {}
            st = {}

            def emit_S(i):
                b, j = items[i]
                if b not in bo_of:
                    bo_of[b] = nb()
                pv = prev_of(b)
                jp, hh = j // 2, j % 2
                lo = 64 * hh
                bs_ = nb()
                sl = 28 + (ep_ctr[0] % 2)
                sp_ = 30 + (ep_ctr[0] % 2)
                ep_ctr[0] += 1
                ncol = 256 if pv is not None else 128
                E = a2[:, sl, 0:ncol]
                Pm = a2[:, sp_, 0:ncol]
                reads = list(qres) + [kcres]
                if pv is not None:
                    reads.append(pv[1])

                def fn(e, b=b, jp=jp, lo=lo, bs_=bs_, pv=pv):
                    Q = qT[lo:lo + 64, jp, b * 128:(b + 1) * 128]
                    ins = e.matmul(ps[bs_][:, 0:128], lhsT=kcur[lo:lo + 64, jp, b * 128:(b + 1) * 128], rhs=Q, start=True, stop=True)
                    if pv is not None:
                        kb = pv[4]
                        ins = e.matmul(ps[bs_][:, 128:256], lhsT=pv[0][lo:lo + 64, jp, kb * 128:(kb + 1) * 128], rhs=Q,
                                       start=True, stop=True)
                    return ins
                P.op("pe", fn, reads, [f"ps{bs_}"])
                act(E, ps[bs_][:, 0:ncol], AF.Exp, [f"ps{bs_}"], [f"a2_{sl}"], scale=0.125)
                tt("dve", Pm, E, mask2[:, 0:ncol], ALU.mult, [f"a2_{sl}", "mask2"], [f"a2_{sp_}"])
                st[i] = (pv, sp_)

            def emit_PV(i):
                b, j = items[i]
                pv, sp_ = st.pop(i)
                bo = bo_of[b]
                reads = [f"a2_{sp_}", vcres]
                if pv is not None:
                    reads.append(pv[3])

                def fn2(e, b=b, j=j, bo=bo, pv=pv, sp_=sp_):
                    o = ps[bo][0:65, j * 128:(j + 1) * 128]
                    ins = e.matmul(o, lhsT=vcur[:, b, j, 0:65], rhs=a2[:, sp_, 0:128], start=True, stop=(pv is None))
                    if pv is not None:
                        ins = e.matmul(o, lhsT=pv[2][:, pv[4], j, 0:65], rhs=a2[:, sp_, 128:256], start=False, stop=True)
                    return ins
                P.op("pe", fn2, reads, [f"ps{bo}"])
                if j == 3:
                    evac(b, bo)

            emit_S(0)
            for i in range(len(items)):
                if i + 1 < len(items):
                    emit_S(i + 1)
                emit_PV(i)

        def load_tables(g, off, k):
            dma(cosT[k][:], cos_d[g, :, off:off + 512], f"cosT{k}", [], [f"cosT{k}"])
            dma(sinT[k][:], sin_d[g, :, off:off + 512], f"sinT{k}", [], [f"sinT{k}"])

        x_g2 = x.ap().rearrange("(t i r) d -> t r i d", i=128, r=16)
        rA, rB = 0, 1
        dma(ring[rA][:].rearrange("p (k c) -> p k c", k=8)[:, :, 0:256], win_s[:, :, Q0 + 512:Q0 + 768], f"ring{rA}", ["win_s"], [f"ring{rA}"])
        dma(ring[rA][:].rearrange("p (k c) -> p k c", k=8)[:, :, 256:512], win_s[:, :, K0 + 512:K0 + 768], f"ring{rA}", ["win_s"], [f"ring{rA}"])
        dma(ring[rB][:, 0:2048].rearrange("p (k c) -> p k c", k=8), win_s[:, :, V0 + 512:V0 + 768], f"ring{rB}", ["win_s"], [f"ring{rB}"])
        WA = ring[rA][:].rearrange("p (k c) -> p k c", k=8)
        WB = ring[rB][:, 0:2048].rearrange("p (k c) -> p k c", k=8)
        qi = 0
        for T in range(NST if STAGE >= 2 else 0):
            for rq in range(4):
                tk = qi % 2
                qi += 1
                load_tables(2, T * 2048 + rq * 512, tk)
                if T > 0:
                    dma(k2p[:], k2_s[T - 1, rq].rearrange("p (c n) -> p c n", c=2), "k2p", ["k2_s"], ["k2p"])
                    dma(v2p[:], v2_s[T - 1, rq].rearrange("p (b j e) -> p b j e", b=4, j=4), "v2p", ["v2_s"], ["v2p"])
                for b in range(4):
                    dma(xt[b][:], x_g2[T, 4 * rq + b], f"xt{b}", [], [f"xt{b}"])
                for b in range(4):
                    norm_transpose(b, b * 128, f"xt{b}", b % 2)
                if STAGE < 2.2:
                    continue
                for c in range(4):
                    bk = nb()
                    mm(ps[bk][:], [(WA[:, kc, c * 128:(c + 1) * 128], hT[:, kc, :]) for kc in range(8)], [f"ring{rA}", "hT"], [f"ps{bk}"])
                    if STAGE < 2.25:
                        continue
                    if c < 2:
                        rope(bk, tk, q2q[:, c, :], "q2q", 24 + c % 2)
                    else:
                        rope(bk, tk, k2q[:, c - 2, :], "k2q", 24 + c % 2)
                if STAGE < 2.3:
                    continue
                for b in range(4):
                    bk = nb()
                    mm(ps[bk][:, 0:256], [(hT[:, kc, b * 128:(b + 1) * 128], WB[:, kc, :]) for kc in range(8)], [f"ring{rB}", "hT"], [f"ps{bk}"])
                    cp("act" if b % 2 else "dve", v2q[:, b, :, 0:64], ps[bk][:, 0:256].rearrange("p (j e) -> p j e", j=4), [f"ps{bk}"], ["v2q"])
                if STAGE < 2.4:
                    continue
                dma(k2_s[T, rq].rearrange("p (c n) -> p c n", c=2), k2q[:], "k2q", ["k2q"], ["k2_s"])
                dma(v2_s[T, rq].rearrange("p (b j e) -> p b j e", b=4, j=4), v2q[:], "v2q", ["v2q"], ["v2_s"])

                if STAGE < 2.5:
                    continue

                def prev2(b, T=T):
                    return None if T == 0 else (k2p, "k2p", v2p, "v2p", b)

                def evac2(b, bo):
                    cp("act", acc0[:, :].rearrange("p (c j b i) -> p j b c i", c=4, j=4, b=4)[:, :, b, :, :],
                       ps[bo][0:65, :].rearrange("p (j c i) -> p j c i", j=4, c=4), [f"ps{bo}"], ["acc0"])
                attention(q2q, ["q2q"], k2q, "k2q", v2q, "v2q", prev2, evac2)
                dma(acc2_s[T, :, :, rq, :].rearrange("c p n -> p c n"), acc0[:, :].rearrange("p (c n) -> p c n", c=4),
                    "acc0", ["acc0"], ["acc2_s"])

        wq = []
        wstate = {"n": 0, "issued": 0, "pieces": []}

        def wpiece(src_view_fn):
            wstate["pieces"].append(src_view_fn)
            return len(wstate["pieces"]) - 1

        def ring_view(r, kind):
            if kind == "k8":
                return ring[r][:].rearrange("p (k c) -> p k c", k=8)
            if kind == "k4":
                return ring[r][:].rearrange("p (k c) -> p k c", k=4)
            raise ValueError

        def wissue(upto):
            while wstate["issued"] <= min(upto, len(wstate["pieces"]) - 1):
                i = wstate["issued"]
                r = i % NRING
                kind, src, sres = wstate["pieces"][i]
                dma(ring_view(r, kind), src, f"ring{r}", [sres], [f"ring{r}"])
                wstate["issued"] += 1

        def wget(i, keep=None):
            wissue((i if keep is None else keep) + NRING - 1)
            r = i % NRING
            return ring_view(r, wstate["pieces"][i][0]), f"ring{r}"

        for t in range(NT if STAGE >= 3 else 0):
            T, c_in = t // 4, t % 4
            par = t % 2
            base = len(wstate["pieces"])
            for c0 in (Q0, K0, V0, U0, Z0, GA0, GB0, GA0 + 512, GB0 + 512):
                wpiece(("k8", win_s[:, :, c0:c0 + 512], "win_s"))
            for h in range(2):
                wpiece(("k4", wout_s[:, 4 * h:4 * h + 4, :], "wout_s"))
            for p in range(8):
                wpiece(("k8", w1_s[:, :, p * 512:(p + 1) * 512], "w1_s"))
            for hf in range(2):
                for p in range(4):
                    wpiece(("k8", w2_s[hf, :, 8 * p:8 * p + 8, :], "w2_s"))
            PQ, PK, PV, PU, PZ, PGA0, PGB0, PGA1, PGB1, PO0, PO1 = [base + i for i in range(11)]
            PW1 = base + 11
            PW2 = base + 19

            for b in range(4):
                dma(xt[b][:], x[t * 512 + b * 128:t * 512 + (b + 1) * 128, :], f"xt{b}", [], [f"xt{b}"])
            if STAGE >= 3.06:
                dma(acc2t[:], acc2_s[T, c_in].rearrange("p q n -> p (q n)"), "acc2t", ["acc2_s"], ["acc2t"])
            for b in range(4):
                norm_transpose(b, b * 128, f"xt{b}", b % 2, g1blk=(b if STAGE >= 3.07 else None))

            if STAGE < 3.1:
                continue
            for which, PIDX in (("q", PQ), ("k", PK)):
                W, wres = wget(PIDX)
                for g in range(2):
                    load_tables(g, t * 512, g) if which == "q" else None
                    for c2 in range(2):
                        bk = nb()
                        col = g * 256 + c2 * 128
                        hsrc, hres_ = (hT, "hT") if g == 0 else (hT1, "hT1")
                        mm(ps[bk][:], [(W[:, kc, col:col + 128], hsrc[:, kc, :]) for kc in range(8)], [wres, hres_], [f"ps{bk}"])
                        if which == "q":
                            rope(bk, g, a2[:, 20 + 2 * g + c2, :], f"a2_{20 + 2 * g + c2}", 24 + c2)
                        else:
                            rope(bk, g, k01[g][par][:, c2, :], f"k01_{g}{par}", 24 + c2)
            if STAGE < 3.15:
                continue
            W, wres = wget(PV)
            for g in range(2):
                for b in range(4):
                    bk = nb()
                    hsrc, hres_ = (hT, "hT") if g == 0 else (hT1, "hT1")
                    mm(ps[bk][:, 0:256], [(hsrc[:, kc, b * 128:(b + 1) * 128], W[:, kc, g * 256:(g + 1) * 256]) for kc in range(8)],
                       [wres, hres_], [f"ps{bk}"])
                    cp("act" if b % 2 else "dve", v01[g][par][:, b, :, 0:64], ps[bk][:, 0:256].rearrange("p (j e) -> p j e", j=4),
                       [f"ps{bk}"], [f"v01_{g}{par}"])
            W, wres = wget(PU)
            for c in range(4):
                bk = nb()
                mm(ps[bk][:], [(W[:, kc, c * 128:(c + 1) * 128], hT[:, kc, :]) for kc in range(8)], [wres, "hT"], [f"ps{bk}"])
                act(a2[:, c, :], ps[bk][:], AF.Gelu_apprx_tanh, [f"ps{bk}"], [f"a2_{c}"])
            W, wres = wget(PZ)
            for b in range(4):
                bk = nb()
                mm(ps[bk][:], [(hT[:, kc, b * 128:(b + 1) * 128], W[:, kc, :]) for kc in range(8)], [wres, "hT"], [f"ps{bk}"])
                f1 = nfs()
                act(fs[f1][:], ps[bk][:], AF.Gelu_apprx_tanh, [f"ps{bk}"], [f"fs{f1}"])
                s = rms_stats([fs[f1][:]], [f"fs{f1}"], "ln")
                ts("dve", a2[:, 4 + b, :], fs[f1][:], stat[:, s, 12:13], stat[:, s, 15:16], ALU.subtract, ALU.mult,
                   [f"fs{f1}", f"stat{s}"], [f"a2_{4 + b}"])

            if STAGE < 3.3:
                continue
            accn = acc0[:, :].rearrange("p (j n) -> p j n", j=4)
            for g in range(2):
                qres_l = [f"a2_{20 + 2 * g}", f"a2_{21 + 2 * g}"]
                kc_t, kc_r = k01[g][par], f"k01_{g}{par}"
                vc_t, vc_r = v01[g][par], f"v01_{g}{par}"
                kp_t, kp_r = k01[g][1 - par], f"k01_{g}{1 - par}"
                vp_t, vp_r = v01[g][1 - par], f"v01_{g}{1 - par}"
                if g == 0:
                    def prev_of(b, t=t, kc_t=kc_t, kc_r=kc_r, vc_t=vc_t, vc_r=vc_r, kp_t=kp_t, kp_r=kp_r, vp_t=vp_t, vp_r=vp_r):
                        if b > 0:
                            return (kc_t, kc_r, vc_t, vc_r, b - 1)
                        return None if t == 0 else (kp_t, kp_r, vp_t, vp_r, 3)

                    def evac(b, bo):
                        cp("act", accn[:, :, b * 128:(b + 1) * 128], ps[bo][0:65, :].rearrange("p (j n) -> p j n", j=4), [f"ps{bo}"], ["acc0"])
                else:
                    def prev_of(b, t=t, kp_t=kp_t, kp_r=kp_r, vp_t=vp_t, vp_r=vp_r):
                        return None if t == 0 else (kp_t, kp_r, vp_t, vp_r, b)

                    def evac(b, bo):
                        dst = acc0[:, :].rearrange("p (j i r) -> p j i r", j=4, r=4)[:, :, :, b]
                        tt("dve", dst, ps[bo][0:65, :].rearrange("p (j n) -> p j n", j=4), dst, ALU.add, [f"ps{bo}", "acc0"], ["acc0"])

                class QT:
                    def __init__(self, g):
                        self.g = g

                    def __getitem__(self, idx):
                        pr, jp, cols = idx
                        return a2[pr, 20 + 2 * self.g + jp, cols]
                attention(QT(g), qres_l, kc_t, kc_r, vc_t, vc_r, prev_of, evac)
            if STAGE < 3.4:
                continue
            a_nat = acc0[:, :].rearrange("p (j i q b) -> p j i q b", j=4, q=4, b=4)
            a_g2 = acc2t[:, :].rearrange("p (q j b i) -> p j i q b", q=4, j=4, b=4)
            for j in range(4):
                tt("dve", a_nat[:, j], a_nat[:, j], a_g2[:, j], ALU.add, ["acc0", "acc2t"], ["acc0"])
            P.op("dve", lambda e: e.reciprocal(rden[64:65, :], acc0[64:65, :]), ["acc0"], ["rden"])
            for j in range(4):
                bk = nb()
                mm(ps[bk][0:64, :], [(ones_f[64:65, 0:64], rden[64:65, j * 512:(j + 1) * 512])], ["ones_f", "rden"], [f"ps{bk}"])
                tt("dve", a2[0:64, 16 + j, :], acc0[0:64, j * 512:(j + 1) * 512], ps[bk][0:64, :], ALU.mult, [f"ps{bk}", "acc0"], [f"a2_{16 + j}"])

            if STAGE < 3.5:
                continue
            for g in range(4):
                bk = nb()

                def fn(e, g=g, bk=bk):
                    ins = None
                    for b in range(4):
                        ins = e.matmul(ps[bk][:, b * 128:(b + 1) * 128], lhsT=a2[:, 4 + b, g * 128:(g + 1) * 128],
                                       rhs=wspT[:, g * 128:(g + 1) * 128], start=True, stop=True)
                    return ins
                P.op("pe", fn, [f"a2_{4 + b}" for b in range(4)] + ["wspT"], [f"ps{bk}"])
                f1 = nfs()
                for b in range(4):
                    stt("dve", fs[f1][:, b * 128:(b + 1) * 128], ps[bk][:, b * 128:(b + 1) * 128], lng[:, g:g + 1],
                        Cgm[:, g * 128:(g + 1) * 128], ALU.mult, ALU.add, [f"ps{bk}", "lng", "Cgm"], [f"fs{f1}"])
                tt("pool", a2[:, 8 + g, :], fs[f1][:], a2[:, g, :], ALU.mult, [f"fs{f1}", f"a2_{g}"], [f"a2_{8 + g}"])

            if STAGE < 3.6:
                continue
            for oc in range(8):
                WGA, rga = wget(PGA0 if oc < 4 else PGA1)
                WGB, rgb = wget(PGB0 if oc < 4 else PGB1, keep=(PGA0 if oc < 4 else PGA1))
                co = (oc % 4) * 128
                bA, bB, bGA, bGB = nb(), nb(), nb(), nb()
                mm(ps[bA][:], [(wba[:, j, oc * 128:(oc + 1) * 128], a2[0:64, 16 + j, :]) for j in range(4)],
                   ["wba"] + [f"a2_{16 + j}" for j in range(4)], [f"ps{bA}"])
                mm(ps[bB][:], [(wbg[:, g, oc * 128:(oc + 1) * 128], a2[:, 8 + g, :]) for g in range(4)],
                   ["wbg"] + [f"a2_{8 + g}" for g in range(4)], [f"ps{bB}"])
                mm(ps[bGA][:], [(WGA[:, kc, co:co + 128], hT[:, kc, :]) for kc in range(8)], [rga, "hT"], [f"ps{bGA}"])
                mm(ps[bGB][:], [(WGB[:, kc, co:co + 128], hT[:, kc, :]) for kc in range(8)], [rgb, "hT"], [f"ps{bGB}"])
                sa, sbb = 24 + (oc % 2), 26 + (oc % 2)
                act(a2[:, sa, :], ps[bGA][:], AF.Sigmoid, [f"ps{bGA}"], [f"a2_{sa}"])
                act(a2[:, sbb, :], ps[bGB][:], AF.Sigmoid, [f"ps{bGB}"], [f"a2_{sbb}"])
                f1, f2 = nfs(), nfs()
                tt("dve", fs[f1][:], ps[bA][:], a2[:, sa, :], ALU.mult, [f"ps{bA}", f"a2_{sa}"], [f"fs{f1}"])
                tt("dve", fs[f2][:], ps[bB][:], a2[:, sbb, :], ALU.mult, [f"ps{bB}", f"a2_{sbb}"], [f"fs{f2}"])
                ms = oc if oc < 8 else oc
                tt("pool", a2[:, ms, :], fs[f1][:], fs[f2][:], ALU.add, [f"fs{f1}", f"fs{f2}"] + [f"a2_{8 + g}" for g in range(4)], [f"a2_{ms}"])

            if STAGE < 3.7:
                continue
            WO0, ro0 = wget(PO0)
            WO1, ro1 = wget(PO1, keep=PO0)
            for b in range(4):
                by = [nb(), nb()]
                for hf in range(2):
                    pairs = []
                    for kc in range(8):
                        Wp = WO0 if kc < 4 else WO1
                        pairs.append((a2[:, kc, b * 128:(b + 1) * 128], Wp[:, kc % 4, hf * 512:(hf + 1) * 512]))
                    mm(ps[by[hf]][:], pairs, [ro0, ro1] + [f"a2_{kc}" for kc in range(8)], [f"ps{by[hf]}"])
                s = rms_stats([ps[by[0]][:], ps[by[1]][:]], [f"ps{by[0]}", f"ps{by[1]}"], "rms")
                for hf in range(2):
                    f1 = nfs()
                    stt("dve", fs[f1][:], ps[by[hf]][:], stat[:, s, 15:16], gpm[:, hf * 512:(hf + 1) * 512], ALU.mult, ALU.mult,
                        [f"ps{by[hf]}", f"stat{s}", "gpm"], [f"fs{f1}"])
                    tt("pool", xt[b][:, hf * 512:(hf + 1) * 512], xt[b][:, hf * 512:(hf + 1) * 512], fs[f1][:], ALU.add,
                       [f"fs{f1}", f"xt{b}"], [f"xt{b}"])
            if STAGE < 3.8:
                continue
            for b in range(4):
                norm_transpose(b, b * 128, f"xt{b}", b % 2)
            for f in range(32):
                W, wres = wget(PW1 + f // 4)
                bk = nb()
                co = (f % 4) * 128
                mm(ps[bk][:], [(W[:, kc, co:co + 128], hT[:, kc, :]) for kc in range(8)], [wres, "hT"], [f"ps{bk}"])
                act(rb[f % 2][:], ps[bk][:], AF.Relu, [f"ps{bk}"], [f"rb{f % 2}"])
                tt("pool", a2[:, f, :], rb[f % 2][:], rb[f % 2][:], ALU.mult, [f"rb{f % 2}"], [f"a2_{f}"])
            if STAGE < 3.9:
                continue
            for hf in range(2):
                for p in range(4):
                    W, wres = wget(PW2 + hf * 4 + p)
                    for b in range(4):
                        bk = hf * 4 + b

                        def fn(e, W=W, p=p, b=b, bk=bk):
                            ins = None
                            for ff in range(8):
                                ins = e.matmul(ps[bk][:], lhsT=a2[:, 8 * p + ff, b * 128:(b + 1) * 128], rhs=W[:, ff, :],
                                               start=(p == 0 and ff == 0), stop=(p == 3 and ff == 7), skip_group_check=True)
                            return ins
                        P.op("pe", fn, [wres] + [f"a2_{8 * p + ff}" for ff in range(8)], [f"ps{bk}"])
            bank_ctr[0] = 0
            for b in range(4):
                s = rms_stats([ps[b][:], ps[4 + b][:]], [f"ps{b}", f"ps{4 + b}"], "rms")
                for hf in range(2):
                    f1 = nfs()
                    bk = hf * 4 + b
                    stt("dve", fs[f1][:], ps[bk][:], stat[:, s, 15:16], gpl[:, hf * 512:(hf + 1) * 512], ALU.mult, ALU.mult,
                        [f"ps{bk}", f"stat{s}", "gpl"], [f"fs{f1}"])
                    tt("pool", xt[b][:, hf * 512:(hf + 1) * 512], xt[b][:, hf * 512:(hf + 1) * 512], fs[f1][:], ALU.add,
                       [f"fs{f1}", f"xt{b}"], [f"xt{b}"])
                dma(out[t * 512 + b * 128:t * 512 + (b + 1) * 128, :], xt[b][:], f"xt{b}", [f"xt{b}"], ["out"])

        sems = {name: es.enter_context(nc.semaphore(name)) for name in sorted(P.semnames)}
        final = [(s, v) for s, v in P.dma_cnt.items()]
        with nc.Block() as block:
            @block.tensor
            def _(e):
                P.replay("pe", e, sems)

            @block.scalar
            def _(e):
                P.replay("act", e, sems)

            @block.vector
            def _(e):
                P.replay("dve", e, sems)

            @block.gpsimd
            def _(e):
                P.replay("pool", e, sems)

            @block.sync
            def _(e):
                P.replay("sp", e, sems, final_waits=final)
    return nc


def _tables(S):
    half = 32
    inv = (10000.0 ** (-np.arange(half, dtype=np.float32) / half)).astype(np.float32)
    idx = np.arange(S)
    pos = [idx.copy()]
    n, r, i = idx // 512, (idx % 512) // 128, idx % 128
    pos.append(512 * n + 4 * i + r)
    T, r, i = idx // 2048, (idx % 2048) // 128, idx % 128
    pos.append(2048 * T + 16 * i + r)
    cos = np.zeros((3, 128, S), np.float32)
    sin = np.zeros((3, 128, S), np.float32)
    m = np.arange(128)
    fr = inv[m % 32]
    sgn = np.where((m % 64) < 32, -1.0, 1.0).astype(np.float32)
    for g in range(3):
        ang = (pos[g].astype(np.float32)[None, :] * fr[:, None]).astype(np.float32)
        cos[g] = np.cos(ang)
        sin[g] = np.sin(ang) * sgn[:, None]
    return cos, sin


def _consts():
    bf = ml_dtypes.bfloat16
    ident = np.eye(128, dtype=np.float32).astype(bf)
    m = np.arange(128)
    sw = np.where((m % 64) < 32, m + 32, m - 32)
    rsw = np.zeros((128, 128), np.float32)
    rsw[sw, m] = 1.0
    k = np.arange(128)[:, None]
    q = np.arange(128)[None, :]
    half = np.concatenate([(k <= q), (k >= q)], axis=1).astype(np.float32)
    mask2 = np.concatenate([half, half], axis=1).astype(bf)
    tril = (k <= q).astype(np.float32)
    trilT = np.tile(tril, (1, 4)).astype(np.float32)
    n = np.arange(128)
    perm4 = np.zeros((128, 128), np.float32)
    perm4[n, 32 * (n % 4) + n // 4] = 1.0
    return ident, rsw.astype(bf), mask2, trilT, perm4.astype(bf)


_NC_CACHE = {}


def _host_inputs(S, x_b, p):
    cos, sin = _tables(S)
    ident, rsw, mask2, trilT, perm4 = _consts()
    f = np.float32

    def col8(v):
        return np.ascontiguousarray(np.asarray(v, f).reshape(8, 128).T)

    def col4(v):
        return np.ascontiguousarray(np.asarray(v, f).reshape(4, 128).T)
    wsp = np.asarray(p["w_spatial"], f)[0]
    wspT = np.ascontiguousarray(wsp.transpose(2, 0, 1).reshape(128, 512))
    bsp = np.asarray(p["b_spatial"], f)[0].reshape(1, 512)
    common = {
        "w_in": np.ascontiguousarray(np.asarray(p["w_in"], f)[0]),
        "w_ba": np.ascontiguousarray(np.asarray(p["w_branch_attn"], f)[0]),
        "w_bg": np.ascontiguousarray(np.asarray(p["w_branch_gmlp"], f)[0]),
        "w_out": np.ascontiguousarray(np.asarray(p["w_out"], f)[0]),
        "w1": np.ascontiguousarray(np.asarray(p["w_mlp_in"], f)[0]),
        "w2": np.ascontiguousarray(np.asarray(p["w_mlp_out"], f)[0]),
        "gpre": col8(np.asarray(p["norm_pre_mix"])[0]),
        "gpre2": col8(np.asarray(p["norm_pre_mlp"])[0]),
        "gpm_b": np.ascontiguousarray(np.broadcast_to(np.asarray(p["norm_post_mix"], f)[0][None, :], (128, D))),
        "gpl_b": np.ascontiguousarray(np.broadcast_to(np.asarray(p["norm_post_mlp"], f)[0][None, :], (128, D))),
        "wspT": wspT,
        "bsp_b": np.ascontiguousarray(np.broadcast_to(bsp, (128, 512))),
        "lng": col4(np.asarray(p["ln_v_gain"])[0]),
        "lnb": col4(np.asarray(p["ln_v_bias"])[0]),
        "cos_t": cos, "sin_t": sin, "ident": ident, "rsw": rsw, "perm4": perm4, "mask2": mask2, "trilT": trilT,
    }
    return [dict(common, x=np.ascontiguousarray(np.asarray(xb, f))) for xb in x_b]


def kernel(**inputs):
    x = np.asarray(inputs["x"], np.float32)
    B, S, _ = x.shape
    if S not in _NC_CACHE:
        _NC_CACHE[S] = build_nc(S)
    nc = _NC_CACHE[S]
    in_maps = _host_inputs(S, [x[b] for b in range(B)], inputs)
    res = run_bass_kernel_spmd(nc, in_maps, core_ids=list(range(B)))
    return np.stack([np.asarray(r["out"], np.float32) for r in res.results], axis=0)
```

```python
import contextlib
import numpy as np
import ml_dtypes
import concourse.bass as bass
import concourse.mybir as mybir
from concourse.bass_utils import run_bass_kernel_spmd

F32 = mybir.dt.float32
BF16 = mybir.dt.bfloat16
AF = mybir.ActivationFunctionType
ALU = mybir.AluOpType

D = 1024
INW = 5376
Q0, K0, V0, U0, Z0, GA0, GB0 = 0, 768, 1536, 2304, 2816, 3328, 4352
EPS = 1e-6
SELF_SYNC = True
import os
STAGE = float(os.environ.get('KSTAGE', '99'))
NRING = 5


class Prog:
    ENGS = ("pe", "act", "dve", "pool", "sp")

    def __init__(self):
        self.ops = {e: [] for e in self.ENGS}
        self.cnt = {e: 0 for e in self.ENGS}
        self.last_w = {}
        self.readers = {}
        self.waited = {e: {} for e in self.ENGS}
        self.dma_cnt = {}
        self.semnames = set()
        self.pending = {}

    def barrier_sp(self):
        self.pending = dict(self.dma_cnt)

    def op(self, eng, fn, reads=(), writes=(), chan=None):
        deps = []
        for r in reads:
            if r in self.last_w:
                deps.append((self.last_w[r], "raw"))
            if r.startswith("ps"):
                for t in self.readers.get(r, ()):
                    if t[2] != eng:
                        deps.append((t, "rar"))
        for w in writes:
            if w in self.last_w:
                deps.append((self.last_w[w], "waw"))
            for t in self.readers.get(w, ()):
                deps.append((t, "war"))
        waits = {}
        for (s, v, e), kind in deps:
            if e == eng:
                if eng == "pe" or eng == "sp":
                    if eng == "pe":
                        continue
                elif not SELF_SYNC or kind == "war":
                    continue
            if self.waited[eng].get(s, 0) >= v:
                continue
            waits[s] = max(waits.get(s, 0), v)
        if eng == "sp" and self.pending:
            for s, v in self.pending.items():
                if self.waited[eng].get(s, 0) < v:
                    waits[s] = max(waits.get(s, 0), v)
            self.pending = {}
        for s, v in waits.items():
            self.waited[eng][s] = v
        if eng == "sp":
            assert chan is not None
            s = "d_" + chan
            self.dma_cnt[s] = self.dma_cnt.get(s, 0) + 16
            tok = (s, self.dma_cnt[s], eng)
            inc = 16
        else:
            s = "c_" + eng
            self.cnt[eng] += 1
            tok = (s, self.cnt[eng], eng)
            inc = 1
        self.semnames.add(s)
        self.ops[eng].append((fn, sorted(waits.items()), s, inc))
        for w in writes:
            self.last_w[w] = tok
            self.readers[w] = []
        for r in reads:
            self.readers.setdefault(r, []).append(tok)
        return tok

    def replay(self, eng_name, eng, sems, final_waits=()):
        for fn, waits, s, inc in self.ops[eng_name]:
            for ws, wv in waits:
                eng.wait_ge(sems[ws], wv)
            ins = fn(eng)
            ins.then_inc(sems[s], inc)
        for ws, wv in final_waits:
            eng.wait_ge(sems[ws], wv)


def build_nc(S):
    NT = S // 512
    NST = S // 2048
    nc = bass.Bass("TRN2", target_bir_lowering=False)
    P = Prog()

    def din(name, shape, dt=F32):
        return nc.dram_tensor(name, list(shape), dt, kind="ExternalInput")

    x = din("x", [S, D])
    w_in = din("w_in", [D, INW])
    w_ba = din("w_ba", [256, D])
    w_bg = din("w_bg", [512, D])
    w_out = din("w_out", [D, D])
    w1 = din("w1", [D, 4096])
    w2 = din("w2", [4096, D])
    gpre_d = din("gpre", [128, 8])
    gpre2_d = din("gpre2", [128, 8])
    gpm_d = din("gpm_b", [128, D])
    gpl_d = din("gpl_b", [128, D])
    wspT_d = din("wspT", [128, 512])
    bsp_d = din("bsp_b", [128, 512])
    lng_d = din("lng", [128, 4])
    lnb_d = din("lnb", [128, 4])
    cos_d = din("cos_t", [3, 128, S])
    sin_d = din("sin_t", [3, 128, S])
    ident_d = din("ident", [128, 128], BF16)
    rsw_d = din("rsw", [128, 128], BF16)
    perm4_d = din("perm4", [128, 128], BF16)
    mask2_d = din("mask2", [128, 512], BF16)
    tril_d = din("trilT", [128, 512])
    out = nc.dram_tensor("out", [S, D], F32, kind="ExternalOutput")

    def dscr(name, shape, dt):
        return nc.dram_tensor(name, list(shape), dt, kind="Internal")

    win_s = dscr("win_s", [128, 8, INW], BF16)
    w1_s = dscr("w1_s", [128, 8, 4096], BF16)
    w2_s = dscr("w2_s", [2, 128, 32, 512], BF16)
    wout_s = dscr("wout_s", [128, 8, D], BF16)
    wbg_s = dscr("wbg_s", [128, 4, D], BF16)
    wba_s = dscr("wba_s", [64, 4, D], BF16)
    k2_s = dscr("k2_s", [NST, 4, 128, 1024], BF16)
    v2_s = dscr("v2_s", [NST, 4, 128, 1280], BF16)
    acc2_s = dscr("acc2_s", [NST, 4, 65, 4, 512], F32)

    es = contextlib.ExitStack()
    with es:
        def sb(name, shape, dt):
            return es.enter_context(nc.sbuf_tensor("s_" + name, list(shape), dt))

        ring = [sb(f"ring{k}", [128, 4096], BF16) for k in range(NRING)]
        wba = sb("wba", [64, 4, D], BF16)
        wbg = sb("wbg", [128, 4, D], BF16)
        hT = sb("hT", [128, 8, 512], BF16)
        hT1 = sb("hT1", [128, 8, 512], BF16)
        xt = [sb(f"xt{k}", [128, D], F32) for k in range(4)]
        hb = [sb(f"hb{k}", [128, D], BF16) for k in range(2)]
        a2 = sb("a2", [128, 32, 512], BF16)
        rb = [sb(f"rb{k}", [128, 512], BF16) for k in range(2)]
        k01 = [[sb(f"k01_{g}{p}", [128, 2, 512], BF16) for p in range(2)] for g in range(2)]
        v01 = [[sb(f"v01_{g}{p}", [128, 4, 4, 80], BF16) for p in range(2)] for g in range(2)]
        q2q = sb("q2q", [128, 2, 512], BF16)
        k2q = sb("k2q", [128, 2, 512], BF16)
        k2p = sb("k2p", [128, 2, 512], BF16)
        v2q = sb("v2q", [128, 4, 4, 80], BF16)
        v2p = sb("v2p", [128, 4, 4, 80], BF16)
        acc0 = sb("acc0", [65, 2048], F32)
        acc2t = sb("acc2t", [65, 2048], F32)
        fs = [sb(f"fs{k}", [128, 512], F32) for k in range(6)]
        cosT = [sb(f"cosT{k}", [128, 512], F32) for k in range(2)]
        sinT = [sb(f"sinT{k}", [128, 512], F32) for k in range(2)]
        ident = sb("ident", [128, 128], BF16)
        rsw = sb("rsw", [128, 128], BF16)
        perm4 = sb("perm4", [128, 128], BF16)
        mask2 = sb("mask2", [128, 512], BF16)
        wspT = sb("wspT", [128, 512], BF16)
        ones_bf = sb("ones_bf", [128, 128], BF16)
        ones_f = sb("ones_f", [128, 64], F32)
        Cgm = sb("Cgm", [128, 512], F32)
        lng = sb("lng", [128, 4], F32)
        lnb = sb("lnb", [128, 4], F32)
        gpre = sb("gpre", [128, 8], F32)
        gpre2 = sb("gpre2", [128, 8], F32)
        gpm = sb("gpm", [128, D], F32)
        gpl = sb("gpl", [128, D], F32)
        NSTAT = 16
        stat = sb("stat", [128, NSTAT, 16], F32)
        ps = [es.enter_context(nc.psum_tensor(f"ps{k}", [128, 512], F32)) for k in range(8)]

        bank_ctr = [0]

        def nb():
            b = bank_ctr[0] % 8
            bank_ctr[0] += 1
            return b

        stat_ctr = [0]

        def nstat():
            s = stat_ctr[0] % NSTAT
            stat_ctr[0] += 1
            return s

        fs_ctr = [0]

        def nfs():
            s = fs_ctr[0] % 6
            fs_ctr[0] += 1
            return s

        def dma(out_ap, in_ap, chan, reads, writes):
            P.op("sp", lambda e, o=out_ap, i=in_ap: e.dma_start(out=o, in_=i), reads, writes, chan=chan)

        def mm(out_ap, pairs, reads, writes):
            def fn(e, o=out_ap, pairs=pairs):
                n = len(pairs)
                ins = None
                for i, (l, r) in enumerate(pairs):
                    ins = e.matmul(o, lhsT=l, rhs=r, start=(i == 0), stop=(i == n - 1))
                return ins
            P.op("pe", fn, reads, writes)

        def act(out_ap, in_ap, func, reads, writes, scale=1.0, bias=0.0):
            P.op("act", lambda e: e.activation(out=out_ap, in_=in_ap, func=func, bias=bias, scale=scale), reads, writes)

        def tt(eng, out_ap, in0, in1, op, reads, writes):
            P.op(eng, lambda e: e.tensor_tensor(out=out_ap, in0=in0, in1=in1, op=op), reads, writes)

        def ts(eng, out_ap, in0, s1, s2, op0, op1, reads, writes):
            if s2 is None:
                P.op(eng, lambda e: e.tensor_scalar(out=out_ap, in0=in0, scalar1=s1, scalar2=None, op0=op0), reads, writes)
            else:
                P.op(eng, lambda e: e.tensor_scalar(out=out_ap, in0=in0, scalar1=s1, scalar2=s2, op0=op0, op1=op1), reads, writes)

        def stt(eng, out_ap, in0, scalar, in1, op0, op1, reads, writes):
            P.op(eng, lambda e: e.scalar_tensor_tensor(out=out_ap, in0=in0, scalar=scalar, in1=in1, op0=op0, op1=op1), reads, writes)

        def cp(eng, out_ap, in_ap, reads, writes):
            if eng == "act":
                P.op(eng, lambda e: e.activation(out=out_ap, in_=in_ap, func=AF.Copy), reads, writes)
            else:
                P.op(eng, lambda e: e.tensor_copy(out=out_ap, in_=in_ap), reads, writes)

        for i, (t, d, nm) in enumerate([(ident, ident_d, "ident"), (rsw, rsw_d, "rsw"), (perm4, perm4_d, "perm4"), (mask2, mask2_d, "mask2"),
                                        (lng, lng_d, "lng"), (lnb, lnb_d, "lnb"), (gpre, gpre_d, "gpre"),
                                        (gpre2, gpre2_d, "gpre2"), (gpm, gpm_d, "gpm"), (gpl, gpl_d, "gpl")]):
            dma(t[:], d.ap(), "c" + str(i), [], [nm])
        P.op("dve", lambda e: e.memset(ones_bf[:], 1.0), [], ["ones_bf"])
        P.op("dve", lambda e: e.memset(ones_f[:], 1.0), [], ["ones_f"])
        for g in range(2):
            for p in range(2):
                P.op("pool", lambda e, g=g, p=p: e.memset(v01[g][p][:, :, :, 64:80], 1.0), [], [f"v01_{g}{p}"])
        P.op("pool", lambda e: e.memset(v2q[:, :, :, 64:80], 1.0), [], ["v2q"])
        dma(fs[0][:], wspT_d.ap(), "fs0", [], ["fs0"])
        dma(fs[1][:], tril_d.ap(), "fs1", [], ["fs1"])
        dma(fs[2][:], bsp_d.ap(), "fs2", [], ["fs2"])
        tt("dve", wspT[:], fs[0][:], fs[1][:], ALU.mult, ["fs0", "fs1"], ["wspT"])
        mm(ps[0][:], [(ones_bf[:], wspT[:])], ["ones_bf", "wspT"], ["ps0"])
        for g in range(4):
            stt("dve", Cgm[:, g * 128:(g + 1) * 128], ps[0][:, g * 128:(g + 1) * 128], lnb[:, g:g + 1],
                fs[2][:, g * 128:(g + 1) * 128], ALU.mult, ALU.add, ["ps0", "lnb", "fs2"], ["Cgm"])

        stg = [0]

        def prep(src_ap, dst_ap, npart, ncol, scal, dst_res):
            k = stg[0] % 4
            r = stg[0] % NRING
            stg[0] += 1
            dma(xt[k][0:npart, 0:ncol], src_ap, f"xt{k}", [], [f"xt{k}"])
            if scal is None:
                cp("dve" if stg[0] % 2 else "act", ring[r][0:npart, 0:ncol], xt[k][0:npart, 0:ncol], [f"xt{k}"], [f"ring{r}"])
            elif stg[0] % 2:
                ts("dve", ring[r][0:npart, 0:ncol], xt[k][0:npart, 0:ncol], scal, None, ALU.mult, None,
                   [f"xt{k}", "gpre", "gpre2"], [f"ring{r}"])
            else:
                P.op("act", lambda e, o=ring[r][0:npart, 0:ncol], i=xt[k][0:npart, 0:ncol], sc=scal:
                     e.activation(out=o, in_=i, func=AF.Copy, scale=sc), [f"xt{k}", "gpre", "gpre2"], [f"ring{r}"])
            dma(dst_ap, ring[r][0:npart, 0:ncol], f"ring{r}", [f"ring{r}"], [dst_res])

        for kc in range(8):
            for c0 in range(0, INW, 1024):
                c1 = min(c0 + 1024, INW)
                prep(w_in[kc * 128:(kc + 1) * 128, c0:c1], win_s[:, kc, c0:c1], 128, c1 - c0, gpre[:, kc:kc + 1], "win_s")
        for kc in range(8):
            for c0 in range(0, 4096, 1024):
                prep(w1[kc * 128:(kc + 1) * 128, c0:c0 + 1024], w1_s[:, kc, c0:c0 + 1024], 128, 1024, gpre2[:, kc:kc + 1], "w1_s")
        for f in range(32):
            k = stg[0] % 4
            r = stg[0] % NRING
            stg[0] += 1
            dma(xt[k][:], w2[f * 128:(f + 1) * 128, :], f"xt{k}", [], [f"xt{k}"])
            cp("dve" if f % 2 else "act", ring[r][:, 0:1024], xt[k][:], [f"xt{k}"], [f"ring{r}"])
            for hf in range(2):
                dma(w2_s[hf, :, f, :], ring[r][:, hf * 512:(hf + 1) * 512], f"ring{r}", [f"ring{r}"], ["w2_s"])
        for kc in range(8):
            prep(w_out[kc * 128:(kc + 1) * 128, :], wout_s[:, kc, :], 128, 1024, None, "wout_s")
        for g in range(4):
            prep(w_bg[g * 128:(g + 1) * 128, :], wbg_s[:, g, :], 128, 1024, None, "wbg_s")
        for j in range(4):
            prep(w_ba[j * 64:(j + 1) * 64, :], wba_s[:, j, :], 64, 1024, None, "wba_s")
        P.barrier_sp()
        dma(wba[:], wba_s.ap(), "wba", ["wba_s"], ["wba"])
        dma(wbg[:], wbg_s.ap(), "wbg", ["wbg_s"], ["wbg"])

        def rms_stats(src_aps, src_res, what):
            s = nstat()
            sr = f"stat{s}"
            n = len(src_aps)
            st3 = stat[:, s, 0:6 * n].rearrange("p (a t) -> p a t", t=3)
            for i, a in enumerate(src_aps):
                P.op("dve", lambda e, i=i, a=a: e.bn_stats(st3[:, 2 * i:2 * i + 2, :], a), src_res, [sr])
            mv = stat[:, s, 12:14]
            P.op("dve", lambda e: e.bn_aggr(mv, st3), [sr], [sr])
            if what == "rms":
                stt("dve", stat[:, s, 14:15], stat[:, s, 12:13], stat[:, s, 12:13], stat[:, s, 13:14], ALU.mult, ALU.add, [sr], [sr])
                src = stat[:, s, 14:15]
            else:
                src = stat[:, s, 13:14]
            act(stat[:, s, 15:16], src, AF.Sqrt, [sr], [sr], scale=1.0, bias=EPS)
            P.op("dve", lambda e: e.reciprocal(stat[:, s, 15:16], stat[:, s, 15:16]), [sr], [sr])
            return s

        def norm_transpose(k, col, xres, hres_idx, g1blk=None):
            s = rms_stats([xt[k][:, 0:512], xt[k][:, 512:1024]], [xres], "rms")
            h = hb[hres_idx]
            hres = f"hb{hres_idx}"
            P.op("act", lambda e: e.activation(out=h[:], in_=xt[k][:], func=AF.Copy, scale=stat[:, s, 15:16]),
                 [xres, f"stat{s}"], [hres])
            b = nb()
            pst = ps[b][:].bitcast(BF16).rearrange("p (c t) -> p c t", t=128)

            def fn(e):
                ins = None
                for kc in range(8):
                    ins = e.transpose(pst[:, kc, :], h[:, kc * 128:(kc + 1) * 128], ident[:])
                return ins
            P.op("pe", fn, [hres, "ident"], [f"ps{b}"])
            cp("dve", hT[:, :, col:col + 128], pst, [f"ps{b}"], ["hT"])
            if g1blk is not None:
                for half in range(2):
                    bp = nb()

                    def fnp(e, half=half, bp=bp):
                        ins = None
                        for kk in range(4):
                            kc = 4 * half + kk
                            ins = e.matmul(ps[bp][:, kk * 128:(kk + 1) * 128], lhsT=h[:, kc * 128:(kc + 1) * 128], rhs=perm4[:],
                                           start=True, stop=True)
                        return ins
                    P.op("pe", fnp, [hres, "perm4"], [f"ps{bp}"])
                    dst = hT1[:, 4 * half:4 * half + 4, :].rearrange("p k (r i) -> p k r i", r=4)[:, :, :, 32 * g1blk:32 * g1blk + 32]
                    src = ps[bp][:].rearrange("p (k r i) -> p k r i", k=4, r=4)
                    cp("act" if half else "dve", dst, src, [f"ps{bp}"], ["hT1"])

        def rope(b, tabk, dst_ap, dst_res, scratch_slot):
            raw = a2[:, scratch_slot, :]
            rres = f"a2_{scratch_slot}"
            act(raw, ps[b][:], AF.Copy, [f"ps{b}"], [rres])
            b2 = nb()
            mm(ps[b2][:], [(rsw[:], raw)], ["rsw", rres], [f"ps{b2}"])
            f1 = nfs()
            f2 = nfs()
            tt("pool", fs[f1][:], raw, cosT[tabk][:], ALU.mult, [rres, f"cosT{tabk}"], [f"fs{f1}"])
            tt("dve", fs[f2][:], ps[b2][:], sinT[tabk][:], ALU.mult, [f"ps{b2}", f"sinT{tabk}"], [f"fs{f2}"])
            tt("pool", dst_ap, fs[f1][:], fs[f2][:], ALU.add, [f"fs{f1}", f"fs{f2}"], [dst_res])

        ep_ctr = [0]

        def attention(qT, qres, kcur, kcres, vcur, vcres, prev_of, evac):
            items = [(b, j) for b in range(4) for j in range(4)]
            bo_of = {}
            st = {}

            def emit_S(i):
                b, j = items[i]
                if b not in bo_of:
                    bo_of[b] = nb()
                pv = prev_of(b)
                jp, hh = j // 2, j % 2
                lo = 64 * hh
                bs_ = nb()
                sl = 28 + (ep_ctr[0] % 2)
                sp_ = 30 + (ep_ctr[0] % 2)
                ep_ctr[0] += 1
                ncol = 256 if pv is not None else 128
                E = a2[:, sl, 0:ncol]
                Pm = a2[:, sp_, 0:ncol]
                reads = list(qres) + [kcres]
                if pv is not None:
                    reads.append(pv[1])

                def fn(e, b=b, jp=jp, lo=lo, bs_=bs_, pv=pv):
                    Q = qT[lo:lo + 64, jp, b * 128:(b + 1) * 128]
                    ins = e.matmul(ps[bs_][:, 0:128], lhsT=kcur[lo:lo + 64, jp, b * 128:(b + 1) * 128], rhs=Q, start=True, stop=True)
                    if pv is not None:
                        kb = pv[4]
                        ins = e.matmul(ps[bs_][:, 128:256], lhsT=pv[0][lo:lo + 64, jp, kb * 128:(kb + 1) * 128], rhs=Q,
                                       start=True, stop=True)
                    return ins
                P.op("pe", fn, reads, [f"ps{bs_}"])
                act(E, ps[bs_][:, 0:ncol], AF.Exp, [f"ps{bs_}"], [f"a2_{sl}"], scale=0.125)
                tt("dve", Pm, E, mask2[:, 0:ncol], ALU.mult, [f"a2_{sl}", "mask2"], [f"a2_{sp_}"])
                st[i] = (pv, sp_)

            def emit_PV(i):
                b, j = items[i]
                pv, sp_ = st.pop(i)
                bo = bo_of[b]
                reads = [f"a2_{sp_}", vcres]
                if pv is not None:
                    reads.append(pv[3])

                def fn2(e, b=b, j=j, bo=bo, pv=pv, sp_=sp_):
                    o = ps[bo][0:65, j * 128:(j + 1) * 128]
                    ins = e.matmul(o, lhsT=vcur[:, b, j, 0:65], rhs=a2[:, sp_, 0:128], start=True, stop=(pv is None))
                    if pv is not None:
                        ins = e.matmul(o, lhsT=pv[2][:, pv[4], j, 0:65], rhs=a2[:, sp_, 128:256], start=False, stop=True)
                    return ins
                P.op("pe", fn2, reads, [f"ps{bo}"])
                if j == 3:
                    evac(b, bo)

            emit_S(0)
            for i in range(len(items)):
                if i + 1 < len(items):
                    emit_S(i + 1)
                emit_PV(i)

        def load_tables(g, off, k):
            dma(cosT[k][:], cos_d[g, :, off:off + 512], f"cosT{k}", [], [f"cosT{k}"])
            dma(sinT[k][:], sin_d[g, :, off:off + 512], f"sinT{k}", [], [f"sinT{k}"])

        x_g2 = x.ap().rearrange("(t i r) d -> t r i d", i=128, r=16)
        rA, rB = 0, 1
        dma(ring[rA][:].rearrange("p (k c) -> p k c", k=8)[:, :, 0:256], win_s[:, :, Q0 + 512:Q0 + 768], f"ring{rA}", ["win_s"], [f"ring{rA}"])
        dma(ring[rA][:].rearrange("p (k c) -> p k c", k=8)[:, :, 256:512], win_s[:, :, K0 + 512:K0 + 768], f"ring{rA}", ["win_s"], [f"ring{rA}"])
        dma(ring[rB][:, 0:2048].rearrange("p (k c) -> p k c", k=8), win_s[:, :, V0 + 512:V0 + 768], f"ring{rB}", ["win_s"], [f"ring{rB}"])
        WA = ring[rA][:].rearrange("p (k c) -> p k c", k=8)
        WB = ring[rB][:, 0:2048].rearrange("p (k c) -> p k c", k=8)
        qi = 0
        for T in range(NST if STAGE >= 2 else 0):
            for rq in range(4):
                tk = qi % 2
                qi += 1
                load_tables(2, T * 2048 + rq * 512, tk)
                if T > 0:
                    dma(k2p[:], k2_s[T - 1, rq].rearrange("p (c n) -> p c n", c=2), "k2p", ["k2_s"], ["k2p"])
                    dma(v2p[:], v2_s[T - 1, rq].rearrange("p (b j e) -> p b j e", b=4, j=4), "v2p", ["v2_s"], ["v2p"])
                for b in range(4):
                    dma(xt[b][:], x_g2[T, 4 * rq + b], f"xt{b}", [], [f"xt{b}"])
                for b in range(4):
                    norm_transpose(b, b * 128, f"xt{b}", b % 2)
                if STAGE < 2.2:
                    continue
                for c in range(4):
                    bk = nb()
                    mm(ps[bk][:], [(WA[:, kc, c * 128:(c + 1) * 128], hT[:, kc, :]) for kc in range(8)], [f"ring{rA}", "hT"], [f"ps{bk}"])
                    if STAGE < 2.25:
                        continue
                    if c < 2:
                        rope(bk, tk, q2q[:, c, :], "q2q", 24 + c % 2)
                    else:
                        rope(bk, tk, k2q[:, c - 2, :], "k2q", 24 + c % 2)
                if STAGE < 2.3:
                    continue
                for b in range(4):
                    bk = nb()
                    mm(ps[bk][:, 0:256], [(hT[:, kc, b * 128:(b + 1) * 128], WB[:, kc, :]) for kc in range(8)], [f"ring{rB}", "hT"], [f"ps{bk}"])
                    cp("act" if b % 2 else "dve", v2q[:, b, :, 0:64], ps[bk][:, 0:256].rearrange("p (j e) -> p j e", j=4), [f"ps{bk}"], ["v2q"])
                if STAGE < 2.4:
                    continue
                dma(k2_s[T, rq].rearrange("p (c n) -> p c n", c=2), k2q[:], "k2q", ["k2q"], ["k2_s"])
                dma(v2_s[T, rq].rearrange("p (b j e) -> p b j e", b=4, j=4), v2q[:], "v2q", ["v2q"], ["v2_s"])

                if STAGE < 2.5:
                    continue

                def prev2(b, T=T):
                    return None if T == 0 else (k2p, "k2p", v2p, "v2p", b)

                def evac2(b, bo):
                    cp("act", acc0[:, :].rearrange("p (c j b i) -> p j b c i", c=4, j=4, b=4)[:, :, b, :, :],
                       ps[bo][0:65, :].rearrange("p (j c i) -> p j c i", j=4, c=4), [f"ps{bo}"], ["acc0"])
                attention(q2q, ["q2q"], k2q, "k2q", v2q, "v2q", prev2, evac2)
                dma(acc2_s[T, :, :, rq, :].rearrange("c p n -> p c n"), acc0[:, :].rearrange("p (c n) -> p c n", c=4),
                    "acc0", ["acc0"], ["acc2_s"])

        wq = []
        wstate = {"n": 0, "issued": 0, "pieces": []}

        def wpiece(src_view_fn):
            wstate["pieces"].append(src_view_fn)
            return len(wstate["pieces"]) - 1

        def ring_view(r, kind):
            if kind == "k8":
                return ring[r][:].rearrange("p (k c) -> p k c", k=8)
            if kind == "k4":
                return ring[r][:].rearrange("p (k c) -> p k c", k=4)
            raise ValueError

        def wissue(upto):
            while wstate["issued"] <= min(upto, len(wstate["pieces"]) - 1):
                i = wstate["issued"]
                r = i % NRING
                kind, src, sres = wstate["pieces"][i]
                dma(ring_view(r, kind), src, f"ring{r}", [sres], [f"ring{r}"])
                wstate["issued"] += 1

        def wget(i, keep=None):
            wissue((i if keep is None else keep) + NRING - 1)
            r = i % NRING
            return ring_view(r, wstate["pieces"][i][0]), f"ring{r}"

        for t in range(NT if STAGE >= 3 else 0):
            T, c_in = t // 4, t % 4
            par = t % 2
            base = len(wstate["pieces"])
            for c0 in (Q0, K0, V0, U0, Z0, GA0, GB0, GA0 + 512, GB0 + 512):
                wpiece(("k8", win_s[:, :, c0:c0 + 512], "win_s"))
            for h in range(2):
                wpiece(("k4", wout_s[:, 4 * h:4 * h + 4, :], "wout_s"))
            for p in range(8):
                wpiece(("k8", w1_s[:, :, p * 512:(p + 1) * 512], "w1_s"))
            for hf in range(2):
                for p in range(4):
                    wpiece(("k8", w2_s[hf, :, 8 * p:8 * p + 8, :], "w2_s"))
            PQ, PK, PV, PU, PZ, PGA0, PGB0, PGA1, PGB1, PO0, PO1 = [base + i for i in range(11)]
            PW1 = base + 11
            PW2 = base + 19

            for b in range(4):
                dma(xt[b][:], x[t * 512 + b * 128:t * 512 + (b + 1) * 128, :], f"xt{b}", [], [f"xt{b}"])
            if STAGE >= 3.06:
                dma(acc2t[:], acc2_s[T, c_in].rearrange("p q n -> p (q n)"), "acc2t", ["acc2_s"], ["acc2t"])
            for b in range(4):
                norm_transpose(b, b * 128, f"xt{b}", b % 2, g1blk=(b if STAGE >= 3.07 else None))

            if STAGE < 3.1:
                continue
            for which, PIDX in (("q", PQ), ("k", PK)):
                W, wres = wget(PIDX)
                for g in range(2):
                    load_tables(g, t * 512, g) if which == "q" else None
                    for c2 in range(2):
                        bk = nb()
                        col = g * 256 + c2 * 128
                        hsrc, hres_ = (hT, "hT") if g == 0 else (hT1, "hT1")
                        mm(ps[bk][:], [(W[:, kc, col:col + 128], hsrc[:, kc, :]) for kc in range(8)], [wres, hres_], [f"ps{bk}"])
                        if which == "q":
                            rope(bk, g, a2[:, 20 + 2 * g + c2, :], f"a2_{20 + 2 * g + c2}", 24 + c2)
                        else:
                            rope(bk, g, k01[g][par][:, c2, :], f"k01_{g}{par}", 24 + c2)
            if STAGE < 3.15:
                continue
            W, wres = wget(PV)
            for g in range(2):
                for b in range(4):
                    bk = nb()
                    hsrc, hres_ = (hT, "hT") if g == 0 else (hT1, "hT1")
                    mm(ps[bk][:, 0:256], [(hsrc[:, kc, b * 128:(b + 1) * 128], W[:, kc, g * 256:(g + 1) * 256]) for kc in range(8)],
                       [wres, hres_], [f"ps{bk}"])
                    cp("act" if b % 2 else "dve", v01[g][par][:, b, :, 0:64], ps[bk][:, 0:256].rearrange("p (j e) -> p j e", j=4),
                       [f"ps{bk}"], [f"v01_{g}{par}"])
            W, wres = wget(PU)
            for c in range(4):
                bk = nb()
                mm(ps[bk][:], [(W[:, kc, c * 128:(c + 1) * 128], hT[:, kc, :]) for kc in range(8)], [wres, "hT"], [f"ps{bk}"])
                act(a2[:, c, :], ps[bk][:], AF.Gelu_apprx_tanh, [f"ps{bk}"], [f"a2_{c}"])
            W, wres = wget(PZ)
            for b in range(4):
                bk = nb()
                mm(ps[bk][:], [(hT[:, kc, b * 128:(b + 1) * 128], W[:, kc, :]) for kc in range(8)], [wres, "hT"], [f"ps{bk}"])
                f1 = nfs()
                act(fs[f1][:], ps[bk][:], AF.Gelu_apprx_tanh, [f"ps{bk}"], [f"fs{f1}"])
                s = rms_stats([fs[f1][:]], [f"fs{f1}"], "ln")
                ts("dve", a2[:, 4 + b, :], fs[f1][:], stat[:, s, 12:13], stat[:, s, 15:16], ALU.subtract, ALU.mult,
                   [f"fs{f1}", f"stat{s}"], [f"a2_{4 + b}"])

            if STAGE < 3.3:
                continue
            accn = acc0[:, :].rearrange("p (j n) -> p j n", j=4)
            for g in range(2):
                qres_l = [f"a2_{20 + 2 * g}", f"a2_{21 + 2 * g}"]
                kc_t, kc_r = k01[g][par], f"k01_{g}{par}"
                vc_t, vc_r = v01[g][par], f"v01_{g}{par}"
                kp_t, kp_r = k01[g][1 - par], f"k01_{g}{1 - par}"
                vp_t, vp_r = v01[g][1 - par], f"v01_{g}{1 - par}"
                if g == 0:
                    def prev_of(b, t=t, kc_t=kc_t, kc_r=kc_r, vc_t=vc_t, vc_r=vc_r, kp_t=kp_t, kp_r=kp_r, vp_t=vp_t, vp_r=vp_r):
                        if b > 0:
                            return (kc_t, kc_r, vc_t, vc_r, b - 1)
                        return None if t == 0 else (kp_t, kp_r, vp_t, vp_r, 3)

                    def evac(b, bo):
                        cp("act", accn[:, :, b * 128:(b + 1) * 128], ps[bo][0:65, :].rearrange("p (j n) -> p j n", j=4), [f"ps{bo}"], ["acc0"])
                else:
                    def prev_of(b, t=t, kp_t=kp_t, kp_r=kp_r, vp_t=vp_t, vp_r=vp_r):
                        return None if t == 0 else (kp_t, kp_r, vp_t, vp_r, b)

                    def evac(b, bo):
                        dst = acc0[:, :].rearrange("p (j i r) -> p j i r", j=4, r=4)[:, :, :, b]
                        tt("dve", dst, ps[bo][0:65, :].rearrange("p (j n) -> p j n", j=4), dst, ALU.add, [f"ps{bo}", "acc0"], ["acc0"])

                class QT:
                    def __init__(self, g):
                        self.g = g

                    def __getitem__(self, idx):
                        pr, jp, cols = idx
                        return a2[pr, 20 + 2 * self.g + jp, cols]
                attention(QT(g), qres_l, kc_t, kc_r, vc_t, vc_r, prev_of, evac)
            if STAGE < 3.4:
                continue
            a_nat = acc0[:, :].rearrange("p (j i q b) -> p j i q b", j=4, q=4, b=4)
            a_g2 = acc2t[:, :].rearrange("p (q j b i) -> p j i q b", q=4, j=4, b=4)
            for j in range(4):
                tt("dve", a_nat[:, j], a_nat[:, j], a_g2[:, j], ALU.add, ["acc0", "acc2t"], ["acc0"])
            P.op("dve", lambda e: e.reciprocal(acc2t[64:65, :], acc0[64:65, :]), ["acc0"], ["acc2t"])
            for j in range(4):
                bk = nb()
                mm(ps[bk][0:64, :], [(ones_f[64:65, 0:64], acc2t[64:65, j * 512:(j + 1) * 512])], ["ones_f", "acc2t"], [f"ps{bk}"])
                tt("dve", a2[0:64, 16 + j, :], acc0[0:64, j * 512:(j + 1) * 512], ps[bk][0:64, :], ALU.mult, [f"ps{bk}", "acc0"], [f"a2_{16 + j}"])

            if STAGE < 3.5:
                continue
            for g in range(4):
                bk = nb()

                def fn(e, g=g, bk=bk):
                    ins = None
                    for b in range(4):
                        ins = e.matmul(ps[bk][:, b * 128:(b + 1) * 128], lhsT=a2[:, 4 + b, g * 128:(g + 1) * 128],
                                       rhs=wspT[:, g * 128:(g + 1) * 128], start=True, stop=True)
                    return ins
                P.op("pe", fn, [f"a2_{4 + b}" for b in range(4)] + ["wspT"], [f"ps{bk}"])
                f1 = nfs()
                for b in range(4):
                    stt("dve", fs[f1][:, b * 128:(b + 1) * 128], ps[bk][:, b * 128:(b + 1) * 128], lng[:, g:g + 1],
                        Cgm[:, g * 128:(g + 1) * 128], ALU.mult, ALU.add, [f"ps{bk}", "lng", "Cgm"], [f"fs{f1}"])
                tt("pool", a2[:, 8 + g, :], fs[f1][:], a2[:, g, :], ALU.mult, [f"fs{f1}", f"a2_{g}"], [f"a2_{8 + g}"])

            if STAGE < 3.6:
                continue
            for oc in range(8):
                WGA, rga = wget(PGA0 if oc < 4 else PGA1)
                WGB, rgb = wget(PGB0 if oc < 4 else PGB1, keep=(PGA0 if oc < 4 else PGA1))
                co = (oc % 4) * 128
                bA, bB, bGA, bGB = nb(), nb(), nb(), nb()
                mm(ps[bA][:], [(wba[:, j, oc * 128:(oc + 1) * 128], a2[0:64, 16 + j, :]) for j in range(4)],
                   ["wba"] + [f"a2_{16 + j}" for j in range(4)], [f"ps{bA}"])
                mm(ps[bB][:], [(wbg[:, g, oc * 128:(oc + 1) * 128], a2[:, 8 + g, :]) for g in range(4)],
                   ["wbg"] + [f"a2_{8 + g}" for g in range(4)], [f"ps{bB}"])
                mm(ps[bGA][:], [(WGA[:, kc, co:co + 128], hT[:, kc, :]) for kc in range(8)], [rga, "hT"], [f"ps{bGA}"])
                mm(ps[bGB][:], [(WGB[:, kc, co:co + 128], hT[:, kc, :]) for kc in range(8)], [rgb, "hT"], [f"ps{bGB}"])
                sa, sbb = 24 + (oc % 2), 26 + (oc % 2)
                act(a2[:, sa, :], ps[bGA][:], AF.Sigmoid, [f"ps{bGA}"], [f"a2_{sa}"])
                act(a2[:, sbb, :], ps[bGB][:], AF.Sigmoid, [f"ps{bGB}"], [f"a2_{sbb}"])
                f1, f2 = nfs(), nfs()
                tt("dve", fs[f1][:], ps[bA][:], a2[:, sa, :], ALU.mult, [f"ps{bA}", f"a2_{sa}"], [f"fs{f1}"])
                tt("dve", fs[f2][:], ps[bB][:], a2[:, sbb, :], ALU.mult, [f"ps{bB}", f"a2_{sbb}"], [f"fs{f2}"])
                ms = oc if oc < 8 else oc
                tt("pool", a2[:, ms, :], fs[f1][:], fs[f2][:], ALU.add, [f"fs{f1}", f"fs{f2}"] + [f"a2_{8 + g}" for g in range(4)], [f"a2_{ms}"])

            if STAGE < 3.7:
                continue
            WO0, ro0 = wget(PO0)
            WO1, ro1 = wget(PO1, keep=PO0)
            for b in range(4):
                by = [nb(), nb()]
                for hf in range(2):
                    pairs = []
                    for kc in range(8):
                        Wp = WO0 if kc < 4 else WO1
                        pairs.append((a2[:, kc, b * 128:(b + 1) * 128], Wp[:, kc % 4, hf * 512:(hf + 1) * 512]))
                    mm(ps[by[hf]][:], pairs, [ro0, ro1] + [f"a2_{kc}" for kc in range(8)], [f"ps{by[hf]}"])
                s = rms_stats([ps[by[0]][:], ps[by[1]][:]], [f"ps{by[0]}", f"ps{by[1]}"], "rms")
                for hf in range(2):
                    f1 = nfs()
                    stt("dve", fs[f1][:], ps[by[hf]][:], stat[:, s, 15:16], gpm[:, hf * 512:(hf + 1) * 512], ALU.mult, ALU.mult,
                        [f"ps{by[hf]}", f"stat{s}", "gpm"], [f"fs{f1}"])
                    tt("pool", xt[b][:, hf * 512:(hf + 1) * 512], xt[b][:, hf * 512:(hf + 1) * 512], fs[f1][:], ALU.add,
                       [f"fs{f1}", f"xt{b}"], [f"xt{b}"])
            if STAGE < 3.8:
                continue
            for b in range(4):
                norm_transpose(b, b * 128, f"xt{b}", b % 2)
            for f in range(32):
                W, wres = wget(PW1 + f // 4)
                bk = nb()
                co = (f % 4) * 128
                mm(ps[bk][:], [(W[:, kc, co:co + 128], hT[:, kc, :]) for kc in range(8)], [wres, "hT"], [f"ps{bk}"])
                act(rb[f % 2][:], ps[bk][:], AF.Relu, [f"ps{bk}"], [f"rb{f % 2}"])
                tt("pool", a2[:, f, :], rb[f % 2][:], rb[f % 2][:], ALU.mult, [f"rb{f % 2}"], [f"a2_{f}"])
            if STAGE < 3.9:
                continue
            for hf in range(2):
                for p in range(4):
                    W, wres = wget(PW2 + hf * 4 + p)
                    for b in range(4):
                        bk = hf * 4 + b

                        def fn(e, W=W, p=p, b=b, bk=bk):
                            ins = None
                            for ff in range(8):
                                ins = e.matmul(ps[bk][:], lhsT=a2[:, 8 * p + ff, b * 128:(b + 1) * 128], rhs=W[:, ff, :],
                                               start=(p == 0 and ff == 0), stop=(p == 3 and ff == 7), skip_group_check=True)
                            return ins
                        P.op("pe", fn, [wres] + [f"a2_{8 * p + ff}" for ff in range(8)], [f"ps{bk}"])
            bank_ctr[0] = 0
            for b in range(4):
                s = rms_stats([ps[b][:], ps[4 + b][:]], [f"ps{b}", f"ps{4 + b}"], "rms")
                for hf in range(2):
                    f1 = nfs()
                    bk = hf * 4 + b
                    stt("dve", fs[f1][:], ps[bk][:], stat[:, s, 15:16], gpl[:, hf * 512:(hf + 1) * 512], ALU.mult, ALU.mult,
                        [f"ps{bk}", f"stat{s}", "gpl"], [f"fs{f1}"])
                    tt("pool", xt[b][:, hf * 512:(hf + 1) * 512], xt[b][:, hf * 512:(hf + 1) * 512], fs[f1][:], ALU.add,
                       [f"fs{f1}", f"xt{b}"], [f"xt{b}"])
                dma(out[t * 512 + b * 128:t * 512 + (b + 1) * 128, :], xt[b][:], f"xt{b}", [f"xt{b}"], ["out"])

        sems = {name: es.enter_context(nc.semaphore(name)) for name in sorted(P.semnames)}
        final = [(s, v) for s, v in P.dma_cnt.items()]
        with nc.Block() as block:
            @block.tensor
            def _(e):
                P.replay("pe", e, sems)

            @block.scalar
            def _(e):
                P.replay("act", e, sems)

            @block.vector
            def _(e):
                P.replay("dve", e, sems)

            @block.gpsimd
            def _(e):
                P.replay("pool", e, sems)

            @block.sync
            def _(e):
                P.replay("sp", e, sems, final_waits=final)
    return nc


def _tables(S):
    half = 32
    inv = (10000.0 ** (-np.arange(half, dtype=np.float32) / half)).astype(np.float32)
    idx = np.arange(S)
    pos = [idx.copy()]
    n, r, i = idx // 512, (idx % 512) // 128, idx % 128
    pos.append(512 * n + 4 * i + r)
    T, r, i = idx // 2048, (idx % 2048) // 128, idx % 128
    pos.append(2048 * T + 16 * i + r)
    cos = np.zeros((3, 128, S), np.float32)
    sin = np.zeros((3, 128, S), np.float32)
    m = np.arange(128)
    fr = inv[m % 32]
    sgn = np.where((m % 64) < 32, -1.0, 1.0).astype(np.float32)
    for g in range(3):
        ang = (pos[g].astype(np.float32)[None, :] * fr[:, None]).astype(np.float32)
        cos[g] = np.cos(ang)
        sin[g] = np.sin(ang) * sgn[:, None]
    return cos, sin


def _consts():
    bf = ml_dtypes.bfloat16
    ident = np.eye(128, dtype=np.float32).astype(bf)
    m = np.arange(128)
    sw = np.where((m % 64) < 32, m + 32, m - 32)
    rsw = np.zeros((128, 128), np.float32)
    rsw[sw, m] = 1.0
    k = np.arange(128)[:, None]
    q = np.arange(128)[None, :]
    half = np.concatenate([(k <= q), (k >= q)], axis=1).astype(np.float32)
    mask2 = np.concatenate([half, half], axis=1).astype(bf)
    tril = (k <= q).astype(np.float32)
    trilT = np.tile(tril, (1, 4)).astype(np.float32)
    n = np.arange(128)
    perm4 = np.zeros((128, 128), np.float32)
    perm4[n, 32 * (n % 4) + n // 4] = 1.0
    return ident, rsw.astype(bf), mask2, trilT, perm4.astype(bf)


_NC_CACHE = {}


def _host_inputs(S, x_b, p):
    cos, sin = _tables(S)
    ident, rsw, mask2, trilT, perm4 = _consts()
    f = np.float32

    def col8(v):
        return np.ascontiguousarray(np.asarray(v, f).reshape(8, 128).T)

    def col4(v):
        return np.ascontiguousarray(np.asarray(v, f).reshape(4, 128).T)
    wsp = np.asarray(p["w_spatial"], f)[0]
    wspT = np.ascontiguousarray(wsp.transpose(2, 0, 1).reshape(128, 512))
    bsp = np.asarray(p["b_spatial"], f)[0].reshape(1, 512)
    common = {
        "w_in": np.ascontiguousarray(np.asarray(p["w_in"], f)[0]),
        "w_ba": np.ascontiguousarray(np.asarray(p["w_branch_attn"], f)[0]),
        "w_bg": np.ascontiguousarray(np.asarray(p["w_branch_gmlp"], f)[0]),
        "w_out": np.ascontiguousarray(np.asarray(p["w_out"], f)[0]),
        "w1": np.ascontiguousarray(np.asarray(p["w_mlp_in"], f)[0]),
        "w2": np.ascontiguousarray(np.asarray(p["w_mlp_out"], f)[0]),
        "gpre": col8(np.asarray(p["norm_pre_mix"])[0]),
        "gpre2": col8(np.asarray(p["norm_pre_mlp"])[0]),
        "gpm_b": np.ascontiguousarray(np.broadcast_to(np.asarray(p["norm_post_mix"], f)[0][None, :], (128, D))),
        "gpl_b": np.ascontiguousarray(np.broadcast_to(np.asarray(p["norm_post_mlp"], f)[0][None, :], (128, D))),
        "wspT": wspT,
        "bsp_b": np.ascontiguousarray(np.broadcast_to(bsp, (128, 512))),
        "lng": col4(np.asarray(p["ln_v_gain"])[0]),
        "lnb": col4(np.asarray(p["ln_v_bias"])[0]),
        "cos_t": cos, "sin_t": sin, "ident": ident, "rsw": rsw, "perm4": perm4, "mask2": mask2, "trilT": trilT,
    }
    return [dict(common, x=np.ascontiguousarray(np.asarray(xb, f))) for xb in x_b]


def kernel(**inputs):
    x = np.asarray(inputs["x"], np.float32)
    B, S, _ = x.shape
    if S not in _NC_CACHE:
        _NC_CACHE[S] = build_nc(S)
    nc = _NC_CACHE[S]
    in_maps = _host_inputs(S, [x[b] for b in range(B)], inputs)
    res = run_bass_kernel_spmd(nc, in_maps, core_ids=list(range(B)))
    return np.stack([np.asarray(r["out"], np.float32) for r in res.results], axis=0)
```

```python
import contextlib
import numpy as np
import ml_dtypes
import concourse.bass as bass
import concourse.mybir as mybir
from concourse.bass_utils import run_bass_kernel_spmd

F32 = mybir.dt.float32
BF16 = mybir.dt.bfloat16
AF = mybir.ActivationFunctionType
ALU = mybir.AluOpType

D = 1024
INW = 5376
Q0, K0, V0, U0, Z0, GA0, GB0 = 0, 768, 1536, 2304, 2816, 3328, 4352
EPS = 1e-6
SELF_SYNC = True
import os
STAGE = float(os.environ.get('KSTAGE', '99'))
NRING = 5


class Prog:
    ENGS = ("pe", "act", "dve", "pool", "sp")

    def __init__(self):
        self.ops = {e: [] for e in self.ENGS}
        self.cnt = {e: 0 for e in self.ENGS}
        self.last_w = {}
        self.readers = {}
        self.waited = {e: {} for e in self.ENGS}
        self.dma_cnt = {}
        self.semnames = set()
        self.pending = {}

    def barrier_sp(self):
        self.pending = dict(self.dma_cnt)

    def op(self, eng, fn, reads=(), writes=(), chan=None):
        deps = []
        for r in reads:
            if r in self.last_w:
                deps.append((self.last_w[r], "raw"))
            if r.startswith("ps"):
                for t in self.readers.get(r, ()):
                    if t[2] != eng:
                        deps.append((t, "rar"))
        for w in writes:
            if w in self.last_w:
                deps.append((self.last_w[w], "waw"))
            for t in self.readers.get(w, ()):
                deps.append((t, "war"))
        waits = {}
        for (s, v, e), kind in deps:
            if e == eng:
                if eng == "pe" or eng == "sp":
                    if eng == "pe":
                        continue
                elif not SELF_SYNC or kind == "war":
                    continue
            if self.waited[eng].get(s, 0) >= v:
                continue
            waits[s] = max(waits.get(s, 0), v)
        if eng == "sp" and self.pending:
            for s, v in self.pending.items():
                if self.waited[eng].get(s, 0) < v:
                    waits[s] = max(waits.get(s, 0), v)
            self.pending = {}
        for s, v in waits.items():
            self.waited[eng][s] = v
        if eng == "sp":
            assert chan is not None
            s = "d_" + chan
            self.dma_cnt[s] = self.dma_cnt.get(s, 0) + 16
            tok = (s, self.dma_cnt[s], eng)
            inc = 16
        else:
            s = "c_" + eng
            self.cnt[eng] += 1
            tok = (s, self.cnt[eng], eng)
            inc = 1
        self.semnames.add(s)
        self.ops[eng].append((fn, sorted(waits.items()), s, inc))
        for w in writes:
            self.last_w[w] = tok
            self.readers[w] = []
        for r in reads:
            self.readers.setdefault(r, []).append(tok)
        return tok

    def replay(self, eng_name, eng, sems, final_waits=()):
        for fn, waits, s, inc in self.ops[eng_name]:
            for ws, wv in waits:
                eng.wait_ge(sems[ws], wv)
            ins = fn(eng)
            ins.then_inc(sems[s], inc)
        for ws, wv in final_waits:
            eng.wait_ge(sems[ws], wv)


def build_nc(S):
    NT = S // 512
    NST = S // 2048
    nc = bass.Bass("TRN2", target_bir_lowering=False)
    P = Prog()

    def din(name, shape, dt=F32):
        return nc.dram_tensor(name, list(shape), dt, kind="ExternalInput")

    x = din("x", [S, D])
    w_in = din("w_in", [D, INW])
    w_ba = din("w_ba", [256, D])
    w_bg = din("w_bg", [512, D])
    w_out = din("w_out", [D, D])
    w1 = din("w1", [D, 4096])
    w2 = din("w2", [4096, D])
    gpre_d = din("gpre", [128, 8])
    gpre2_d = din("gpre2", [128, 8])
    gpm_d = din("gpm_b", [128, D])
    gpl_d = din("gpl_b", [128, D])
    wspT_d = din("wspT", [128, 512])
    bsp_d = din("bsp_b", [128, 512])
    lng_d = din("lng", [128, 4])
    lnb_d = din("lnb", [128, 4])
    cos_d = din("cos_t", [3, 128, S])
    sin_d = din("sin_t", [3, 128, S])
    ident_d = din("ident", [128, 128], BF16)
    rsw_d = din("rsw", [128, 128], BF16)
    perm4_d = din("perm4", [128, 128], BF16)
    mask2_d = din("mask2", [128, 512], BF16)
    tril_d = din("trilT", [128, 512])
    out = nc.dram_tensor("out", [S, D], F32, kind="ExternalOutput")

    def dscr(name, shape, dt):
        return nc.dram_tensor(name, list(shape), dt, kind="Internal")

    win_s = dscr("win_s", [128, 8, INW], BF16)
    w1_s = dscr("w1_s", [128, 8, 4096], BF16)
    w2_s = dscr("w2_s", [2, 128, 32, 512], BF16)
    wout_s = dscr("wout_s", [128, 8, D], BF16)
    wbg_s = dscr("wbg_s", [128, 4, D], BF16)
    wba_s = dscr("wba_s", [64, 4, D], BF16)
    k2_s = dscr("k2_s", [NST, 4, 128, 1024], BF16)
    v2_s = dscr("v2_s", [NST, 4, 128, 1280], BF16)
    acc2_s = dscr("acc2_s", [NST, 4, 65, 4, 512], F32)

    es = contextlib.ExitStack()
    with es:
        def sb(name, shape, dt):
            return es.enter_context(nc.sbuf_tensor("s_" + name, list(shape), dt))

        ring = [sb(f"ring{k}", [128, 4096], BF16) for k in range(NRING)]
        wba = sb("wba", [64, 4, D], BF16)
        wbg = sb("wbg", [128, 4, D], BF16)
        hT = sb("hT", [128, 8, 512], BF16)
        hT1 = sb("hT1", [128, 8, 512], BF16)
        xt = [sb(f"xt{k}", [128, D], F32) for k in range(4)]
        hb = [sb(f"hb{k}", [128, D], BF16) for k in range(2)]
        a2 = sb("a2", [128, 32, 512], BF16)
        rb = [sb(f"rb{k}", [128, 512], BF16) for k in range(2)]
        k01 = [[sb(f"k01_{g}{p}", [128, 2, 512], BF16) for p in range(2)] for g in range(2)]
        v01 = [[sb(f"v01_{g}{p}", [128, 4, 4, 80], BF16) for p in range(2)] for g in range(2)]
        q2q = sb("q2q", [128, 2, 512], BF16)
        k2q = sb("k2q", [128, 2, 512], BF16)
        k2p = sb("k2p", [128, 2, 512], BF16)
        v2q = sb("v2q", [128, 4, 4, 80], BF16)
        v2p = sb("v2p", [128, 4, 4, 80], BF16)
        acc0 = sb("acc0", [65, 2048], F32)
        acc2t = sb("acc2t", [65, 2048], F32)
        fs = [sb(f"fs{k}", [128, 512], F32) for k in range(6)]
        cosT = [sb(f"cosT{k}", [128, 512], F32) for k in range(2)]
        sinT = [sb(f"sinT{k}", [128, 512], F32) for k in range(2)]
        ident = sb("ident", [128, 128], BF16)
        rsw = sb("rsw", [128, 128], BF16)
        perm4 = sb("perm4", [128, 128], BF16)
        mask2 = sb("mask2", [128, 512], BF16)
        wspT = sb("wspT", [128, 512], BF16)
        ones_bf = sb("ones_bf", [128, 128], BF16)
        ones_f = sb("ones_f", [128, 64], F32)
        Cgm = sb("Cgm", [128, 512], F32)
        lng = sb("lng", [128, 4], F32)
        lnb = sb("lnb", [128, 4], F32)
        gpre = sb("gpre", [128, 8], F32)
        gpre2 = sb("gpre2", [128, 8], F32)
        gpm = sb("gpm", [128, D], F32)
        gpl = sb("gpl", [128, D], F32)
        NSTAT = 16
        stat = sb("stat", [128, NSTAT, 16], F32)
        ps = [es.enter_context(nc.psum_tensor(f"ps{k}", [128, 512], F32)) for k in range(8)]

        bank_ctr = [0]

        def nb():
            b = bank_ctr[0] % 8
            bank_ctr[0] += 1
            return b

        stat_ctr = [0]

        def nstat():
            s = stat_ctr[0] % NSTAT
            stat_ctr[0] += 1
            return s

        fs_ctr = [0]

        def nfs():
            s = fs_ctr[0] % 6
            fs_ctr[0] += 1
            return s

        def dma(out_ap, in_ap, chan, reads, writes):
            P.op("sp", lambda e, o=out_ap, i=in_ap: e.dma_start(out=o, in_=i), reads, writes, chan=chan)

        def mm(out_ap, pairs, reads, writes):
            def fn(e, o=out_ap, pairs=pairs):
                n = len(pairs)
                ins = None
                for i, (l, r) in enumerate(pairs):
                    ins = e.matmul(o, lhsT=l, rhs=r, start=(i == 0), stop=(i == n - 1))
                return ins
            P.op("pe", fn, reads, writes)

        def act(out_ap, in_ap, func, reads, writes, scale=1.0, bias=0.0):
            P.op("act", lambda e: e.activation(out=out_ap, in_=in_ap, func=func, bias=bias, scale=scale), reads, writes)

        def tt(eng, out_ap, in0, in1, op, reads, writes):
            P.op(eng, lambda e: e.tensor_tensor(out=out_ap, in0=in0, in1=in1, op=op), reads, writes)

        def ts(eng, out_ap, in0, s1, s2, op0, op1, reads, writes):
            if s2 is None:
                P.op(eng, lambda e: e.tensor_scalar(out=out_ap, in0=in0, scalar1=s1, scalar2=None, op0=op0), reads, writes)
            else:
                P.op(eng, lambda e: e.tensor_scalar(out=out_ap, in0=in0, scalar1=s1, scalar2=s2, op0=op0, op1=op1), reads, writes)

        def stt(eng, out_ap, in0, scalar, in1, op0, op1, reads, writes):
            P.op(eng, lambda e: e.scalar_tensor_tensor(out=out_ap, in0=in0, scalar=scalar, in1=in1, op0=op0, op1=op1), reads, writes)

        def cp(eng, out_ap, in_ap, reads, writes):
            if eng == "act":
                P.op(eng, lambda e: e.activation(out=out_ap, in_=in_ap, func=AF.Copy), reads, writes)
            else:
                P.op(eng, lambda e: e.tensor_copy(out=out_ap, in_=in_ap), reads, writes)

        for i, (t, d, nm) in enumerate([(ident, ident_d, "ident"), (rsw, rsw_d, "rsw"), (perm4, perm4_d, "perm4"), (mask2, mask2_d, "mask2"),
                                        (lng, lng_d, "lng"), (lnb, lnb_d, "lnb"), (gpre, gpre_d, "gpre"),
                                        (gpre2, gpre2_d, "gpre2"), (gpm, gpm_d, "gpm"), (gpl, gpl_d, "gpl")]):
            dma(t[:], d.ap(), "c" + str(i), [], [nm])
        P.op("dve", lambda e: e.memset(ones_bf[:], 1.0), [], ["ones_bf"])
        P.op("dve", lambda e: e.memset(ones_f[:], 1.0), [], ["ones_f"])
        for g in range(2):
            for p in range(2):
                P.op("pool", lambda e, g=g, p=p: e.memset(v01[g][p][:, :, :, 64:80], 1.0), [], [f"v01_{g}{p}"])
        P.op("pool", lambda e: e.memset(v2q[:, :, :, 64:80], 1.0), [], ["v2q"])
        dma(fs[0][:], wspT_d.ap(), "fs0", [], ["fs0"])
        dma(fs[1][:], tril_d.ap(), "fs1", [], ["fs1"])
        dma(fs[2][:], bsp_d.ap(), "fs2", [], ["fs2"])
        tt("dve", wspT[:], fs[0][:], fs[1][:], ALU.mult, ["fs0", "fs1"], ["wspT"])
        mm(ps[0][:], [(ones_bf[:], wspT[:])], ["ones_bf", "wspT"], ["ps0"])
        for g in range(4):
            stt("dve", Cgm[:, g * 128:(g + 1) * 128], ps[0][:, g * 128:(g + 1) * 128], lnb[:, g:g + 1],
                fs[2][:, g * 128:(g + 1) * 128], ALU.mult, ALU.add, ["ps0", "lnb", "fs2"], ["Cgm"])

        stg = [0]

        def prep(src_ap, dst_ap, npart, ncol, scal, dst_res):
            k = stg[0] % 4
            r = stg[0] % NRING
            stg[0] += 1
            dma(xt[k][0:npart, 0:ncol], src_ap, f"xt{k}", [], [f"xt{k}"])
            if scal is None:
                cp("dve" if stg[0] % 2 else "act", ring[r][0:npart, 0:ncol], xt[k][0:npart, 0:ncol], [f"xt{k}"], [f"ring{r}"])
            elif stg[0] % 2:
                ts("dve", ring[r][0:npart, 0:ncol], xt[k][0:npart, 0:ncol], scal, None, ALU.mult, None,
                   [f"xt{k}", "gpre", "gpre2"], [f"ring{r}"])
            else:
                P.op("act", lambda e, o=ring[r][0:npart, 0:ncol], i=xt[k][0:npart, 0:ncol], sc=scal:
                     e.activation(out=o, in_=i, func=AF.Copy, scale=sc), [f"xt{k}", "gpre", "gpre2"], [f"ring{r}"])
            dma(dst_ap, ring[r][0:npart, 0:ncol], f"ring{r}", [f"ring{r}"], [dst_res])

        for kc in range(8):
            for c0 in range(0, INW, 1024):
                c1 = min(c0 + 1024, INW)
                prep(w_in[kc * 128:(kc + 1) * 128, c0:c1], win_s[:, kc, c0:c1], 128, c1 - c0, gpre[:, kc:kc + 1], "win_s")
        for kc in range(8):
            for c0 in range(0, 4096, 1024):
                prep(w1[kc * 128:(kc + 1) * 128, c0:c0 + 1024], w1_s[:, kc, c0:c0 + 1024], 128, 1024, gpre2[:, kc:kc + 1], "w1_s")
        for f in range(32):
            k = stg[0] % 4
            r = stg[0] % NRING
            stg[0] += 1
            dma(xt[k][:], w2[f * 128:(f + 1) * 128, :], f"xt{k}", [], [f"xt{k}"])
            cp("dve" if f % 2 else "act", ring[r][:, 0:1024], xt[k][:], [f"xt{k}"], [f"ring{r}"])
            for hf in range(2):
                dma(w2_s[hf, :, f, :], ring[r][:, hf * 512:(hf + 1) * 512], f"ring{r}", [f"ring{r}"], ["w2_s"])
        for kc in range(8):
            prep(w_out[kc * 128:(kc + 1) * 128, :], wout_s[:, kc, :], 128, 1024, None, "wout_s")
        for g in range(4):
            prep(w_bg[g * 128:(g + 1) * 128, :], wbg_s[:, g, :], 128, 1024, None, "wbg_s")
        for j in range(4):
            prep(w_ba[j * 64:(j + 1) * 64, :], wba_s[:, j, :], 64, 1024, None, "wba_s")
        P.barrier_sp()
        dma(wba[:], wba_s.ap(), "wba", ["wba_s"], ["wba"])
        dma(wbg[:], wbg_s.ap(), "wbg", ["wbg_s"], ["wbg"])

        def rms_stats(src_aps, src_res, what):
            s = nstat()
            sr = f"stat{s}"
            n = len(src_aps)
            st3 = stat[:, s, 0:6 * n].rearrange("p (a t) -> p a t", t=3)
            for i, a in enumerate(src_aps):
                P.op("dve", lambda e, i=i, a=a: e.bn_stats(st3[:, 2 * i:2 * i + 2, :], a), src_res, [sr])
            mv = stat[:, s, 12:14]
            P.op("dve", lambda e: e.bn_aggr(mv, st3), [sr], [sr])
            if what == "rms":
                stt("dve", stat[:, s, 14:15], stat[:, s, 12:13], stat[:, s, 12:13], stat[:, s, 13:14], ALU.mult, ALU.add, [sr], [sr])
                src = stat[:, s, 14:15]
            else:
                src = stat[:, s, 13:14]
            act(stat[:, s, 15:16], src, AF.Sqrt, [sr], [sr], scale=1.0, bias=EPS)
            P.op("dve", lambda e: e.reciprocal(stat[:, s, 15:16], stat[:, s, 15:16]), [sr], [sr])
            return s

        def norm_transpose(k, col, xres, hres_idx, g1blk=None):
            s = rms_stats([xt[k][:, 0:512], xt[k][:, 512:1024]], [xres], "rms")
            h = hb[hres_idx]
            hres = f"hb{hres_idx}"
            P.op("act", lambda e: e.activation(out=h[:], in_=xt[k][:], func=AF.Copy, scale=stat[:, s, 15:16]),
                 [xres, f"stat{s}"], [hres])
            b = nb()
            pst = ps[b][:].bitcast(BF16).rearrange("p (c t) -> p c t", t=128)

            def fn(e):
                ins = None
                for kc in range(8):
                    ins = e.transpose(pst[:, kc, :], h[:, kc * 128:(kc + 1) * 128], ident[:])
                return ins
            P.op("pe", fn, [hres, "ident"], [f"ps{b}"])
            cp("dve", hT[:, :, col:col + 128], pst, [f"ps{b}"], ["hT"])
            if g1blk is not None:
                for half in range(2):
                    bp = nb()

                    def fnp(e, half=half, bp=bp):
                        ins = None
                        for kk in range(4):
                            kc = 4 * half + kk
                            ins = e.matmul(ps[bp][:, kk * 128:(kk + 1) * 128], lhsT=h[:, kc * 128:(kc + 1) * 128], rhs=perm4[:],
                                           start=True, stop=True)
                        return ins
                    P.op("pe", fnp, [hres, "perm4"], [f"ps{bp}"])
                    dst = hT1[:, 4 * half:4 * half + 4, :].rearrange("p k (r i) -> p k r i", r=4)[:, :, :, 32 * g1blk:32 * g1blk + 32]
                    src = ps[bp][:].rearrange("p (k r i) -> p k r i", k=4, r=4)
                    cp("act" if half else "dve", dst, src, [f"ps{bp}"], ["hT1"])

        def rms_stats_batch(jobs, what):
            slots = [nstat() for _ in jobs]
            for (aps, res), s_ in zip(jobs, slots):
                st3 = stat[:, s_, 0:6 * len(aps)].rearrange("p (a t) -> p a t", t=3)
                for i, a in enumerate(aps):
                    P.op("dve", lambda e, i=i, a=a, st3=st3: e.bn_stats(st3[:, 2 * i:2 * i + 2, :], a), res, [f"stat{s_}"])
            for (aps, res), s_ in zip(jobs, slots):
                st3 = stat[:, s_, 0:6 * len(aps)].rearrange("p (a t) -> p a t", t=3)
                P.op("dve", lambda e, s_=s_, st3=st3: e.bn_aggr(stat[:, s_, 12:14], st3), [f"stat{s_}"], [f"stat{s_}"])
            if what == "rms":
                for s_ in slots:
                    stt("dve", stat[:, s_, 14:15], stat[:, s_, 12:13], stat[:, s_, 12:13], stat[:, s_, 13:14], ALU.mult, ALU.add,
                        [f"stat{s_}"], [f"stat{s_}"])
            col = 14 if what == "rms" else 13
            for s_ in slots:
                act(stat[:, s_, 15:16], stat[:, s_, col:col + 1], AF.Sqrt, [f"stat{s_}"], [f"stat{s_}"], scale=1.0, bias=EPS)
            for s_ in slots:
                P.op("dve", lambda e, s_=s_: e.reciprocal(stat[:, s_, 15:16], stat[:, s_, 15:16]), [f"stat{s_}"], [f"stat{s_}"])
            return slots

        def norm_transpose4(jobs):
            slots = [nstat() for _ in jobs]
            st3s = [stat[:, s_, 0:12].rearrange("p (a t) -> p a t", t=3) for s_ in slots]
            for (k, col, xres, g1), s_, st3 in zip(jobs, slots, st3s):
                for i in range(2):
                    P.op("dve", lambda e, i=i, st3=st3, k=k: e.bn_stats(st3[:, 2 * i:2 * i + 2, :], xt[k][:, i * 512:(i + 1) * 512]),
                         [xres], [f"stat{s_}"])
            for s_, st3 in zip(slots, st3s):
                P.op("dve", lambda e, s_=s_, st3=st3: e.bn_aggr(stat[:, s_, 12:14], st3), [f"stat{s_}"], [f"stat{s_}"])
            for s_ in slots:
                stt("dve", stat[:, s_, 14:15], stat[:, s_, 12:13], stat[:, s_, 12:13], stat[:, s_, 13:14], ALU.mult, ALU.add,
                    [f"stat{s_}"], [f"stat{s_}"])
            for s_ in slots:
                act(stat[:, s_, 15:16], stat[:, s_, 14:15], AF.Sqrt, [f"stat{s_}"], [f"stat{s_}"], scale=1.0, bias=EPS)
            for s_ in slots:
                P.op("dve", lambda e, s_=s_: e.reciprocal(stat[:, s_, 15:16], stat[:, s_, 15:16]), [f"stat{s_}"], [f"stat{s_}"])
            for idx, ((k, col, xres, g1blk), s_) in enumerate(zip(jobs, slots)):
                hi = idx % 2
                h = hb[hi]
                hres = f"hb{hi}"
                P.op("act", lambda e, h=h, k=k, s_=s_: e.activation(out=h[:], in_=xt[k][:], func=AF.Copy, scale=stat[:, s_, 15:16]),
                     [xres, f"stat{s_}"], [hres])
                b = nb()
                pst = ps[b][:].bitcast(BF16).rearrange("p (c t) -> p c t", t=128)

                def fn(e, h=h, pst=pst):
                    ins = None
                    for kc in range(8):
                        ins = e.transpose(pst[:, kc, :], h[:, kc * 128:(kc + 1) * 128], ident[:])
                    return ins
                P.op("pe", fn, [hres, "ident"], [f"ps{b}"])
                cp("dve", hT[:, :, col:col + 128], pst, [f"ps{b}"], ["hT"])
                if g1blk is not None:
                    for half in range(2):
                        bp = nb()

                        def fnp(e, half=half, bp=bp, h=h):
                            ins = None
                            for kk in range(4):
                                kc = 4 * half + kk
                                ins = e.matmul(ps[bp][:, kk * 128:(kk + 1) * 128], lhsT=h[:, kc * 128:(kc + 1) * 128], rhs=perm4[:],
                                               start=True, stop=True)
                            return ins
                        P.op("pe", fnp, [hres, "perm4"], [f"ps{bp}"])
                        dst = hT1[:, 4 * half:4 * half + 4, :].rearrange("p k (r i) -> p k r i", r=4)[:, :, :, 32 * g1blk:32 * g1blk + 32]
                        src = ps[bp][:].rearrange("p (k r i) -> p k r i", k=4, r=4)
                        cp("act" if half else "dve", dst, src, [f"ps{bp}"], ["hT1"])

        def rope(b, tabk, dst_ap, dst_res, scratch_slot):
            raw = a2[:, scratch_slot, :]
            rres = f"a2_{scratch_slot}"
            act(raw, ps[b][:], AF.Copy, [f"ps{b}"], [rres])
            b2 = nb()
            mm(ps[b2][:], [(rsw[:], raw)], ["rsw", rres], [f"ps{b2}"])
            f1 = nfs()
            f2 = nfs()
            tt("pool", fs[f1][:], raw, cosT[tabk][:], ALU.mult, [rres, f"cosT{tabk}"], [f"fs{f1}"])
            tt("dve", fs[f2][:], ps[b2][:], sinT[tabk][:], ALU.mult, [f"ps{b2}", f"sinT{tabk}"], [f"fs{f2}"])
            tt("pool", dst_ap, fs[f1][:], fs[f2][:], ALU.add, [f"fs{f1}", f"fs{f2}"], [dst_res])

        ep_ctr = [0]

        def attention(qT, qres, kcur, kcres, vcur, vcres, prev_of, evac):
            items = [(b, j) for b in range(4) for j in range(4)]
            bo_of = {}
            st = {}

            def emit_S(i):
                b, j = items[i]
                if b not in bo_of:
                    bo_of[b] = nb()
                pv = prev_of(b)
                jp, hh = j // 2, j % 2
                lo = 64 * hh
                bs_ = nb()
                sl = 28 + (ep_ctr[0] % 2)
                sp_ = 30 + (ep_ctr[0] % 2)
                ep_ctr[0] += 1
                ncol = 256 if pv is not None else 128
                E = a2[:, sl, 0:ncol]
                Pm = a2[:, sp_, 0:ncol]
                reads = list(qres) + [kcres]
                if pv is not None:
                    reads.append(pv[1])

                def fn(e, b=b, jp=jp, lo=lo, bs_=bs_, pv=pv):
                    Q = qT[lo:lo + 64, jp, b * 128:(b + 1) * 128]
                    ins = e.matmul(ps[bs_][:, 0:128], lhsT=kcur[lo:lo + 64, jp, b * 128:(b + 1) * 128], rhs=Q, start=True, stop=True)
                    if pv is not None:
                        kb = pv[4]
                        ins = e.matmul(ps[bs_][:, 128:256], lhsT=pv[0][lo:lo + 64, jp, kb * 128:(kb + 1) * 128], rhs=Q,
                                       start=True, stop=True)
                    return ins
                P.op("pe", fn, reads, [f"ps{bs_}"])
                act(E, ps[bs_][:, 0:ncol], AF.Exp, [f"ps{bs_}"], [f"a2_{sl}"], scale=0.125)
                tt("dve", Pm, E, mask2[:, 0:ncol], ALU.mult, [f"a2_{sl}", "mask2"], [f"a2_{sp_}"])
                st[i] = (pv, sp_)

            def emit_PV(i):
                b, j = items[i]
                pv, sp_ = st.pop(i)
                bo = bo_of[b]
                reads = [f"a2_{sp_}", vcres]
                if pv is not None:
                    reads.append(pv[3])

                def fn2(e, b=b, j=j, bo=bo, pv=pv, sp_=sp_):
                    o = ps[bo][0:65, j * 128:(j + 1) * 128]
                    ins = e.matmul(o, lhsT=vcur[:, b, j, 0:65], rhs=a2[:, sp_, 0:128], start=True, stop=(pv is None))
                    if pv is not None:
                        ins = e.matmul(o, lhsT=pv[2][:, pv[4], j, 0:65], rhs=a2[:, sp_, 128:256], start=False, stop=True)
                    return ins
                P.op("pe", fn2, reads, [f"ps{bo}"])
                if j == 3:
                    evac(b, bo)

            emit_S(0)
            for i in range(len(items)):
                if i + 1 < len(items):
                    emit_S(i + 1)
                emit_PV(i)

        def load_tables(g, off, k):
            dma(cosT[k][:], cos_d[g, :, off:off + 512], f"cosT{k}", [], [f"cosT{k}"])
            dma(sinT[k][:], sin_d[g, :, off:off + 512], f"sinT{k}", [], [f"sinT{k}"])

        x_g2 = x.ap().rearrange("(t i r) d -> t r i d", i=128, r=16)
        rA, rB = 0, 1
        dma(ring[rA][:].rearrange("p (k c) -> p k c", k=8)[:, :, 0:256], win_s[:, :, Q0 + 512:Q0 + 768], f"ring{rA}", ["win_s"], [f"ring{rA}"])
        dma(ring[rA][:].rearrange("p (k c) -> p k c", k=8)[:, :, 256:512], win_s[:, :, K0 + 512:K0 + 768], f"ring{rA}", ["win_s"], [f"ring{rA}"])
        dma(ring[rB][:, 0:2048].rearrange("p (k c) -> p k c", k=8), win_s[:, :, V0 + 512:V0 + 768], f"ring{rB}", ["win_s"], [f"ring{rB}"])
        WA = ring[rA][:].rearrange("p (k c) -> p k c", k=8)
        WB = ring[rB][:, 0:2048].rearrange("p (k c) -> p k c", k=8)
        qi = 0
        for T in range(NST if STAGE >= 2 else 0):
            for rq in range(4):
                tk = qi % 2
                qi += 1
                load_tables(2, T * 2048 + rq * 512, tk)
                if T > 0:
                    dma(k2p[:], k2_s[T - 1, rq].rearrange("p (c n) -> p c n", c=2), "k2p", ["k2_s"], ["k2p"])
                    dma(v2p[:], v2_s[T - 1, rq].rearrange("p (b j e) -> p b j e", b=4, j=4), "v2p", ["v2_s"], ["v2p"])
                for b in range(4):
                    dma(xt[b][:], x_g2[T, 4 * rq + b], f"xt{b}", [], [f"xt{b}"])
                norm_transpose4([(b, b * 128, f"xt{b}", None) for b in range(4)])
                if STAGE < 2.2:
                    continue
                for c in range(4):
                    bk = nb()
                    mm(ps[bk][:], [(WA[:, kc, c * 128:(c + 1) * 128], hT[:, kc, :]) for kc in range(8)], [f"ring{rA}", "hT"], [f"ps{bk}"])
                    if STAGE < 2.25:
                        continue
                    if c < 2:
                        rope(bk, tk, q2q[:, c, :], "q2q", 24 + c % 2)
                    else:
                        rope(bk, tk, k2q[:, c - 2, :], "k2q", 24 + c % 2)
                if STAGE < 2.3:
                    continue
                for b in range(4):
                    bk = nb()
                    mm(ps[bk][:, 0:256], [(hT[:, kc, b * 128:(b + 1) * 128], WB[:, kc, :]) for kc in range(8)], [f"ring{rB}", "hT"], [f"ps{bk}"])
                    cp("act" if b % 2 else "dve", v2q[:, b, :, 0:64], ps[bk][:, 0:256].rearrange("p (j e) -> p j e", j=4), [f"ps{bk}"], ["v2q"])
                if STAGE < 2.4:
                    continue
                dma(k2_s[T, rq].rearrange("p (c n) -> p c n", c=2), k2q[:], "k2q", ["k2q"], ["k2_s"])
                dma(v2_s[T, rq].rearrange("p (b j e) -> p b j e", b=4, j=4), v2q[:], "v2q", ["v2q"], ["v2_s"])

                if STAGE < 2.5:
                    continue

                def prev2(b, T=T):
                    return None if T == 0 else (k2p, "k2p", v2p, "v2p", b)

                def evac2(b, bo):
                    cp("act", acc0[:, :].rearrange("p (c j b i) -> p j b c i", c=4, j=4, b=4)[:, :, b, :, :],
                       ps[bo][0:65, :].rearrange("p (j c i) -> p j c i", j=4, c=4), [f"ps{bo}"], ["acc0"])
                attention(q2q, ["q2q"], k2q, "k2q", v2q, "v2q", prev2, evac2)
                dma(acc2_s[T, :, :, rq, :].rearrange("c p n -> p c n"), acc0[:, :].rearrange("p (c n) -> p c n", c=4),
                    "acc0", ["acc0"], ["acc2_s"])

        wq = []
        wstate = {"n": 0, "issued": 0, "pieces": []}

        def wpiece(src_view_fn):
            wstate["pieces"].append(src_view_fn)
            return len(wstate["pieces"]) - 1

        def ring_view(r, kind):
            if kind == "k8":
                return ring[r][:].rearrange("p (k c) -> p k c", k=8)
            if kind == "k4":
                return ring[r][:].rearrange("p (k c) -> p k c", k=4)
            raise ValueError

        def wissue(upto):
            while wstate["issued"] <= min(upto, len(wstate["pieces"]) - 1):
                i = wstate["issued"]
                r = i % NRING
                kind, src, sres = wstate["pieces"][i]
                dma(ring_view(r, kind), src, f"ring{r}", [sres], [f"ring{r}"])
                wstate["issued"] += 1

        def wget(i, keep=None):
            wissue((i if keep is None else keep) + NRING - 1)
            r = i % NRING
            return ring_view(r, wstate["pieces"][i][0]), f"ring{r}"

        for t in range(NT if STAGE >= 3 else 0):
            T, c_in = t // 4, t % 4
            par = t % 2
            base = len(wstate["pieces"])
            for c0 in (Q0, K0, V0, U0, Z0, GA0, GB0, GA0 + 512, GB0 + 512):
                wpiece(("k8", win_s[:, :, c0:c0 + 512], "win_s"))
            for h in range(2):
                wpiece(("k4", wout_s[:, 4 * h:4 * h + 4, :], "wout_s"))
            for p in range(8):
                wpiece(("k8", w1_s[:, :, p * 512:(p + 1) * 512], "w1_s"))
            for hf in range(2):
                for p in range(4):
                    wpiece(("k8", w2_s[hf, :, 8 * p:8 * p + 8, :], "w2_s"))
            PQ, PK, PV, PU, PZ, PGA0, PGB0, PGA1, PGB1, PO0, PO1 = [base + i for i in range(11)]
            PW1 = base + 11
            PW2 = base + 19

            for b in range(4):
                dma(xt[b][:], x[t * 512 + b * 128:t * 512 + (b + 1) * 128, :], f"xt{b}", [], [f"xt{b}"])
            if STAGE >= 3.06:
                dma(acc2t[:], acc2_s[T, c_in].rearrange("p q n -> p (q n)"), "acc2t", ["acc2_s"], ["acc2t"])
            norm_transpose4([(b, b * 128, f"xt{b}", b) for b in range(4)])

            if STAGE < 3.1:
                continue
            for which, PIDX in (("q", PQ), ("k", PK)):
                W, wres = wget(PIDX)
                for g in range(2):
                    load_tables(g, t * 512, g) if which == "q" else None
                    for c2 in range(2):
                        bk = nb()
                        col = g * 256 + c2 * 128
                        hsrc, hres_ = (hT, "hT") if g == 0 else (hT1, "hT1")
                        mm(ps[bk][:], [(W[:, kc, col:col + 128], hsrc[:, kc, :]) for kc in range(8)], [wres, hres_], [f"ps{bk}"])
                        if which == "q":
                            rope(bk, g, a2[:, 20 + 2 * g + c2, :], f"a2_{20 + 2 * g + c2}", 24 + c2)
                        else:
                            rope(bk, g, k01[g][par][:, c2, :], f"k01_{g}{par}", 24 + c2)
            if STAGE < 3.15:
                continue
            W, wres = wget(PV)
            for g in range(2):
                for b in range(4):
                    bk = nb()
                    hsrc, hres_ = (hT, "hT") if g == 0 else (hT1, "hT1")
                    mm(ps[bk][:, 0:256], [(hsrc[:, kc, b * 128:(b + 1) * 128], W[:, kc, g * 256:(g + 1) * 256]) for kc in range(8)],
                       [wres, hres_], [f"ps{bk}"])
                    cp("act" if b % 2 else "dve", v01[g][par][:, b, :, 0:64], ps[bk][:, 0:256].rearrange("p (j e) -> p j e", j=4),
                       [f"ps{bk}"], [f"v01_{g}{par}"])
            W, wres = wget(PU)
            for c in range(4):
                bk = nb()
                mm(ps[bk][:], [(W[:, kc, c * 128:(c + 1) * 128], hT[:, kc, :]) for kc in range(8)], [wres, "hT"], [f"ps{bk}"])
                act(a2[:, c, :], ps[bk][:], AF.Gelu_apprx_tanh, [f"ps{bk}"], [f"a2_{c}"])
            W, wres = wget(PZ)
            zf = []
            for b in range(4):
                bk = nb()
                mm(ps[bk][:], [(hT[:, kc, b * 128:(b + 1) * 128], W[:, kc, :]) for kc in range(8)], [wres, "hT"], [f"ps{bk}"])
                f1 = nfs()
                act(fs[f1][:], ps[bk][:], AF.Gelu_apprx_tanh, [f"ps{bk}"], [f"fs{f1}"])
                zf.append(f1)
            zs = rms_stats_batch([([fs[f1][:]], [f"fs{f1}"]) for f1 in zf], "ln")
            for b in range(4):
                f1, s = zf[b], zs[b]
                ts("dve", a2[:, 4 + b, :], fs[f1][:], stat[:, s, 12:13], stat[:, s, 15:16], ALU.subtract, ALU.mult,
                   [f"fs{f1}", f"stat{s}"], [f"a2_{4 + b}"])

            if STAGE < 3.3:
                continue
            accn = acc0[:, :].rearrange("p (j n) -> p j n", j=4)
            for g in range(2):
                qres_l = [f"a2_{20 + 2 * g}", f"a2_{21 + 2 * g}"]
                kc_t, kc_r = k01[g][par], f"k01_{g}{par}"
                vc_t, vc_r = v01[g][par], f"v01_{g}{par}"
                kp_t, kp_r = k01[g][1 - par], f"k01_{g}{1 - par}"
                vp_t, vp_r = v01[g][1 - par], f"v01_{g}{1 - par}"
                if g == 0:
                    def prev_of(b, t=t, kc_t=kc_t, kc_r=kc_r, vc_t=vc_t, vc_r=vc_r, kp_t=kp_t, kp_r=kp_r, vp_t=vp_t, vp_r=vp_r):
                        if b > 0:
                            return (kc_t, kc_r, vc_t, vc_r, b - 1)
                        return None if t == 0 else (kp_t, kp_r, vp_t, vp_r, 3)

                    def evac(b, bo):
                        cp("act", accn[:, :, b * 128:(b + 1) * 128], ps[bo][0:65, :].rearrange("p (j n) -> p j n", j=4), [f"ps{bo}"], ["acc0"])
                else:
                    def prev_of(b, t=t, kp_t=kp_t, kp_r=kp_r, vp_t=vp_t, vp_r=vp_r):
                        return None if t == 0 else (kp_t, kp_r, vp_t, vp_r, b)

                    def evac(b, bo):
                        dst = acc0[:, :].rearrange("p (j i r) -> p j i r", j=4, r=4)[:, :, :, b]
                        tt("dve", dst, ps[bo][0:65, :].rearrange("p (j n) -> p j n", j=4), dst, ALU.add, [f"ps{bo}", "acc0"], ["acc0"])

                class QT:
                    def __init__(self, g):
                        self.g = g

                    def __getitem__(self, idx):
                        pr, jp, cols = idx
                        return a2[pr, 20 + 2 * self.g + jp, cols]
                attention(QT(g), qres_l, kc_t, kc_r, vc_t, vc_r, prev_of, evac)
            if STAGE < 3.4:
                continue
            a_nat = acc0[:, :].rearrange("p (j i q b) -> p j i q b", j=4, q=4, b=4)
            a_g2 = acc2t[:, :].rearrange("p (q j b i) -> p j i q b", q=4, j=4, b=4)
            for j in range(4):
                tt("dve", a_nat[:, j], a_nat[:, j], a_g2[:, j], ALU.add, ["acc0", "acc2t"], ["acc0"])
            P.op("dve", lambda e: e.reciprocal(acc2t[64:65, :], acc0[64:65, :]), ["acc0"], ["acc2t"])
            for j in range(4):
                bk = nb()
                mm(ps[bk][0:64, :], [(ones_f[64:65, 0:64], acc2t[64:65, j * 512:(j + 1) * 512])], ["ones_f", "acc2t"], [f"ps{bk}"])
                tt("dve", a2[0:64, 16 + j, :], acc0[0:64, j * 512:(j + 1) * 512], ps[bk][0:64, :], ALU.mult, [f"ps{bk}", "acc0"], [f"a2_{16 + j}"])

            if STAGE < 3.5:
                continue
            for g in range(4):
                bk = nb()

                def fn(e, g=g, bk=bk):
                    ins = None
                    for b in range(4):
                        ins = e.matmul(ps[bk][:, b * 128:(b + 1) * 128], lhsT=a2[:, 4 + b, g * 128:(g + 1) * 128],
                                       rhs=wspT[:, g * 128:(g + 1) * 128], start=True, stop=True)
                    return ins
                P.op("pe", fn, [f"a2_{4 + b}" for b in range(4)] + ["wspT"], [f"ps{bk}"])
                f1 = nfs()
                for b in range(4):
                    stt("dve", fs[f1][:, b * 128:(b + 1) * 128], ps[bk][:, b * 128:(b + 1) * 128], lng[:, g:g + 1],
                        Cgm[:, g * 128:(g + 1) * 128], ALU.mult, ALU.add, [f"ps{bk}", "lng", "Cgm"], [f"fs{f1}"])
                tt("pool", a2[:, 8 + g, :], fs[f1][:], a2[:, g, :], ALU.mult, [f"fs{f1}", f"a2_{g}"], [f"a2_{8 + g}"])

            if STAGE < 3.6:
                continue
            for oc in range(8):
                WGA, rga = wget(PGA0 if oc < 4 else PGA1)
                WGB, rgb = wget(PGB0 if oc < 4 else PGB1, keep=(PGA0 if oc < 4 else PGA1))
                co = (oc % 4) * 128
                bA, bB, bGA, bGB = nb(), nb(), nb(), nb()
                mm(ps[bA][:], [(wba[:, j, oc * 128:(oc + 1) * 128], a2[0:64, 16 + j, :]) for j in range(4)],
                   ["wba"] + [f"a2_{16 + j}" for j in range(4)], [f"ps{bA}"])
                mm(ps[bB][:], [(wbg[:, g, oc * 128:(oc + 1) * 128], a2[:, 8 + g, :]) for g in range(4)],
                   ["wbg"] + [f"a2_{8 + g}" for g in range(4)], [f"ps{bB}"])
                mm(ps[bGA][:], [(WGA[:, kc, co:co + 128], hT[:, kc, :]) for kc in range(8)], [rga, "hT"], [f"ps{bGA}"])
                mm(ps[bGB][:], [(WGB[:, kc, co:co + 128], hT[:, kc, :]) for kc in range(8)], [rgb, "hT"], [f"ps{bGB}"])
                sa, sbb = 24 + (oc % 2), 26 + (oc % 2)
                act(a2[:, sa, :], ps[bGA][:], AF.Sigmoid, [f"ps{bGA}"], [f"a2_{sa}"])
                act(a2[:, sbb, :], ps[bGB][:], AF.Sigmoid, [f"ps{bGB}"], [f"a2_{sbb}"])
                f1, f2 = nfs(), nfs()
                tt("dve", fs[f1][:], ps[bA][:], a2[:, sa, :], ALU.mult, [f"ps{bA}", f"a2_{sa}"], [f"fs{f1}"])
                tt("dve", fs[f2][:], ps[bB][:], a2[:, sbb, :], ALU.mult, [f"ps{bB}", f"a2_{sbb}"], [f"fs{f2}"])
                ms = oc if oc < 8 else oc
                tt("pool", a2[:, ms, :], fs[f1][:], fs[f2][:], ALU.add, [f"fs{f1}", f"fs{f2}"] + [f"a2_{8 + g}" for g in range(4)], [f"a2_{ms}"])

            if STAGE < 3.7:
                continue
            WO0, ro0 = wget(PO0)
            WO1, ro1 = wget(PO1, keep=PO0)
            for b in range(4):
                by = [nb(), nb()]
                for hf in range(2):
                    pairs = []
                    for kc in range(8):
                        Wp = WO0 if kc < 4 else WO1
                        pairs.append((a2[:, kc, b * 128:(b + 1) * 128], Wp[:, kc % 4, hf * 512:(hf + 1) * 512]))
                    mm(ps[by[hf]][:], pairs, [ro0, ro1] + [f"a2_{kc}" for kc in range(8)], [f"ps{by[hf]}"])
                s = rms_stats([ps[by[0]][:], ps[by[1]][:]], [f"ps{by[0]}", f"ps{by[1]}"], "rms")
                for hf in range(2):
                    f1 = nfs()
                    stt("dve", fs[f1][:], ps[by[hf]][:], stat[:, s, 15:16], gpm[:, hf * 512:(hf + 1) * 512], ALU.mult, ALU.mult,
                        [f"ps{by[hf]}", f"stat{s}", "gpm"], [f"fs{f1}"])
                    tt("pool", xt[b][:, hf * 512:(hf + 1) * 512], xt[b][:, hf * 512:(hf + 1) * 512], fs[f1][:], ALU.add,
                       [f"fs{f1}", f"xt{b}"], [f"xt{b}"])
            if STAGE < 3.8:
                continue
            norm_transpose4([(b, b * 128, f"xt{b}", None) for b in range(4)])
            for f in range(32):
                W, wres = wget(PW1 + f // 4)
                bk = nb()
                co = (f % 4) * 128
                mm(ps[bk][:], [(W[:, kc, co:co + 128], hT[:, kc, :]) for kc in range(8)], [wres, "hT"], [f"ps{bk}"])
                act(rb[f % 2][:], ps[bk][:], AF.Relu, [f"ps{bk}"], [f"rb{f % 2}"])
                tt("pool", a2[:, f, :], rb[f % 2][:], rb[f % 2][:], ALU.mult, [f"rb{f % 2}"], [f"a2_{f}"])
            if STAGE < 3.9:
                continue
            for hf in range(2):
                for p in range(4):
                    W, wres = wget(PW2 + hf * 4 + p)
                    for b in range(4):
                        bk = hf * 4 + b

                        def fn(e, W=W, p=p, b=b, bk=bk):
                            ins = None
                            for ff in range(8):
                                ins = e.matmul(ps[bk][:], lhsT=a2[:, 8 * p + ff, b * 128:(b + 1) * 128], rhs=W[:, ff, :],
                                               start=(p == 0 and ff == 0), stop=(p == 3 and ff == 7), skip_group_check=True)
                            return ins
                        P.op("pe", fn, [wres] + [f"a2_{8 * p + ff}" for ff in range(8)], [f"ps{bk}"])
            bank_ctr[0] = 0
            fin = rms_stats_batch([([ps[b][:], ps[4 + b][:]], [f"ps{b}", f"ps{4 + b}"]) for b in range(4)], "rms")
            for b in range(4):
                s = fin[b]
                for hf in range(2):
                    f1 = nfs()
                    bk = hf * 4 + b
                    stt("dve", fs[f1][:], ps[bk][:], stat[:, s, 15:16], gpl[:, hf * 512:(hf + 1) * 512], ALU.mult, ALU.mult,
                        [f"ps{bk}", f"stat{s}", "gpl"], [f"fs{f1}"])
                    tt("pool", xt[b][:, hf * 512:(hf + 1) * 512], xt[b][:, hf * 512:(hf + 1) * 512], fs[f1][:], ALU.add,
                       [f"fs{f1}", f"xt{b}"], [f"xt{b}"])
                dma(out[t * 512 + b * 128:t * 512 + (b + 1) * 128, :], xt[b][:], f"xt{b}", [f"xt{b}"], ["out"])

        sems = {name: es.enter_context(nc.semaphore(name)) for name in sorted(P.semnames)}
        final = [(s, v) for s, v in P.dma_cnt.items()]
        with nc.Block() as block:
            @block.tensor
            def _(e):
                P.replay("pe", e, sems)

            @block.scalar
            def _(e):
                P.replay("act", e, sems)

            @block.vector
            def _(e):
                P.replay("dve", e, sems)

            @block.gpsimd
            def _(e):
                P.replay("pool", e, sems)

            @block.sync
            def _(e):
                P.replay("sp", e, sems, final_waits=final)
    return nc


def _tables(S):
    half = 32
    inv = (10000.0 ** (-np.arange(half, dtype=np.float32) / half)).astype(np.float32)
    idx = np.arange(S)
    pos = [idx.copy()]
    n, r, i = idx // 512, (idx % 512) // 128, idx % 128
    pos.append(512 * n + 4 * i + r)
    T, r, i = idx // 2048, (idx % 2048) // 128, idx % 128
    pos.append(2048 * T + 16 * i + r)
    cos = np.zeros((3, 128, S), np.float32)
    sin = np.zeros((3, 128, S), np.float32)
    m = np.arange(128)
    fr = inv[m % 32]
    sgn = np.where((m % 64) < 32, -1.0, 1.0).astype(np.float32)
    for g in range(3):
        ang = (pos[g].astype(np.float32)[None, :] * fr[:, None]).astype(np.float32)
        cos[g] = np.cos(ang)
        sin[g] = np.sin(ang) * sgn[:, None]
    return cos, sin


def _consts():
    bf = ml_dtypes.bfloat16
    ident = np.eye(128, dtype=np.float32).astype(bf)
    m = np.arange(128)
    sw = np.where((m % 64) < 32, m + 32, m - 32)
    rsw = np.zeros((128, 128), np.float32)
    rsw[sw, m] = 1.0
    k = np.arange(128)[:, None]
    q = np.arange(128)[None, :]
    half = np.concatenate([(k <= q), (k >= q)], axis=1).astype(np.float32)
    mask2 = np.concatenate([half, half], axis=1).astype(bf)
    tril = (k <= q).astype(np.float32)
    trilT = np.tile(tril, (1, 4)).astype(np.float32)
    n = np.arange(128)
    perm4 = np.zeros((128, 128), np.float32)
    perm4[n, 32 * (n % 4) + n // 4] = 1.0
    return ident, rsw.astype(bf), mask2, trilT, perm4.astype(bf)


_NC_CACHE = {}


def _host_inputs(S, x_b, p):
    cos, sin = _tables(S)
    ident, rsw, mask2, trilT, perm4 = _consts()
    f = np.float32

    def col8(v):
        return np.ascontiguousarray(np.asarray(v, f).reshape(8, 128).T)

    def col4(v):
        return np.ascontiguousarray(np.asarray(v, f).reshape(4, 128).T)
    wsp = np.asarray(p["w_spatial"], f)[0]
    wspT = np.ascontiguousarray(wsp.transpose(2, 0, 1).reshape(128, 512))
    bsp = np.asarray(p["b_spatial"], f)[0].reshape(1, 512)
    common = {
        "w_in": np.ascontiguousarray(np.asarray(p["w_in"], f)[0]),
        "w_ba": np.ascontiguousarray(np.asarray(p["w_branch_attn"], f)[0]),
        "w_bg": np.ascontiguousarray(np.asarray(p["w_branch_gmlp"], f)[0]),
        "w_out": np.ascontiguousarray(np.asarray(p["w_out"], f)[0]),
        "w1": np.ascontiguousarray(np.asarray(p["w_mlp_in"], f)[0]),
        "w2": np.ascontiguousarray(np.asarray(p["w_mlp_out"], f)[0]),
        "gpre": col8(np.asarray(p["norm_pre_mix"])[0]),
        "gpre2": col8(np.asarray(p["norm_pre_mlp"])[0]),
        "gpm_b": np.ascontiguousarray(np.broadcast_to(np.asarray(p["norm_post_mix"], f)[0][None, :], (128, D))),
        "gpl_b": np.ascontiguousarray(np.broadcast_to(np.asarray(p["norm_post_mlp"], f)[0][None, :], (128, D))),
        "wspT": wspT,
        "bsp_b": np.ascontiguousarray(np.broadcast_to(bsp, (128, 512))),
        "lng": col4(np.asarray(p["ln_v_gain"])[0]),
        "lnb": col4(np.asarray(p["ln_v_bias"])[0]),
        "cos_t": cos, "sin_t": sin, "ident": ident, "rsw": rsw, "perm4": perm4, "mask2": mask2, "trilT": trilT,
    }
    return [dict(common, x=np.ascontiguousarray(np.asarray(xb, f))) for xb in x_b]


def kernel(**inputs):
    x = np.asarray(inputs["x"], np.float32)
    B, S, _ = x.shape
    if S not in _NC_CACHE:
        _NC_CACHE[S] = build_nc(S)
    nc = _NC_CACHE[S]
    in_maps = _host_inputs(S, [x[b] for b in range(B)], inputs)
    res = run_bass_kernel_spmd(nc, in_maps, core_ids=list(range(B)))
    return np.stack([np.asarray(r["out"], np.float32) for r in res.results], axis=0)
```

```python
import contextlib
import numpy as np
import ml_dtypes
import concourse.bass as bass
import concourse.mybir as mybir
from concourse.bass_utils import run_bass_kernel_spmd

F32 = mybir.dt.float32
BF16 = mybir.dt.bfloat16
AF = mybir.ActivationFunctionType
ALU = mybir.AluOpType

D = 1024
INW = 5376
Q0, K0, V0, U0, Z0, GA0, GB0 = 0, 768, 1536, 2304, 2816, 3328, 4352
EPS = 1e-6
SELF_SYNC = True
import os
STAGE = float(os.environ.get('KSTAGE', '99'))
NRING = 5


class Prog:
    ENGS = ("pe", "act", "dve", "pool", "sp")

    def __init__(self):
        self.ops = {e: [] for e in self.ENGS}
        self.cnt = {e: 0 for e in self.ENGS}
        self.last_w = {}
        self.readers = {}
        self.waited = {e: {} for e in self.ENGS}
        self.dma_cnt = {}
        self.semnames = set()
        self.pending = {}

    def barrier_sp(self):
        self.pending = dict(self.dma_cnt)

    def op(self, eng, fn, reads=(), writes=(), chan=None):
        deps = []
        for r in reads:
            if r in self.last_w:
                deps.append((self.last_w[r], "raw"))
            if r.startswith("ps"):
                for t in self.readers.get(r, ()):
                    if t[2] != eng:
                        deps.append((t, "rar"))
        for w in writes:
            if w in self.last_w:
                deps.append((self.last_w[w], "waw"))
            for t in self.readers.get(w, ()):
                deps.append((t, "war"))
        waits = {}
        for (s, v, e), kind in deps:
            if e == eng:
                if eng == "pe" or eng == "sp":
                    if eng == "pe":
                        continue
                elif not SELF_SYNC or kind == "war":
                    continue
            if self.waited[eng].get(s, 0) >= v:
                continue
            waits[s] = max(waits.get(s, 0), v)
        if eng == "sp" and self.pending:
            for s, v in self.pending.items():
                if self.waited[eng].get(s, 0) < v:
                    waits[s] = max(waits.get(s, 0), v)
            self.pending = {}
        for s, v in waits.items():
            self.waited[eng][s] = v
        if eng == "sp":
            assert chan is not None
            s = "d_" + chan
            self.dma_cnt[s] = self.dma_cnt.get(s, 0) + 16
            tok = (s, self.dma_cnt[s], eng)
            inc = 16
        else:
            s = "c_" + eng
            self.cnt[eng] += 1
            tok = (s, self.cnt[eng], eng)
            inc = 1
        self.semnames.add(s)
        self.ops[eng].append((fn, sorted(waits.items()), s, inc))
        for w in writes:
            self.last_w[w] = tok
            self.readers[w] = []
        for r in reads:
            self.readers.setdefault(r, []).append(tok)
        return tok

    def replay(self, eng_name, eng, sems, final_waits=()):
        for fn, waits, s, inc in self.ops[eng_name]:
            for ws, wv in waits:
                eng.wait_ge(sems[ws], wv)
            ins = fn(eng)
            ins.then_inc(sems[s], inc)
        for ws, wv in final_waits:
            eng.wait_ge(sems[ws], wv)


def build_nc(S):
    NT = S // 512
    NST = S // 2048
    nc = bass.Bass("TRN2", target_bir_lowering=False)
    P = Prog()

    def din(name, shape, dt=F32):
        return nc.dram_tensor(name, list(shape), dt, kind="ExternalInput")

    x = din("x", [S, D])
    w_in = din("w_in", [D, INW])
    w_ba = din("w_ba", [256, D])
    w_bg = din("w_bg", [512, D])
    w_out = din("w_out", [D, D])
    w1 = din("w1", [D, 4096])
    w2 = din("w2", [4096, D])
    gpre_d = din("gpre", [128, 8])
    gpre2_d = din("gpre2", [128, 8])
    gpm_d = din("gpm_b", [128, D])
    gpl_d = din("gpl_b", [128, D])
    wspT_d = din("wspT", [128, 512])
    bsp_d = din("bsp_b", [128, 512])
    lng_d = din("lng", [128, 4])
    lnb_d = din("lnb", [128, 4])
    cos_d = din("cos_t", [3, 128, S])
    sin_d = din("sin_t", [3, 128, S])
    ident_d = din("ident", [128, 128], BF16)
    rsw_d = din("rsw", [128, 128], BF16)
    perm4_d = din("perm4", [128, 128], BF16)
    mask2_d = din("mask2", [128, 512], BF16)
    tril_d = din("trilT", [128, 512])
    out = nc.dram_tensor("out", [S, D], F32, kind="ExternalOutput")

    def dscr(name, shape, dt):
        return nc.dram_tensor(name, list(shape), dt, kind="Internal")

    win_s = dscr("win_s", [128, 8, INW], BF16)
    w1_s = dscr("w1_s", [128, 8, 4096], BF16)
    w2_s = dscr("w2_s", [2, 128, 32, 512], BF16)
    wout_s = dscr("wout_s", [128, 8, D], BF16)
    wbg_s = dscr("wbg_s", [128, 4, D], BF16)
    wba_s = dscr("wba_s", [64, 4, D], BF16)
    k2_s = dscr("k2_s", [NST, 4, 128, 1024], BF16)
    v2_s = dscr("v2_s", [NST, 4, 128, 1280], BF16)
    acc2_s = dscr("acc2_s", [NST, 4, 65, 4, 512], F32)

    es = contextlib.ExitStack()
    with es:
        def sb(name, shape, dt):
            return es.enter_context(nc.sbuf_tensor("s_" + name, list(shape), dt))

        ring = [sb(f"ring{k}", [128, 4096], BF16) for k in range(NRING)]
        wba = sb("wba", [64, 4, D], BF16)
        wbg = sb("wbg", [128, 4, D], BF16)
        hT = sb("hT", [128, 8, 512], BF16)
        hT1 = sb("hT1", [128, 8, 512], BF16)
        xt = [sb(f"xt{k}", [128, D], F32) for k in range(4)]
        hb = [sb(f"hb{k}", [128, D], BF16) for k in range(2)]
        a2 = sb("a2", [128, 32, 512], BF16)
        rb = [sb(f"rb{k}", [128, 512], BF16) for k in range(2)]
        k01 = [[sb(f"k01_{g}{p}", [128, 2, 512], BF16) for p in range(2)] for g in range(2)]
        v01 = [[sb(f"v01_{g}{p}", [128, 4, 4, 80], BF16) for p in range(2)] for g in range(2)]
        q2q = sb("q2q", [128, 2, 512], BF16)
        k2q = sb("k2q", [128, 2, 512], BF16)
        k2p = sb("k2p", [128, 2, 512], BF16)
        v2q = sb("v2q", [128, 4, 4, 80], BF16)
        v2p = sb("v2p", [128, 4, 4, 80], BF16)
        acc0 = sb("acc0", [65, 2048], F32)
        acc2t = sb("acc2t", [65, 2048], F32)
        fs = [sb(f"fs{k}", [128, 512], F32) for k in range(6)]
        cosT = [sb(f"cosT{k}", [128, 512], F32) for k in range(2)]
        sinT = [sb(f"sinT{k}", [128, 512], F32) for k in range(2)]
        ident = sb("ident", [128, 128], BF16)
        rsw = sb("rsw", [128, 128], BF16)
        perm4 = sb("perm4", [128, 128], BF16)
        mask2 = sb("mask2", [128, 512], BF16)
        wspT = sb("wspT", [128, 512], BF16)
        ones_bf = sb("ones_bf", [128, 128], BF16)
        ones_f = sb("ones_f", [128, 64], F32)
        Cgm = sb("Cgm", [128, 512], F32)
        lng = sb("lng", [128, 4], F32)
        lnb = sb("lnb", [128, 4], F32)
        gpre = sb("gpre", [128, 8], F32)
        gpre2 = sb("gpre2", [128, 8], F32)
        gpm = sb("gpm", [128, D], F32)
        gpl = sb("gpl", [128, D], F32)
        NSTAT = 16
        stat = sb("stat", [128, NSTAT, 16], F32)
        ps = [es.enter_context(nc.psum_tensor(f"ps{k}", [128, 512], F32)) for k in range(8)]

        bank_ctr = [0]

        def nb():
            b = bank_ctr[0] % 8
            bank_ctr[0] += 1
            return b

        stat_ctr = [0]

        def nstat():
            s = stat_ctr[0] % NSTAT
            stat_ctr[0] += 1
            return s

        fs_ctr = [0]

        def nfs():
            s = fs_ctr[0] % 6
            fs_ctr[0] += 1
            return s

        def dma(out_ap, in_ap, chan, reads, writes):
            P.op("sp", lambda e, o=out_ap, i=in_ap: e.dma_start(out=o, in_=i), reads, writes, chan=chan)

        def mm(out_ap, pairs, reads, writes):
            def fn(e, o=out_ap, pairs=pairs):
                n = len(pairs)
                ins = None
                for i, (l, r) in enumerate(pairs):
                    ins = e.matmul(o, lhsT=l, rhs=r, start=(i == 0), stop=(i == n - 1))
                return ins
            P.op("pe", fn, reads, writes)

        def act(out_ap, in_ap, func, reads, writes, scale=1.0, bias=0.0):
            P.op("act", lambda e: e.activation(out=out_ap, in_=in_ap, func=func, bias=bias, scale=scale), reads, writes)

        def tt(eng, out_ap, in0, in1, op, reads, writes):
            P.op(eng, lambda e: e.tensor_tensor(out=out_ap, in0=in0, in1=in1, op=op), reads, writes)

        def ts(eng, out_ap, in0, s1, s2, op0, op1, reads, writes):
            if s2 is None:
                P.op(eng, lambda e: e.tensor_scalar(out=out_ap, in0=in0, scalar1=s1, scalar2=None, op0=op0), reads, writes)
            else:
                P.op(eng, lambda e: e.tensor_scalar(out=out_ap, in0=in0, scalar1=s1, scalar2=s2, op0=op0, op1=op1), reads, writes)

        def stt(eng, out_ap, in0, scalar, in1, op0, op1, reads, writes):
            P.op(eng, lambda e: e.scalar_tensor_tensor(out=out_ap, in0=in0, scalar=scalar, in1=in1, op0=op0, op1=op1), reads, writes)

        def cp(eng, out_ap, in_ap, reads, writes):
            if eng == "act":
                P.op(eng, lambda e: e.activation(out=out_ap, in_=in_ap, func=AF.Copy), reads, writes)
            else:
                P.op(eng, lambda e: e.tensor_copy(out=out_ap, in_=in_ap), reads, writes)

        for i, (t, d, nm) in enumerate([(ident, ident_d, "ident"), (rsw, rsw_d, "rsw"), (perm4, perm4_d, "perm4"), (mask2, mask2_d, "mask2"),
                                        (lng, lng_d, "lng"), (lnb, lnb_d, "lnb"), (gpre, gpre_d, "gpre"),
                                        (gpre2, gpre2_d, "gpre2"), (gpm, gpm_d, "gpm"), (gpl, gpl_d, "gpl")]):
            dma(t[:], d.ap(), "c" + str(i), [], [nm])
        P.op("dve", lambda e: e.memset(ones_bf[:], 1.0), [], ["ones_bf"])
        P.op("dve", lambda e: e.memset(ones_f[:], 1.0), [], ["ones_f"])
        for g in range(2):
            for p in range(2):
                P.op("pool", lambda e, g=g, p=p: e.memset(v01[g][p][:, :, :, 64:80], 1.0), [], [f"v01_{g}{p}"])
        P.op("pool", lambda e: e.memset(v2q[:, :, :, 64:80], 1.0), [], ["v2q"])
        dma(fs[0][:], wspT_d.ap(), "fs0", [], ["fs0"])
        dma(fs[1][:], tril_d.ap(), "fs1", [], ["fs1"])
        dma(fs[2][:], bsp_d.ap(), "fs2", [], ["fs2"])
        tt("dve", wspT[:], fs[0][:], fs[1][:], ALU.mult, ["fs0", "fs1"], ["wspT"])
        mm(ps[0][:], [(ones_bf[:], wspT[:])], ["ones_bf", "wspT"], ["ps0"])
        for g in range(4):
            stt("dve", Cgm[:, g * 128:(g + 1) * 128], ps[0][:, g * 128:(g + 1) * 128], lnb[:, g:g + 1],
                fs[2][:, g * 128:(g + 1) * 128], ALU.mult, ALU.add, ["ps0", "lnb", "fs2"], ["Cgm"])

        stg = [0]

        def prep(src_ap, dst_ap, npart, ncol, scal, dst_res):
            k = stg[0] % 4
            r = stg[0] % NRING
            stg[0] += 1
            dma(xt[k][0:npart, 0:ncol], src_ap, f"xt{k}", [], [f"xt{k}"])
            if scal is None:
                cp("dve" if stg[0] % 2 else "act", ring[r][0:npart, 0:ncol], xt[k][0:npart, 0:ncol], [f"xt{k}"], [f"ring{r}"])
            elif stg[0] % 2:
                ts("dve", ring[r][0:npart, 0:ncol], xt[k][0:npart, 0:ncol], scal, None, ALU.mult, None,
                   [f"xt{k}", "gpre", "gpre2"], [f"ring{r}"])
            else:
                P.op("act", lambda e, o=ring[r][0:npart, 0:ncol], i=xt[k][0:npart, 0:ncol], sc=scal:
                     e.activation(out=o, in_=i, func=AF.Copy, scale=sc), [f"xt{k}", "gpre", "gpre2"], [f"ring{r}"])
            dma(dst_ap, ring[r][0:npart, 0:ncol], f"ring{r}", [f"ring{r}"], [dst_res])

        for kc in range(8):
            for c0 in range(0, INW, 1024):
                c1 = min(c0 + 1024, INW)
                prep(w_in[kc * 128:(kc + 1) * 128, c0:c1], win_s[:, kc, c0:c1], 128, c1 - c0, gpre[:, kc:kc + 1], "win_s")
        for kc in range(8):
            for c0 in range(0, 4096, 1024):
                prep(w1[kc * 128:(kc + 1) * 128, c0:c0 + 1024], w1_s[:, kc, c0:c0 + 1024], 128, 1024, gpre2[:, kc:kc + 1], "w1_s")
        for f in range(32):
            k = stg[0] % 4
            r = stg[0] % NRING
            stg[0] += 1
            dma(xt[k][:], w2[f * 128:(f + 1) * 128, :], f"xt{k}", [], [f"xt{k}"])
            cp("dve" if f % 2 else "act", ring[r][:, 0:1024], xt[k][:], [f"xt{k}"], [f"ring{r}"])
            for hf in range(2):
                dma(w2_s[hf, :, f, :], ring[r][:, hf * 512:(hf + 1) * 512], f"ring{r}", [f"ring{r}"], ["w2_s"])
        for kc in range(8):
            prep(w_out[kc * 128:(kc + 1) * 128, :], wout_s[:, kc, :], 128, 1024, None, "wout_s")
        for g in range(4):
            prep(w_bg[g * 128:(g + 1) * 128, :], wbg_s[:, g, :], 128, 1024, None, "wbg_s")
        for j in range(4):
            prep(w_ba[j * 64:(j + 1) * 64, :], wba_s[:, j, :], 64, 1024, None, "wba_s")
        P.barrier_sp()
        dma(wba[:], wba_s.ap(), "wba", ["wba_s"], ["wba"])
        dma(wbg[:], wbg_s.ap(), "wbg", ["wbg_s"], ["wbg"])

        def rms_stats(src_aps, src_res, what):
            s = nstat()
            sr = f"stat{s}"
            n = len(src_aps)
            st3 = stat[:, s, 0:6 * n].rearrange("p (a t) -> p a t", t=3)
            for i, a in enumerate(src_aps):
                P.op("dve", lambda e, i=i, a=a: e.bn_stats(st3[:, 2 * i:2 * i + 2, :], a), src_res, [sr])
            mv = stat[:, s, 12:14]
            P.op("dve", lambda e: e.bn_aggr(mv, st3), [sr], [sr])
            if what == "rms":
                stt("dve", stat[:, s, 14:15], stat[:, s, 12:13], stat[:, s, 12:13], stat[:, s, 13:14], ALU.mult, ALU.add, [sr], [sr])
                src = stat[:, s, 14:15]
            else:
                src = stat[:, s, 13:14]
            act(stat[:, s, 15:16], src, AF.Sqrt, [sr], [sr], scale=1.0, bias=EPS)
            P.op("dve", lambda e: e.reciprocal(stat[:, s, 15:16], stat[:, s, 15:16]), [sr], [sr])
            return s

        def norm_transpose(k, col, xres, hres_idx, g1blk=None):
            s = rms_stats([xt[k][:, 0:512], xt[k][:, 512:1024]], [xres], "rms")
            h = hb[hres_idx]
            hres = f"hb{hres_idx}"
            P.op("act", lambda e: e.activation(out=h[:], in_=xt[k][:], func=AF.Copy, scale=stat[:, s, 15:16]),
                 [xres, f"stat{s}"], [hres])
            b = nb()
            pst = ps[b][:].bitcast(BF16).rearrange("p (c t) -> p c t", t=128)

            def fn(e):
                ins = None
                for kc in range(8):
                    ins = e.transpose(pst[:, kc, :], h[:, kc * 128:(kc + 1) * 128], ident[:])
                return ins
            P.op("pe", fn, [hres, "ident"], [f"ps{b}"])
            cp("dve", hT[:, :, col:col + 128], pst, [f"ps{b}"], ["hT"])
            if g1blk is not None:
                for half in range(2):
                    bp = nb()

                    def fnp(e, half=half, bp=bp):
                        ins = None
                        for kk in range(4):
                            kc = 4 * half + kk
                            ins = e.matmul(ps[bp][:, kk * 128:(kk + 1) * 128], lhsT=h[:, kc * 128:(kc + 1) * 128], rhs=perm4[:],
                                           start=True, stop=True)
                        return ins
                    P.op("pe", fnp, [hres, "perm4"], [f"ps{bp}"])
                    dst = hT1[:, 4 * half:4 * half + 4, :].rearrange("p k (r i) -> p k r i", r=4)[:, :, :, 32 * g1blk:32 * g1blk + 32]
                    src = ps[bp][:].rearrange("p (k r i) -> p k r i", k=4, r=4)
                    cp("act" if half else "dve", dst, src, [f"ps{bp}"], ["hT1"])

        def rms_stats_batch(jobs, what):
            slots = [nstat() for _ in jobs]
            for (aps, res), s_ in zip(jobs, slots):
                st3 = stat[:, s_, 0:6 * len(aps)].rearrange("p (a t) -> p a t", t=3)
                for i, a in enumerate(aps):
                    P.op("dve", lambda e, i=i, a=a, st3=st3: e.bn_stats(st3[:, 2 * i:2 * i + 2, :], a), res, [f"stat{s_}"])
            for (aps, res), s_ in zip(jobs, slots):
                st3 = stat[:, s_, 0:6 * len(aps)].rearrange("p (a t) -> p a t", t=3)
                P.op("dve", lambda e, s_=s_, st3=st3: e.bn_aggr(stat[:, s_, 12:14], st3), [f"stat{s_}"], [f"stat{s_}"])
            if what == "rms":
                for s_ in slots:
                    stt("dve", stat[:, s_, 14:15], stat[:, s_, 12:13], stat[:, s_, 12:13], stat[:, s_, 13:14], ALU.mult, ALU.add,
                        [f"stat{s_}"], [f"stat{s_}"])
            col = 14 if what == "rms" else 13
            for s_ in slots:
                act(stat[:, s_, 15:16], stat[:, s_, col:col + 1], AF.Sqrt, [f"stat{s_}"], [f"stat{s_}"], scale=1.0, bias=EPS)
            for s_ in slots:
                P.op("dve", lambda e, s_=s_: e.reciprocal(stat[:, s_, 15:16], stat[:, s_, 15:16]), [f"stat{s_}"], [f"stat{s_}"])
            return slots

        def norm_transpose4(jobs):
            slots = [nstat() for _ in jobs]
            st3s = [stat[:, s_, 0:12].rearrange("p (a t) -> p a t", t=3) for s_ in slots]
            for (k, col, xres, g1), s_, st3 in zip(jobs, slots, st3s):
                for i in range(2):
                    P.op("dve", lambda e, i=i, st3=st3, k=k: e.bn_stats(st3[:, 2 * i:2 * i + 2, :], xt[k][:, i * 512:(i + 1) * 512]),
                         [xres], [f"stat{s_}"])
            for s_, st3 in zip(slots, st3s):
                P.op("dve", lambda e, s_=s_, st3=st3: e.bn_aggr(stat[:, s_, 12:14], st3), [f"stat{s_}"], [f"stat{s_}"])
            for s_ in slots:
                stt("dve", stat[:, s_, 14:15], stat[:, s_, 12:13], stat[:, s_, 12:13], stat[:, s_, 13:14], ALU.mult, ALU.add,
                    [f"stat{s_}"], [f"stat{s_}"])
            for s_ in slots:
                act(stat[:, s_, 15:16], stat[:, s_, 14:15], AF.Sqrt, [f"stat{s_}"], [f"stat{s_}"], scale=1.0, bias=EPS)
            for s_ in slots:
                P.op("dve", lambda e, s_=s_: e.reciprocal(stat[:, s_, 15:16], stat[:, s_, 15:16]), [f"stat{s_}"], [f"stat{s_}"])
            for idx, ((k, col, xres, g1blk), s_) in enumerate(zip(jobs, slots)):
                hi = idx % 2
                h = hb[hi]
                hres = f"hb{hi}"
                P.op("act", lambda e, h=h, k=k, s_=s_: e.activation(out=h[:], in_=xt[k][:], func=AF.Copy, scale=stat[:, s_, 15:16]),
                     [xres, f"stat{s_}"], [hres])
                b = nb()
                pst = ps[b][:].bitcast(BF16).rearrange("p (c t) -> p c t", t=128)

                def fn(e, h=h, pst=pst):
                    ins = None
                    for kc in range(8):
                        ins = e.transpose(pst[:, kc, :], h[:, kc * 128:(kc + 1) * 128], ident[:])
                    return ins
                P.op("pe", fn, [hres, "ident"], [f"ps{b}"])
                cp("dve", hT[:, :, col:col + 128], pst, [f"ps{b}"], ["hT"])
                if g1blk is not None:
                    for half in range(2):
                        bp = nb()

                        def fnp(e, half=half, bp=bp, h=h):
                            ins = None
                            for kk in range(4):
                                kc = 4 * half + kk
                                ins = e.matmul(ps[bp][:, kk * 128:(kk + 1) * 128], lhsT=h[:, kc * 128:(kc + 1) * 128], rhs=perm4[:],
                                               start=True, stop=True)
                            return ins
                        P.op("pe", fnp, [hres, "perm4"], [f"ps{bp}"])
                        dst = hT1[:, 4 * half:4 * half + 4, :].rearrange("p k (r i) -> p k r i", r=4)[:, :, :, 32 * g1blk:32 * g1blk + 32]
                        src = ps[bp][:].rearrange("p (k r i) -> p k r i", k=4, r=4)
                        cp("act" if half else "dve", dst, src, [f"ps{bp}"], ["hT1"])

        def rope(b, tabk, dst_ap, dst_res, scratch_slot):
            raw = a2[:, scratch_slot, :]
            rres = f"a2_{scratch_slot}"
            act(raw, ps[b][:], AF.Copy, [f"ps{b}"], [rres])
            b2 = nb()
            mm(ps[b2][:], [(rsw[:], raw)], ["rsw", rres], [f"ps{b2}"])
            f1 = nfs()
            f2 = nfs()
            tt("pool", fs[f1][:], raw, cosT[tabk][:], ALU.mult, [rres, f"cosT{tabk}"], [f"fs{f1}"])
            tt("dve", fs[f2][:], ps[b2][:], sinT[tabk][:], ALU.mult, [f"ps{b2}", f"sinT{tabk}"], [f"fs{f2}"])
            tt("pool", dst_ap, fs[f1][:], fs[f2][:], ALU.add, [f"fs{f1}", f"fs{f2}"], [dst_res])

        ep_ctr = [0]

        def attention(qT, qres, kcur, kcres, vcur, vcres, prev_of, evac):
            items = [(b, j) for b in range(4) for j in range(4)]
            bo_of = {}
            st = {}

            def emit_S(i):
                b, j = items[i]
                if b not in bo_of:
                    bo_of[b] = nb()
                pv = prev_of(b)
                jp, hh = j // 2, j % 2
                lo = 64 * hh
                bs_ = nb()
                sl = 28 + (ep_ctr[0] % 2)
                sp_ = 30 + (ep_ctr[0] % 2)
                ep_ctr[0] += 1
                ncol = 256 if pv is not None else 128
                E = a2[:, sl, 0:ncol]
                Pm = a2[:, sp_, 0:ncol]
                reads = list(qres) + [kcres]
                if pv is not None:
                    reads.append(pv[1])

                def fn(e, b=b, jp=jp, lo=lo, bs_=bs_, pv=pv):
                    Q = qT[lo:lo + 64, jp, b * 128:(b + 1) * 128]
                    ins = e.matmul(ps[bs_][:, 0:128], lhsT=kcur[lo:lo + 64, jp, b * 128:(b + 1) * 128], rhs=Q, start=True, stop=True)
                    if pv is not None:
                        kb = pv[4]
                        ins = e.matmul(ps[bs_][:, 128:256], lhsT=pv[0][lo:lo + 64, jp, kb * 128:(kb + 1) * 128], rhs=Q,
                                       start=True, stop=True)
                    return ins
                P.op("pe", fn, reads, [f"ps{bs_}"])
                act(E, ps[bs_][:, 0:ncol], AF.Exp, [f"ps{bs_}"], [f"a2_{sl}"], scale=0.125)
                tt("dve", Pm, E, mask2[:, 0:ncol], ALU.mult, [f"a2_{sl}", "mask2"], [f"a2_{sp_}"])
                st[i] = (pv, sp_)

            def emit_PV(i):
                b, j = items[i]
                pv, sp_ = st.pop(i)
                bo = bo_of[b]
                reads = [f"a2_{sp_}", vcres]
                if pv is not None:
                    reads.append(pv[3])

                def fn2(e, b=b, j=j, bo=bo, pv=pv, sp_=sp_):
                    o = ps[bo][0:65, j * 128:(j + 1) * 128]
                    ins = e.matmul(o, lhsT=vcur[:, b, j, 0:65], rhs=a2[:, sp_, 0:128], start=True, stop=(pv is None))
                    if pv is not None:
                        ins = e.matmul(o, lhsT=pv[2][:, pv[4], j, 0:65], rhs=a2[:, sp_, 128:256], start=False, stop=True)
                    return ins
                P.op("pe", fn2, reads, [f"ps{bo}"])
                if j == 3:
                    evac(b, bo)

            emit_S(0)
            for i in range(len(items)):
                if i + 1 < len(items):
                    emit_S(i + 1)
                emit_PV(i)

        def load_tables(g, off, k):
            dma(cosT[k][:], cos_d[g, :, off:off + 512], f"cosT{k}", [], [f"cosT{k}"])
            dma(sinT[k][:], sin_d[g, :, off:off + 512], f"sinT{k}", [], [f"sinT{k}"])

        x_g2 = x.ap().rearrange("(t i r) d -> t r i d", i=128, r=16)
        rA, rB = 0, 1
        dma(ring[rA][:].rearrange("p (k c) -> p k c", k=8)[:, :, 0:256], win_s[:, :, Q0 + 512:Q0 + 768], f"ring{rA}", ["win_s"], [f"ring{rA}"])
        dma(ring[rA][:].rearrange("p (k c) -> p k c", k=8)[:, :, 256:512], win_s[:, :, K0 + 512:K0 + 768], f"ring{rA}", ["win_s"], [f"ring{rA}"])
        dma(ring[rB][:, 0:2048].rearrange("p (k c) -> p k c", k=8), win_s[:, :, V0 + 512:V0 + 768], f"ring{rB}", ["win_s"], [f"ring{rB}"])
        WA = ring[rA][:].rearrange("p (k c) -> p k c", k=8)
        WB = ring[rB][:, 0:2048].rearrange("p (k c) -> p k c", k=8)
        qi = 0
        for T in range(NST if STAGE >= 2 else 0):
            for rq in range(4):
                tk = qi % 2
                qi += 1
                load_tables(2, T * 2048 + rq * 512, tk)
                if T > 0:
                    dma(k2p[:], k2_s[T - 1, rq].rearrange("p (c n) -> p c n", c=2), "k2p", ["k2_s"], ["k2p"])
                    dma(v2p[:], v2_s[T - 1, rq].rearrange("p (b j e) -> p b j e", b=4, j=4), "v2p", ["v2_s"], ["v2p"])
                for b in range(4):
                    dma(xt[b][:], x_g2[T, 4 * rq + b], f"xt{b}", [], [f"xt{b}"])
                norm_transpose4([(b, b * 128, f"xt{b}", None) for b in range(4)])
                if STAGE < 2.2:
                    continue
                for c in range(4):
                    bk = nb()
                    mm(ps[bk][:], [(WA[:, kc, c * 128:(c + 1) * 128], hT[:, kc, :]) for kc in range(8)], [f"ring{rA}", "hT"], [f"ps{bk}"])
                    if STAGE < 2.25:
                        continue
                    if c < 2:
                        rope(bk, tk, q2q[:, c, :], "q2q", 24 + c % 2)
                    else:
                        rope(bk, tk, k2q[:, c - 2, :], "k2q", 24 + c % 2)
                if STAGE < 2.3:
                    continue
                for b in range(4):
                    bk = nb()
                    mm(ps[bk][:, 0:256], [(hT[:, kc, b * 128:(b + 1) * 128], WB[:, kc, :]) for kc in range(8)], [f"ring{rB}", "hT"], [f"ps{bk}"])
                    cp("act" if b % 2 else "dve", v2q[:, b, :, 0:64], ps[bk][:, 0:256].rearrange("p (j e) -> p j e", j=4), [f"ps{bk}"], ["v2q"])
                if STAGE < 2.4:
                    continue
                dma(k2_s[T, rq].rearrange("p (c n) -> p c n", c=2), k2q[:], "k2q", ["k2q"], ["k2_s"])
                dma(v2_s[T, rq].rearrange("p (b j e) -> p b j e", b=4, j=4), v2q[:], "v2q", ["v2q"], ["v2_s"])

                if STAGE < 2.5:
                    continue

                def prev2(b, T=T):
                    return None if T == 0 else (k2p, "k2p", v2p, "v2p", b)

                def evac2(b, bo):
                    cp("act", acc0[:, :].rearrange("p (c j b i) -> p j b c i", c=4, j=4, b=4)[:, :, b, :, :],
                       ps[bo][0:65, :].rearrange("p (j c i) -> p j c i", j=4, c=4), [f"ps{bo}"], ["acc0"])
                attention(q2q, ["q2q"], k2q, "k2q", v2q, "v2q", prev2, evac2)
                dma(acc2_s[T, :, :, rq, :].rearrange("c p n -> p c n"), acc0[:, :].rearrange("p (c n) -> p c n", c=4),
                    "acc0", ["acc0"], ["acc2_s"])

        wq = []
        wstate = {"n": 0, "issued": 0, "pieces": []}

        def wpiece(src_view_fn):
            wstate["pieces"].append(src_view_fn)
            return len(wstate["pieces"]) - 1

        def ring_view(r, kind):
            if kind == "k8":
                return ring[r][:].rearrange("p (k c) -> p k c", k=8)
            if kind == "k4":
                return ring[r][:].rearrange("p (k c) -> p k c", k=4)
            raise ValueError

        def wissue(upto):
            while wstate["issued"] <= min(upto, len(wstate["pieces"]) - 1):
                i = wstate["issued"]
                r = i % NRING
                kind, src, sres = wstate["pieces"][i]
                dma(ring_view(r, kind), src, f"ring{r}", [sres], [f"ring{r}"])
                wstate["issued"] += 1

        def wget(i, keep=None):
            wissue((i if keep is None else keep) + NRING - 1)
            r = i % NRING
            return ring_view(r, wstate["pieces"][i][0]), f"ring{r}"

        for t in range(NT if STAGE >= 3 else 0):
            T, c_in = t // 4, t % 4
            par = t % 2
            base = len(wstate["pieces"])
            for c0 in (Q0, K0, V0, U0, Z0, GA0, GB0, GA0 + 512, GB0 + 512):
                wpiece(("k8", win_s[:, :, c0:c0 + 512], "win_s"))
            for h in range(2):
                wpiece(("k4", wout_s[:, 4 * h:4 * h + 4, :], "wout_s"))
            for p in range(8):
                wpiece(("k8", w1_s[:, :, p * 512:(p + 1) * 512], "w1_s"))
            for hf in range(2):
                for p in range(4):
                    wpiece(("k8", w2_s[hf, :, 8 * p:8 * p + 8, :], "w2_s"))
            PQ, PK, PV, PU, PZ, PGA0, PGB0, PGA1, PGB1, PO0, PO1 = [base + i for i in range(11)]
            PW1 = base + 11
            PW2 = base + 19

            for b in range(4):
                dma(xt[b][:], x[t * 512 + b * 128:t * 512 + (b + 1) * 128, :], f"xt{b}", [], [f"xt{b}"])
            if STAGE >= 3.06:
                dma(acc2t[:], acc2_s[T, c_in].rearrange("p q n -> p (q n)"), "acc2t", ["acc2_s"], ["acc2t"])
            norm_transpose4([(b, b * 128, f"xt{b}", b) for b in range(4)])

            if STAGE < 3.1:
                continue
            for which, PIDX in (("q", PQ), ("k", PK)):
                W, wres = wget(PIDX)
                for g in range(2):
                    load_tables(g, t * 512, g) if which == "q" else None
                    for c2 in range(2):
                        bk = nb()
                        col = g * 256 + c2 * 128
                        hsrc, hres_ = (hT, "hT") if g == 0 else (hT1, "hT1")
                        mm(ps[bk][:], [(W[:, kc, col:col + 128], hsrc[:, kc, :]) for kc in range(8)], [wres, hres_], [f"ps{bk}"])
                        if which == "q":
                            rope(bk, g, a2[:, 20 + 2 * g + c2, :], f"a2_{20 + 2 * g + c2}", 24 + c2)
                        else:
                            rope(bk, g, k01[g][par][:, c2, :], f"k01_{g}{par}", 24 + c2)
            if STAGE < 3.15:
                continue
            W, wres = wget(PV)
            for g in range(2):
                for b in range(4):
                    bk = nb()
                    hsrc, hres_ = (hT, "hT") if g == 0 else (hT1, "hT1")
                    mm(ps[bk][:, 0:256], [(hsrc[:, kc, b * 128:(b + 1) * 128], W[:, kc, g * 256:(g + 1) * 256]) for kc in range(8)],
                       [wres, hres_], [f"ps{bk}"])
                    cp("act" if b % 2 else "dve", v01[g][par][:, b, :, 0:64], ps[bk][:, 0:256].rearrange("p (j e) -> p j e", j=4),
                       [f"ps{bk}"], [f"v01_{g}{par}"])
            W, wres = wget(PU)
            for c in range(4):
                bk = nb()
                mm(ps[bk][:], [(W[:, kc, c * 128:(c + 1) * 128], hT[:, kc, :]) for kc in range(8)], [wres, "hT"], [f"ps{bk}"])
                act(a2[:, c, :], ps[bk][:], AF.Gelu_apprx_tanh, [f"ps{bk}"], [f"a2_{c}"])
            W, wres = wget(PZ)
            zf = []
            for b in range(4):
                bk = nb()
                mm(ps[bk][:], [(hT[:, kc, b * 128:(b + 1) * 128], W[:, kc, :]) for kc in range(8)], [wres, "hT"], [f"ps{bk}"])
                f1 = nfs()
                act(fs[f1][:], ps[bk][:], AF.Gelu_apprx_tanh, [f"ps{bk}"], [f"fs{f1}"])
                zf.append(f1)
            zs = rms_stats_batch([([fs[f1][:]], [f"fs{f1}"]) for f1 in zf], "ln")
            for b in range(4):
                f1, s = zf[b], zs[b]
                ts("dve", a2[:, 4 + b, :], fs[f1][:], stat[:, s, 12:13], stat[:, s, 15:16], ALU.subtract, ALU.mult,
                   [f"fs{f1}", f"stat{s}"], [f"a2_{4 + b}"])

            if STAGE < 3.3:
                continue
            accn = acc0[:, :].rearrange("p (j n) -> p j n", j=4)
            for g in range(2):
                qres_l = [f"a2_{20 + 2 * g}", f"a2_{21 + 2 * g}"]
                kc_t, kc_r = k01[g][par], f"k01_{g}{par}"
                vc_t, vc_r = v01[g][par], f"v01_{g}{par}"
                kp_t, kp_r = k01[g][1 - par], f"k01_{g}{1 - par}"
                vp_t, vp_r = v01[g][1 - par], f"v01_{g}{1 - par}"
                if g == 0:
                    def prev_of(b, t=t, kc_t=kc_t, kc_r=kc_r, vc_t=vc_t, vc_r=vc_r, kp_t=kp_t, kp_r=kp_r, vp_t=vp_t, vp_r=vp_r):
                        if b > 0:
                            return (kc_t, kc_r, vc_t, vc_r, b - 1)
                        return None if t == 0 else (kp_t, kp_r, vp_t, vp_r, 3)

                    def evac(b, bo):
                        cp("act", accn[:, :, b * 128:(b + 1) * 128], ps[bo][0:65, :].rearrange("p (j n) -> p j n", j=4), [f"ps{bo}"], ["acc0"])
                else:
                    def prev_of(b, t=t, kp_t=kp_t, kp_r=kp_r, vp_t=vp_t, vp_r=vp_r):
                        return None if t == 0 else (kp_t, kp_r, vp_t, vp_r, b)

                    def evac(b, bo):
                        dst = acc0[:, :].rearrange("p (j i r) -> p j i r", j=4, r=4)[:, :, :, b]
                        tt("dve", dst, ps[bo][0:65, :].rearrange("p (j n) -> p j n", j=4), dst, ALU.add, [f"ps{bo}", "acc0"], ["acc0"])

                class QT:
                    def __init__(self, g):
                        self.g = g

                    def __getitem__(self, idx):
                        pr, jp, cols = idx
                        return a2[pr, 20 + 2 * self.g + jp, cols]
                attention(QT(g), qres_l, kc_t, kc_r, vc_t, vc_r, prev_of, evac)
            if STAGE < 3.4:
                continue
            a_nat = acc0[:, :].rearrange("p (j i q b) -> p j i q b", j=4, q=4, b=4)
            a_g2 = acc2t[:, :].rearrange("p (q j b i) -> p j i q b", q=4, j=4, b=4)
            for j in range(4):
                tt("dve", a_nat[:, j], a_nat[:, j], a_g2[:, j], ALU.add, ["acc0", "acc2t"], ["acc0"])
            P.op("dve", lambda e: e.reciprocal(acc2t[64:65, :], acc0[64:65, :]), ["acc0"], ["acc2t"])
            for j in range(4):
                bk = nb()
                mm(ps[bk][0:64, :], [(ones_f[64:65, 0:64], acc2t[64:65, j * 512:(j + 1) * 512])], ["ones_f", "acc2t"], [f"ps{bk}"])
                tt("dve", a2[0:64, 16 + j, :], acc0[0:64, j * 512:(j + 1) * 512], ps[bk][0:64, :], ALU.mult, [f"ps{bk}", "acc0"], [f"a2_{16 + j}"])

            if STAGE < 3.5:
                continue
            for g in range(4):
                bk = nb()

                def fn(e, g=g, bk=bk):
                    ins = None
                    for b in range(4):
                        ins = e.matmul(ps[bk][:, b * 128:(b + 1) * 128], lhsT=a2[:, 4 + b, g * 128:(g + 1) * 128],
                                       rhs=wspT[:, g * 128:(g + 1) * 128], start=True, stop=True)
                    return ins
                P.op("pe", fn, [f"a2_{4 + b}" for b in range(4)] + ["wspT"], [f"ps{bk}"])
                f1 = nfs()
                for b in range(4):
                    stt("dve", fs[f1][:, b * 128:(b + 1) * 128], ps[bk][:, b * 128:(b + 1) * 128], lng[:, g:g + 1],
                        Cgm[:, g * 128:(g + 1) * 128], ALU.mult, ALU.add, [f"ps{bk}", "lng", "Cgm"], [f"fs{f1}"])
                tt("pool", a2[:, 8 + g, :], fs[f1][:], a2[:, g, :], ALU.mult, [f"fs{f1}", f"a2_{g}"], [f"a2_{8 + g}"])

            if STAGE < 3.6:
                continue
            for oc in range(8):
                WGA, rga = wget(PGA0 if oc < 4 else PGA1)
                WGB, rgb = wget(PGB0 if oc < 4 else PGB1, keep=(PGA0 if oc < 4 else PGA1))
                co = (oc % 4) * 128
                bA, bB, bGA, bGB = nb(), nb(), nb(), nb()
                mm(ps[bA][:], [(wba[:, j, oc * 128:(oc + 1) * 128], a2[0:64, 16 + j, :]) for j in range(4)],
                   ["wba"] + [f"a2_{16 + j}" for j in range(4)], [f"ps{bA}"])
                mm(ps[bB][:], [(wbg[:, g, oc * 128:(oc + 1) * 128], a2[:, 8 + g, :]) for g in range(4)],
                   ["wbg"] + [f"a2_{8 + g}" for g in range(4)], [f"ps{bB}"])
                mm(ps[bGA][:], [(WGA[:, kc, co:co + 128], hT[:, kc, :]) for kc in range(8)], [rga, "hT"], [f"ps{bGA}"])
                mm(ps[bGB][:], [(WGB[:, kc, co:co + 128], hT[:, kc, :]) for kc in range(8)], [rgb, "hT"], [f"ps{bGB}"])
                sa, sbb = 24 + (oc % 2), 26 + (oc % 2)
                act(a2[:, sa, :], ps[bGA][:], AF.Sigmoid, [f"ps{bGA}"], [f"a2_{sa}"])
                act(a2[:, sbb, :], ps[bGB][:], AF.Sigmoid, [f"ps{bGB}"], [f"a2_{sbb}"])
                f1, f2 = nfs(), nfs()
                tt("dve", fs[f1][:], ps[bA][:], a2[:, sa, :], ALU.mult, [f"ps{bA}", f"a2_{sa}"], [f"fs{f1}"])
                tt("dve", fs[f2][:], ps[bB][:], a2[:, sbb, :], ALU.mult, [f"ps{bB}", f"a2_{sbb}"], [f"fs{f2}"])
                ms = oc if oc < 8 else oc
                tt("pool", a2[:, ms, :], fs[f1][:], fs[f2][:], ALU.add, [f"fs{f1}", f"fs{f2}"] + [f"a2_{8 + g}" for g in range(4)], [f"a2_{ms}"])

            if STAGE < 3.7:
                continue
            WO0, ro0 = wget(PO0)
            WO1, ro1 = wget(PO1, keep=PO0)
            for b0 in (0, 2):
                bys = {}
                for b in (b0, b0 + 1):
                    by = [nb(), nb()]
                    bys[b] = by
                    for hf in range(2):
                        pairs = []
                        for kc in range(8):
                            Wp = WO0 if kc < 4 else WO1
                            pairs.append((a2[:, kc, b * 128:(b + 1) * 128], Wp[:, kc % 4, hf * 512:(hf + 1) * 512]))
                        mm(ps[by[hf]][:], pairs, [ro0, ro1] + [f"a2_{kc}" for kc in range(8)], [f"ps{by[hf]}"])
                sl2 = rms_stats_batch([([ps[bys[b][0]][:], ps[bys[b][1]][:]], [f"ps{bys[b][0]}", f"ps{bys[b][1]}"]) for b in (b0, b0 + 1)], "rms")
                for bi, b in enumerate((b0, b0 + 1)):
                    s = sl2[bi]
                    by = bys[b]
                    for hf in range(2):
                        f1 = nfs()
                        stt("dve", fs[f1][:], ps[by[hf]][:], stat[:, s, 15:16], gpm[:, hf * 512:(hf + 1) * 512], ALU.mult, ALU.mult,
                            [f"ps{by[hf]}", f"stat{s}", "gpm"], [f"fs{f1}"])
                        tt("pool", xt[b][:, hf * 512:(hf + 1) * 512], xt[b][:, hf * 512:(hf + 1) * 512], fs[f1][:], ALU.add,
                           [f"fs{f1}", f"xt{b}"], [f"xt{b}"])
            if STAGE < 3.8:
                continue
            norm_transpose4([(b, b * 128, f"xt{b}", None) for b in range(4)])
            for f in range(32):
                W, wres = wget(PW1 + f // 4)
                bk = nb()
                co = (f % 4) * 128
                mm(ps[bk][:], [(W[:, kc, co:co + 128], hT[:, kc, :]) for kc in range(8)], [wres, "hT"], [f"ps{bk}"])
                act(rb[f % 2][:], ps[bk][:], AF.Relu, [f"ps{bk}"], [f"rb{f % 2}"])
                tt("pool", a2[:, f, :], rb[f % 2][:], rb[f % 2][:], ALU.mult, [f"rb{f % 2}"], [f"a2_{f}"])
            if STAGE < 3.9:
                continue
            for hf in range(2):
                for p in range(4):
                    W, wres = wget(PW2 + hf * 4 + p)
                    for b in range(4):
                        bk = hf * 4 + b

                        def fn(e, W=W, p=p, b=b, bk=bk):
                            ins = None
                            for ff in range(8):
                                ins = e.matmul(ps[bk][:], lhsT=a2[:, 8 * p + ff, b * 128:(b + 1) * 128], rhs=W[:, ff, :],
                                               start=(p == 0 and ff == 0), stop=(p == 3 and ff == 7), skip_group_check=True)
                            return ins
                        P.op("pe", fn, [wres] + [f"a2_{8 * p + ff}" for ff in range(8)], [f"ps{bk}"])
            bank_ctr[0] = 0
            fin = rms_stats_batch([([ps[b][:], ps[4 + b][:]], [f"ps{b}", f"ps{4 + b}"]) for b in range(4)], "rms")
            for b in range(4):
                s = fin[b]
                for hf in range(2):
                    f1 = nfs()
                    bk = hf * 4 + b
                    stt("dve", fs[f1][:], ps[bk][:], stat[:, s, 15:16], gpl[:, hf * 512:(hf + 1) * 512], ALU.mult, ALU.mult,
                        [f"ps{bk}", f"stat{s}", "gpl"], [f"fs{f1}"])
                    tt("pool", xt[b][:, hf * 512:(hf + 1) * 512], xt[b][:, hf * 512:(hf + 1) * 512], fs[f1][:], ALU.add,
                       [f"fs{f1}", f"xt{b}"], [f"xt{b}"])
                dma(out[t * 512 + b * 128:t * 512 + (b + 1) * 128, :], xt[b][:], f"xt{b}", [f"xt{b}"], ["out"])

        sems = {name: es.enter_context(nc.semaphore(name)) for name in sorted(P.semnames)}
        final = [(s, v) for s, v in P.dma_cnt.items()]
        with nc.Block() as block:
            @block.tensor
            def _(e):
                P.replay("pe", e, sems)

            @block.scalar
            def _(e):
                P.replay("act", e, sems)

            @block.vector
            def _(e):
                P.replay("dve", e, sems)

            @block.gpsimd
            def _(e):
                P.replay("pool", e, sems)

            @block.sync
            def _(e):
                P.replay("sp", e, sems, final_waits=final)
    return nc


def _tables(S):
    half = 32
    inv = (10000.0 ** (-np.arange(half, dtype=np.float32) / half)).astype(np.float32)
    idx = np.arange(S)
    pos = [idx.copy()]
    n, r, i = idx // 512, (idx % 512) // 128, idx % 128
    pos.append(512 * n + 4 * i + r)
    T, r, i = idx // 2048, (idx % 2048) // 128, idx % 128
    pos.append(2048 * T + 16 * i + r)
    cos = np.zeros((3, 128, S), np.float32)
    sin = np.zeros((3, 128, S), np.float32)
    m = np.arange(128)
    fr = inv[m % 32]
    sgn = np.where((m % 64) < 32, -1.0, 1.0).astype(np.float32)
    for g in range(3):
        ang = (pos[g].astype(np.float32)[None, :] * fr[:, None]).astype(np.float32)
        cos[g] = np.cos(ang)
        sin[g] = np.sin(ang) * sgn[:, None]
    return cos, sin


def _consts():
    bf = ml_dtypes.bfloat16
    ident = np.eye(128, dtype=np.float32).astype(bf)
    m = np.arange(128)
    sw = np.where((m % 64) < 32, m + 32, m - 32)
    rsw = np.zeros((128, 128), np.float32)
    rsw[sw, m] = 1.0
    k = np.arange(128)[:, None]
    q = np.arange(128)[None, :]
    half = np.concatenate([(k <= q), (k >= q)], axis=1).astype(np.float32)
    mask2 = np.concatenate([half, half], axis=1).astype(bf)
    tril = (k <= q).astype(np.float32)
    trilT = np.tile(tril, (1, 4)).astype(np.float32)
    n = np.arange(128)
    perm4 = np.zeros((128, 128), np.float32)
    perm4[n, 32 * (n % 4) + n // 4] = 1.0
    return ident, rsw.astype(bf), mask2, trilT, perm4.astype(bf)


_NC_CACHE = {}


def _host_inputs(S, x_b, p):
    cos, sin = _tables(S)
    ident, rsw, mask2, trilT, perm4 = _consts()
    f = np.float32

    def col8(v):
        return np.ascontiguousarray(np.asarray(v, f).reshape(8, 128).T)

    def col4(v):
        return np.ascontiguousarray(np.asarray(v, f).reshape(4, 128).T)
    wsp = np.asarray(p["w_spatial"], f)[0]
    wspT = np.ascontiguousarray(wsp.transpose(2, 0, 1).reshape(128, 512))
    bsp = np.asarray(p["b_spatial"], f)[0].reshape(1, 512)
    common = {
        "w_in": np.ascontiguousarray(np.asarray(p["w_in"], f)[0]),
        "w_ba": np.ascontiguousarray(np.asarray(p["w_branch_attn"], f)[0]),
        "w_bg": np.ascontiguousarray(np.asarray(p["w_branch_gmlp"], f)[0]),
        "w_out": np.ascontiguousarray(np.asarray(p["w_out"], f)[0]),
        "w1": np.ascontiguousarray(np.asarray(p["w_mlp_in"], f)[0]),
        "w2": np.ascontiguousarray(np.asarray(p["w_mlp_out"], f)[0]),
        "gpre": col8(np.asarray(p["norm_pre_mix"])[0]),
        "gpre2": col8(np.asarray(p["norm_pre_mlp"])[0]),
        "gpm_b": np.ascontiguousarray(np.broadcast_to(np.asarray(p["norm_post_mix"], f)[0][None, :], (128, D))),
        "gpl_b": np.ascontiguousarray(np.broadcast_to(np.asarray(p["norm_post_mlp"], f)[0][None, :], (128, D))),
        "wspT": wspT,
        "bsp_b": np.ascontiguousarray(np.broadcast_to(bsp, (128, 512))),
        "lng": col4(np.asarray(p["ln_v_gain"])[0]),
        "lnb": col4(np.asarray(p["ln_v_bias"])[0]),
        "cos_t": cos, "sin_t": sin, "ident": ident, "rsw": rsw, "perm4": perm4, "mask2": mask2, "trilT": trilT,
    }
    return [dict(common, x=np.ascontiguousarray(np.asarray(xb, f))) for xb in x_b]


def kernel(**inputs):
    x = np.asarray(inputs["x"], np.float32)
    B, S, _ = x.shape
    if S not in _NC_CACHE:
        _NC_CACHE[S] = build_nc(S)
    nc = _NC_CACHE[S]
    in_maps = _host_inputs(S, [x[b] for b in range(B)], inputs)
    res = run_bass_kernel_spmd(nc, in_maps, core_ids=list(range(B)))
    return np.stack([np.asarray(r["out"], np.float32) for r in res.results], axis=0)
```
